# Optimizing a Trainium2 kernel written in Bass

```python
import jax, jax.numpy as jnp
from jax import lax
import numpy as np

D_MODEL = 1024
BATCH = 1
SEQ = 16384
DEPTH = 4
DEC_BATCH = 16
DEC_SEQ = 4096
PAST_LEN = 128

RET_HEADS = 8
RET_DK = 64
RET_DV = 128
D_QK = RET_HEADS * RET_DK
D_V = RET_HEADS * RET_DV
CHUNK = 128
ROPE_BASE = 10000.0
POOL_WINDOWS = (2, 4, 8, 16)
POOL_GROUPS = 4
D_POOL = D_MODEL
POOL_GW = D_POOL // POOL_GROUPS
D_FF = 4 * D_MODEL
EPS = 1e-6
IN_SPLITS = (D_QK, D_QK, D_V, D_V, D_POOL, D_MODEL, D_MODEL)
D_IN = D_QK + D_QK + D_V + D_V + D_POOL + D_MODEL + D_MODEL

kernel_name = "hybrid_retention_pool_encoder"


def rms_norm(x, g):
    xf = x.astype(jnp.float32)
    y = xf * lax.rsqrt(jnp.mean(xf * xf, axis=-1, keepdims=True) + EPS)
    return (y * g.astype(jnp.float32)).astype(x.dtype)


def rope(x):
    s, d = x.shape[1], x.shape[-1]
    half = d // 2
    inv = ROPE_BASE ** (-jnp.arange(half, dtype=jnp.float32) / half)
    ang = jnp.arange(s, dtype=jnp.float32)[:, None] * inv[None, :]
    cos = jnp.cos(ang)[None, :, None, :]
    sin = jnp.sin(ang)[None, :, None, :]
    xf = x.astype(jnp.float32)
    x1, x2 = xf[..., :half], xf[..., half:]
    return jnp.concatenate([x1 * cos - x2 * sin, x1 * sin + x2 * cos], axis=-1)


def retention_scan(q, k, v, log_g, strict):
    b, s, h, dk = q.shape
    dv = v.shape[-1]
    n = s // CHUNK
    qc = q.reshape(b, n, CHUNK, h, dk)
    kc = k.reshape(b, n, CHUNK, h, dk)
    vc = v.reshape(b, n, CHUNK, h, dv)
    idx = jnp.arange(CHUNK, dtype=jnp.float32)
    diff = idx[:, None] - idx[None, :]
    mask = diff > 0 if strict else diff >= 0
    safe = jnp.where(mask, diff, 0.0)
    decay = jnp.where(mask[None], jnp.exp(safe[None] * log_g[:, None, None]), 0.0)
    scores = jnp.einsum('bnihd,bnjhd->bnhij', qc, kc) * decay[None, None]
    intra = jnp.einsum('bnhij,bnjhe->bnihe', scores, vc)
    zeta = jnp.exp((CHUNK - 1.0 - idx)[None, :] * log_g[:, None])
    xi = jnp.exp((idx + 1.0)[None, :] * log_g[:, None])
    chunk_decay = jnp.exp(CHUNK * log_g)[None, :, None, None]
    kv = jnp.einsum('bnjhd,hj,bnjhe->nbhde', kc, zeta, vc)

    def step(state, kv_n):
        return chunk_decay * state + kv_n, state

    _, prev_states = lax.scan(step, jnp.zeros((b, h, dk, dv), jnp.float32), kv)
    inter = jnp.einsum('bnihd,hi,nbhde->bnihe', qc, xi, prev_states)
    return (intra + inter).reshape(b, s, h, dv)


def retention_branch(q, k, v, g, decay_fwd, decay_bwd, gn_gain):
    b, s, _ = q.shape
    qh = rope(q.reshape(b, s, RET_HEADS, RET_DK))
    kh = rope(k.reshape(b, s, RET_HEADS, RET_DK)) * (RET_DK ** -0.5)
    vh = v.reshape(b, s, RET_HEADS, RET_DV).astype(jnp.float32)
    lg_f = jax.nn.log_sigmoid(decay_fwd.astype(jnp.float32))
    lg_b = jax.nn.log_sigmoid(decay_bwd.astype(jnp.float32))
    o_f = retention_scan(qh, kh, vh, lg_f, strict=False)
    o_b = jnp.flip(retention_scan(jnp.flip(qh, 1), jnp.flip(kh, 1), jnp.flip(vh, 1), lg_b, strict=True), 1)
    o = o_f + o_b
    mu = jnp.mean(o, axis=-1, keepdims=True)
    var = jnp.mean(jnp.square(o - mu), axis=-1, keepdims=True)
    o = (o - mu) * lax.rsqrt(var + EPS) * gn_gain.astype(jnp.float32).reshape(RET_HEADS, RET_DV)
    y = jax.nn.silu(g.astype(jnp.float32)) * o.reshape(b, s, D_V)
    return y.astype(g.dtype)


def pool_branch(u, w_pool, scale):
    b, s, _ = u.shape
    uf = u.astype(jnp.float32).reshape(b, s, POOL_GROUPS, POOL_GW)
    cs = jnp.concatenate([jnp.zeros((b, 1, POOL_GROUPS, POOL_GW), jnp.float32), jnp.cumsum(uf, axis=1)], axis=1)
    pos = jnp.arange(s)
    outs = []
    for gi, w in enumerate(POOL_WINDOWS):
        half = w // 2
        hi = jnp.minimum(pos + half, s)
        lo = jnp.maximum(pos - half, 0)
        csg = cs[:, :, gi]
        seg = jnp.take(csg, hi, axis=1) - jnp.take(csg, lo, axis=1)
        mean = seg / (hi - lo).astype(jnp.float32)[None, :, None]
        outs.append(mean - uf[:, :, gi])
    p = jnp.stack(outs, axis=2)
    y = jnp.einsum('bsgc,gcd->bsgd', p, w_pool.astype(jnp.float32)).reshape(b, s, D_POOL)
    return (y * scale.astype(jnp.float32)).astype(u.dtype)


def trunk(x, norm_mix_pre, norm_mix_post, w_in, ret_decay_fwd, ret_decay_bwd, ret_gn,
          pool_w, pool_scale, w_out, norm_mlp_pre, norm_mlp_post, w_mlp1, w_mlp2):
    cuts = [int(c) for c in np.cumsum(IN_SPLITS)[:-1]]
    for l in range(DEPTH):
        h = rms_norm(x, norm_mix_pre[l])
        proj = jnp.einsum('bsd,de->bse', h, w_in[l])
        q, k, v, g, u, gr, gp = jnp.split(proj, cuts, axis=-1)
        y_r = retention_branch(q, k, v, g, ret_decay_fwd[l], ret_decay_bwd[l], ret_gn[l])
        y_p = pool_branch(u, pool_w[l], pool_scale[l])
        m = jax.nn.sigmoid(gr) * y_r + jax.nn.sigmoid(gp) * y_p
        x = x + rms_norm(jnp.einsum('bsd,de->bse', m, w_out[l]), norm_mix_post[l])
        h2 = rms_norm(x, norm_mlp_pre[l])
        f = jnp.square(jax.nn.relu(jnp.einsum('bsd,df->bsf', h2, w_mlp1[l])))
        f = jnp.einsum('bsf,fd->bsd', f, w_mlp2[l])
        x = x + rms_norm(f, norm_mlp_post[l])
    return x


def setup_inputs(seed: int = 0) -> dict:
    key = jax.random.key(seed)
    ks = jax.random.split(key, 16)
    f32 = jnp.float32
    base_decay = jnp.log(2.0 ** (5.0 + jnp.arange(RET_HEADS, dtype=f32)) - 1.0)
    return {
        "x_prompt": jax.random.normal(ks[0], (BATCH, SEQ, D_MODEL), f32),
        "x_sample": jax.random.normal(ks[1], (DEC_BATCH, DEC_SEQ, D_MODEL), f32),
        "norm_mix_pre": 1.0 + 0.05 * jax.random.normal(ks[2], (DEPTH, D_MODEL), f32),
        "norm_mix_post": 1.0 + 0.05 * jax.random.normal(ks[3], (DEPTH, D_MODEL), f32),
        "w_in": jax.random.normal(ks[4], (DEPTH, D_MODEL, D_IN), f32) * D_MODEL ** -0.5,
        "ret_decay_fwd": base_decay[None, :] + 0.1 * jax.random.normal(ks[5], (DEPTH, RET_HEADS), f32),
        "ret_decay_bwd": base_decay[None, :] + 0.1 * jax.random.normal(ks[6], (DEPTH, RET_HEADS), f32),
        "ret_gn": 1.0 + 0.05 * jax.random.normal(ks[7], (DEPTH, D_V), f32),
        "pool_w": jax.random.normal(ks[8], (DEPTH, POOL_GROUPS, POOL_GW, POOL_GW), f32) * POOL_GW ** -0.5,
        "pool_scale": 1.0 + 0.05 * jax.random.normal(ks[9], (DEPTH, D_POOL), f32),
        "w_out": jax.random.normal(ks[10], (DEPTH, D_MODEL, D_MODEL), f32) * D_MODEL ** -0.5,
        "norm_mlp_pre": 1.0 + 0.05 * jax.random.normal(ks[11], (DEPTH, D_MODEL), f32),
        "norm_mlp_post": 1.0 + 0.05 * jax.random.normal(ks[12], (DEPTH, D_MODEL), f32),
        "w_mlp1": jax.random.normal(ks[13], (DEPTH, D_MODEL, D_FF), f32) * D_MODEL ** -0.5,
        "w_mlp2": jax.random.normal(ks[14], (DEPTH, D_FF, D_MODEL), f32) * D_FF ** -0.5,
    }


def reference(x_prompt, x_sample, norm_mix_pre, norm_mix_post, w_in, ret_decay_fwd, ret_decay_bwd,
              ret_gn, pool_w, pool_scale, w_out, norm_mlp_pre, norm_mlp_post, w_mlp1, w_mlp2):
    y_prompt = trunk(x_prompt, norm_mix_pre, norm_mix_post, w_in, ret_decay_fwd, ret_decay_bwd, ret_gn,
                     pool_w, pool_scale, w_out, norm_mlp_pre, norm_mlp_post, w_mlp1, w_mlp2)
    y_sample = trunk(x_sample, norm_mix_pre, norm_mix_post, w_in, ret_decay_fwd, ret_decay_bwd, ret_gn,
                     pool_w, pool_scale, w_out, norm_mlp_pre, norm_mlp_post, w_mlp1, w_mlp2)
    return (y_prompt, y_sample)
```

```python
import numpy as np
import ml_dtypes
from contextlib import ExitStack
import concourse.bass as bass
import concourse.mybir as mybir
from concourse.bass_utils import run_bass_kernel_spmd

F32 = mybir.dt.float32
BF16 = mybir.dt.bfloat16
ALU = mybir.AluOpType
AF = mybir.ActivationFunctionType

D = 1024
HEADS = 8
DK = 64
DV = 128
DIN = 6144
DFF = 4096
EPS = 1e-6
CH = 128
ROPE_BASE = 10000.0
POOL_WINDOWS = (2, 4, 8, 16)

ENGS = ["pe", "act", "dve", "pool", "sp"]
import os as _os
KSTOP = int(_os.environ.get("KSTOP", "4"))


class Buf:
    __slots__ = ("name", "w", "r")

    def __init__(self, name=""):
        self.name = name
        self.w = None
        self.r = {}


class Prog:
    def __init__(self, nc, n_dma_sems=40):
        self.nc = nc
        self.q = {e: [] for e in ENGS}
        self.cnt = {e: 0 for e in ENGS}
        self.waited = {e: {} for e in ENGS}
        self.n_dma = n_dma_sems
        self.dma_i = 0
        self.n_sw = 8
        self.sw_i = 0
        self.dma_cnt = [0] * n_dma_sems
        self.dma_last = [None] * n_dma_sems
        self.ninstr = 0

    def _need(self, eng, tok, waits):
        if tok is None:
            return
        key = (tok[0], tok[1])
        val = tok[2]
        if self.waited[eng].get(key, 0) >= val:
            return
        if tok[0] == "e" and tok[1] == eng and (val > self.cnt[eng] or eng == "pe" or _os.environ.get("NOSELF")):
            return
        self.waited[eng][key] = val
        waits.append((key, val))

    def _deps(self, eng, reads, writes, waits):
        for b in reads:
            self._need(eng, b.w, waits)
        for b in writes:
            self._need(eng, b.w, waits)
            for k, v in b.r.items():
                self._need(eng, (k[0], k[1], v), waits)

    def _mark(self, tok, reads, writes):
        k = (tok[0], tok[1])
        for b in reads:
            if b.r.get(k, 0) < tok[2]:
                b.r[k] = tok[2]
        for b in writes:
            b.w = tok
            b.r = {}

    def op(self, eng, fn, reads=(), writes=(), inc=True):
        waits = []
        self._deps(eng, reads, writes, waits)
        c = self.cnt[eng] + 1
        tok = ("e", eng, c)
        if inc:
            self.cnt[eng] = c
        self.q[eng].append((waits, fn, ("e", inc)))
        self._mark(tok, reads, writes)
        self.ninstr += 1
        return tok

    def dma(self, fn, reads=(), writes=(), eng="sp"):
        waits = []
        self._deps(eng, reads, writes, waits)
        if eng == "pool":
            i = self.n_dma - self.n_sw + self.sw_i
            self.sw_i = (self.sw_i + 1) % self.n_sw
        else:
            i = self.dma_i
            self.dma_i = (i + 1) % (self.n_dma - self.n_sw)
        self._need(eng, self.dma_last[i], waits)
        self.dma_cnt[i] += 16
        tok = ("d", i, self.dma_cnt[i])
        self.dma_last[i] = tok
        self.q[eng].append((waits, fn, ("d", i)))
        self._mark(tok, reads, writes)
        self.ninstr += 1
        return tok

    def barrier(self):
        for e in ENGS:
            waits = []
            for i in range(self.n_dma):
                self._need(e, self.dma_last[i], waits)
            for o in ENGS:
                if o != "sp" and o != e and self.cnt[o] > 0:
                    self._need(e, ("e", o, self.cnt[o]), waits)
            if e != "sp" and self.cnt[e] > 0:
                self._need(e, ("e", e, self.cnt[e]), waits)
            self.q[e].append((waits, None, None))

    def emit(self):
        nc = self.nc
        with ExitStack() as es:
            esem = {e: es.enter_context(nc.semaphore("s_" + e)) for e in ENGS if e != "sp"}
            dsem = [es.enter_context(nc.semaphore("d_%d" % i)) for i in range(self.n_dma)]
            block = es.enter_context(nc.Block())

            def semof(key):
                return esem[key[1]] if key[0] == "e" else dsem[key[1]]

            def run(engname):
                def body(engine):
                    for waits, fn, kind in self.q[engname]:
                        for key, val in waits:
                            engine.wait_ge(semof(key), val)
                        if fn is None:
                            continue
                        ins = fn(engine)
                        if kind[0] == "e":
                            if kind[1]:
                                ins.then_inc(esem[engname], 1)
                        else:
                            ins.then_inc(dsem[kind[1]], 16)
                return body

            block.tensor(run("pe"))
            block.scalar(run("act"))
            block.vector(run("dve"))
            block.gpsimd(run("pool"))
            block.sync(run("sp"))


class T:
    def __init__(self, t, nb=1):
        self.t = t
        self.b = [Buf() for _ in range(nb)]

    def __getitem__(self, k):
        return self.t[k]


def build(NT, depth, TA=512, TC=512, TM=256):
    NCH = NT // CH
    nc = bass.Bass("TRN2", target_bir_lowering=False)
    P = Prog(nc)

    def din(name, shape, dt=F32):
        return nc.dram_tensor(name, list(shape), dt, kind="ExternalInput").ap()

    def dscr(name, shape, dt):
        return nc.dram_tensor(name, list(shape), dt, kind="Internal").ap()

    x_in = din("x_in", [NT, D])
    y_out = nc.dram_tensor("y_out", [NT, D], F32, kind="ExternalOutput").ap()
    w_in = din("w_in", [depth, D, DIN])
    w_out = din("w_out", [depth, D, D])
    w_mlp1 = din("w_mlp1", [depth, D, DFF])
    w_mlp2 = din("w_mlp2", [depth, DFF, D])
    pool_w = din("pool_w", [depth, 4, 256, 256])
    vecs = din("vecs", [128, 6, depth, 8])
    decays = din("decays", [128, 2 * depth * 8])
    rope_t = din("rope_t", [NT, 64])
    flagkv = din("flagkv", [128, NCH])
    scanflag = din("scanflag", [128, NCH])
    bandM = din("bandM", [NT // TC, 128, (TC // CH) * 4 * 128], BF16)
    bandH = din("bandH", [NT // TC, 16, (TC // CH) * 4 * 128], BF16)
    consts = din("consts", [128, 5 * 128 + 4])

    xT_d = dscr("xT_d", [D, NT], F32)
    xm_d = dscr("xm_d", [D, NT], F32)
    qT_d = dscr("qT_d", [512, NT], BF16)
    kT_d = dscr("kT_d", [1024, NT], BF16)
    qsT_d = dscr("qsT_d", [1024, NT], BF16)
    gate_d = dscr("gate_d", [1024, NT], BF16)
    sgp_d = dscr("sgp_d", [1024, NT], BF16)
    v_d = dscr("v_d", [NT, 1024], BF16)
    u_d = dscr("u_d", [NT + 16, 1024], BF16)
    kv_d = dscr("kv_d", [NCH, 128, 1024], F32)
    st_d = dscr("st_d", [NCH, 128, 1024], BF16)

    top = ExitStack()

    _uid = [0]

    def sb(es, name, shape, dt, nb=1):
        _uid[0] += 1
        return T(es.enter_context(nc.sbuf_tensor("%s_%d" % (name, _uid[0]), list(shape), dt)), nb)

    with top:
        psb = [T(top.enter_context(nc.psum_tensor("ps%d" % i, [128, 512], F32))) for i in range(8)]
        ps_i = [0]
        NBANK = 7 if _os.environ.get("K4") else 8

        def bank():
            b = psb[ps_i[0]]
            ps_i[0] = (ps_i[0] + 1) % NBANK
            return b

        ident_b = sb(top, "ident_b", [128, 128], BF16)
        ident_f = sb(top, "ident_f", [128, 128], F32)
        ones_b = sb(top, "ones_b", [128, 128], BF16)
        nhalf = sb(top, "nhalf", [128, 512], F32)
        cst = sb(top, "cst", [128, 5 * 128 + 4], F32)
        vec = sb(top, "vec", [128, 6, depth, 8], F32)
        dcy = sb(top, "dcy", [128, 2 * depth * 8], F32)
        lg = sb(top, "lg", [128, 2 * depth * 8], F32)
        fkv = sb(top, "fkv", [128, NCH], F32)
        sfl = sb(top, "sfl", [128, NCH], F32)
        dmask = sb(top, "dmask", [128, 8, 128], F32)
        dtmp = sb(top, "dtmp", [128, 128], F32)
        zx = sb(top, "zx", [128, 4, 8], F32)
        dec = sb(top, "dec", [128, 8], F32)
        atab = sb(top, "atab", [128, NCH, 8], F32)
        tmpes = ExitStack()
        zero_b = sb(tmpes, "zero_b", [8, 1024], BF16)

        P.op("pool", lambda e: e.memset(ident_f[:], 0.0), writes=ident_f.b)
        P.dma(lambda e: e.dma_start(out=cst[:], in_=consts), writes=cst.b)
        P.dma(lambda e: e.dma_start(out=vec[:], in_=vecs), writes=vec.b)
        P.dma(lambda e: e.dma_start(out=dcy[:], in_=decays), writes=dcy.b)
        P.dma(lambda e: e.dma_start(out=fkv[:], in_=flagkv), writes=fkv.b)
        P.dma(lambda e: e.dma_start(out=sfl[:], in_=scanflag), writes=sfl.b)
        P.op("dve", lambda e: e.tensor_copy(out=ident_f[:], in_=cst[:, 516:644]), reads=cst.b, writes=ident_f.b)
        P.op("dve", lambda e: e.tensor_copy(out=ident_b[:], in_=ident_f[:]), reads=ident_f.b, writes=ident_b.b)
        P.op("pool", lambda e: e.memset(ones_b[:], 1.0), writes=ones_b.b)
        P.op("pool", lambda e: e.memset(nhalf[:], -0.5), writes=nhalf.b)
        P.op("pool", lambda e: e.memset(zero_b[:], 0.0), writes=zero_b.b)
        P.dma(lambda e: e.dma_start(out=u_d[0:8, :], in_=zero_b[:]), reads=zero_b.b)
        P.dma(lambda e: e.dma_start(out=u_d[NT + 8:NT + 16, :], in_=zero_b[:]), reads=zero_b.b)
        P.op("act", lambda e: e.activation(out=lg[:], in_=dcy[:], func=AF.Exp, scale=-1.0), reads=dcy.b, writes=lg.b)
        P.op("act", lambda e: e.activation(out=lg[:], in_=lg[:], func=AF.Ln, bias=1.0, scale=1.0), reads=lg.b, writes=lg.b)
        P.op("dve", lambda e: e.tensor_scalar(out=lg[:], in0=lg[:], scalar1=-1.0, scalar2=None, op0=ALU.mult),
             reads=lg.b, writes=lg.b)
        P.barrier()
        tmpes.close()

        def rms_stats(es_tiles, xT, width, sqr, rstd, src_is_psum=None):
            msb = bank() if not _os.environ.get("K4") else psb[7]
            K5 = _os.environ.get("K5", "")
            for k in range(8):
                if K5 == "d":
                    P.op("dve", lambda e, k=k: e.tensor_tensor(out=sqr[:, k, 0:width], in0=xT[:, k, 0:width], in1=xT[:, k, 0:width], op=ALU.mult),
                         reads=[xT.b[k]], writes=[sqr.b[k]])
                else:
                    P.op("act", lambda e, k=k: e.activation(out=sqr[:, k, 0:width], in_=xT[:, k, 0:width], func=AF.Square),
                         reads=[xT.b[k]], writes=[sqr.b[k]])
            for k in range(8 if K5 != "s" else 0):
                P.op("pe", lambda e, k=k: e.matmul(msb[:, 0:width], lhsT=ones_b[:], rhs=sqr[:, k, 0:width],
                                                   start=(k == 0), stop=(k == 7)),
                     reads=[sqr.b[k]] + ones_b.b, writes=msb.b, inc=(k == 7))
            P.op("act", lambda e: e.activation(out=rstd[:, 0:width], in_=msb[:, 0:width], func=AF.Ln, scale=1.0 / D, bias=EPS),
                 reads=msb.b, writes=rstd.b)
            P.op("act", lambda e: e.activation(out=rstd[:, 0:width], in_=rstd[:, 0:width], func=AF.Exp, scale=-0.5),
                 reads=rstd.b, writes=rstd.b)

        def do_layer(l):
            lgf = lambda h: lg[:, l * 8 + h:l * 8 + h + 1]
            lgb = lambda h: lg[:, depth * 8 + l * 8 + h:depth * 8 + l * 8 + h + 1]
            for h in range(HEADS):
                P.op("act", lambda e, h=h: e.activation(out=dmask[:, h, :], in_=cst[:, 0:128], func=AF.Exp, scale=lgf(h)),
                     reads=cst.b + lg.b, writes=dmask.b)
                P.op("dve", lambda e, h=h: e.tensor_tensor(out=dmask[:, h, :], in0=dmask[:, h, :], in1=cst[:, 256:384],
                                                           op=ALU.mult), reads=dmask.b + cst.b, writes=dmask.b)
                P.op("act", lambda e, h=h: e.activation(out=dtmp[:], in_=cst[:, 128:256], func=AF.Exp, scale=lgb(h)),
                     reads=cst.b + lg.b, writes=dtmp.b)
                P.op("dve", lambda e, h=h: e.tensor_tensor(out=dtmp[:], in0=dtmp[:], in1=cst[:, 384:512], op=ALU.mult),
                     reads=dtmp.b + cst.b, writes=dtmp.b)
                P.op("dve", lambda e, h=h: e.tensor_tensor(out=dmask[:, h, :], in0=dmask[:, h, :], in1=dtmp[:], op=ALU.add),
                     reads=dmask.b + dtmp.b, writes=dmask.b)
            lo = l * 8
            hb = depth * 8 + l * 8
            for idx, (col, base) in enumerate([(512, lo), (513, hb), (514, lo), (515, hb)]):
                P.op("dve", lambda e, idx=idx, col=col, base=base: e.tensor_scalar(
                    out=zx[:, idx, :], in0=lg[:, base:base + 8], scalar1=cst[:, col:col + 1], scalar2=None, op0=ALU.mult),
                    reads=lg.b + cst.b, writes=zx.b)
            P.op("act", lambda e: e.activation(out=zx[:], in_=zx[:], func=AF.Exp), reads=zx.b, writes=zx.b)
            P.op("act", lambda e: e.activation(out=dec[0:64, :], in_=lg[0:64, lo:lo + 8], func=AF.Exp, scale=128.0),
                 reads=lg.b, writes=dec.b)
            P.op("act", lambda e: e.activation(out=dec[64:128, :], in_=lg[64:128, hb:hb + 8], func=AF.Exp, scale=128.0),
                 reads=lg.b + dec.b, writes=dec.b)
            P.op("dve", lambda e: e.tensor_tensor(out=atab[:], in0=dec[:].unsqueeze(1).to_broadcast([128, NCH, 8]),
                                                  in1=sfl[:].unsqueeze(2).to_broadcast([128, NCH, 8]), op=ALU.mult),
                 reads=dec.b + sfl.b, writes=atab.b)
            P.barrier()

            def _pass0():
                with ExitStack() as es:
                    NTA = NT // TA
                    CPT = TA // CH
                    wA = sb(es, "wA", [128, 8, DIN], BF16, nb=8)
                    for k in range(8 if not _os.environ.get("KNOW") else 0):
                        P.dma(lambda e, k=k: e.dma_start(out=wA[:, k, :], in_=w_in[l, k * 128:(k + 1) * 128, :]),
                              writes=[wA.b[k]], eng="pool")
                    xT = sb(es, "a_xT", [128, 8, TA], F32, nb=8)
                    xtok = [sb(es, "a_xtok%d" % i, [128, D], F32) for i in range(1)] if l == 0 else None
                    hT = [sb(es, "a_hT%d" % i, [128, 8, TA], BF16, nb=8) for i in range(2)]
                    rstd = sb(es, "a_rstd", [128, TA], F32)
                    sig = [sb(es, "a_sig%d" % i, [128, TA], F32) for i in range(3)]
                    gst = [sb(es, "a_gst%d" % i, [128, TA], BF16) for i in range(2)]
                    pst = [sb(es, "a_pst%d" % i, [128, TA], BF16) for i in range(2)]
                    q32 = sb(es, "a_q32", [128, 512], F32)
                    k32 = sb(es, "a_k32", [128, 512], F32)
                    rt1 = sb(es, "a_rt1", [128, 8, 32], F32)
                    rt2 = sb(es, "a_rt2", [128, 8, 32], F32)
                    qbf = sb(es, "a_qbf", [128, 8, 64], BF16)
                    kbf = sb(es, "a_kbf", [128, 8, 64], BF16)
                    qs = sb(es, "a_qs", [128, 8, 2, 64], BF16)
                    kz = sb(es, "a_kz", [128, 8, 2, 64], BF16)
                    vbf = [sb(es, "a_vbf%d" % i, [128, 1024], BF16) for i in range(1)]
                    ubf = [sb(es, "a_ubf%d" % i, [128, 1024], BF16) for i in range(1)]
                    kvs = [sb(es, "a_kvs%d" % i, [128, 1024], F32) for i in range(1)]
                    rope = [sb(es, "a_rope%d" % i, [128, 64], F32) for i in range(2)]
                    qT = sb(es, "a_qT", [128, 4, TA], BF16, nb=CPT)
                    kT = sb(es, "a_kT", [128, 8, TA], BF16, nb=CPT)
                    kpad = sb(es, "a_kpad", [128, 8, 128], BF16)
                    P.op("pool", lambda e: e.memset(kpad[:], 0.0), writes=kpad.b)
                    qsT = sb(es, "a_qsT", [128, 8, TA], BF16, nb=CPT)
                    g_pre = vec[:, 0, l, :]

                    def load_x(t):
                        c0 = t * TA
                        if l > 0:
                            for k in range(8):
                                P.dma(lambda e, k=k: e.dma_start(out=xT[:, k, :], in_=xT_d[k * 128:(k + 1) * 128, c0:c0 + TA]),
                                      writes=[xT.b[k]])
                        else:
                            for c in range(CPT if not _os.environ.get("K6") else int(_os.environ.get("K6"))):
                                xt = xtok[0]
                                P.dma(lambda e, c=c, xt=xt: e.dma_start(out=xt[:], in_=x_in[c0 + c * CH:c0 + (c + 1) * CH, :]),
                                      writes=xt.b)
                                for half in range(2):
                                    pb = bank()
                                    for kk in range(4 if _os.environ.get("K2", "") != "c" else 0):
                                        k = half * 4 + kk
                                        P.op("pe", lambda e, k=k, kk=kk, pb=pb, xt=xt: e.transpose(
                                            pb[:, kk * 128:(kk + 1) * 128], xt[:, k * 128:(k + 1) * 128], ident_f[:]),
                                            reads=xt.b + ident_f.b, writes=pb.b, inc=(kk == 3))
                                    for kk in range(4 if _os.environ.get("K2", "") not in ("b", "c") else 0):
                                        k = half * 4 + kk
                                        K7 = _os.environ.get("K7", "")
                                        useact = (half == 1)
                                        P.op("act" if useact else "dve",
                                             (lambda e, k=k, kk=kk, pb=pb, c=c: e.copy(out=xT[:, k, c * CH:(c + 1) * CH],
                                                                                       in_=pb[:, kk * 128:(kk + 1) * 128]))
                                             if useact else
                                             (lambda e, k=k, kk=kk, pb=pb, c=c: e.tensor_copy(out=xT[:, k, c * CH:(c + 1) * CH],
                                                                                              in_=pb[:, kk * 128:(kk + 1) * 128])),
                                             reads=pb.b, writes=[xT.b[k]])
                            for k in range(8 if _os.environ.get("K2", "") not in ("a", "b", "c") else 0):
                                P.dma(lambda e, k=k: e.dma_start(out=xT_d[k * 128:(k + 1) * 128, c0:c0 + TA], in_=xT[:, k, :]),
                                      reads=[xT.b[k]])

                    def norm_x(t):
                        h = hT[t % 2]
                        rms_stats(None, xT, TA, h, rstd)
                        for k in range(8 if _os.environ.get("K3", "z") >= "d" else 0):
                            eng = "dve"
                            P.op(eng, lambda e, k=k, h=h: e.scalar_tensor_tensor(
                                out=h[:, k, :], in0=xT[:, k, :], scalar=g_pre[:, k:k + 1], in1=rstd[:], op0=ALU.mult, op1=ALU.mult),
                                reads=[xT.b[k]] + rstd.b + vec.b, writes=[h.b[k]])

                    def fm_group(t):
                        h = hT[t % 2]
                        c0 = t * TA
                        for j in range(8):
                            pg, pr, pp = bank(), bank(), bank()
                            for (pb, col) in ((pg, 2048 + j * 128), (pr, 4096 + j * 128), (pp, 5120 + j * 128)):
                                for k in range(8):
                                    P.op("pe", lambda e, pb=pb, col=col, k=k: e.matmul(
                                        pb[:, 0:TA], lhsT=wA[:, k, col:col + 128], rhs=h[:, k, :], start=(k == 0), stop=(k == 7)),
                                        reads=[wA.b[k], h.b[k]], writes=pb.b, inc=(k == 7))
                            s1, s2 = sig[j % 2], sig[2]
                            go, po = gst[j % 2], pst[j % 2]
                            P.op("act", lambda e, pg=pg, s1=s1: e.activation(out=s1[:], in_=pg[:, 0:TA], func=AF.Sigmoid),
                                 reads=pg.b, writes=s1.b)
                            P.op("act", lambda e, pr=pr, s2=s2: e.activation(out=s2[:], in_=pr[:, 0:TA], func=AF.Sigmoid),
                                 reads=pr.b, writes=s2.b)
                            P.op("act", lambda e, pp=pp, po=po: e.activation(out=po[:], in_=pp[:, 0:TA], func=AF.Sigmoid),
                                 reads=pp.b, writes=po.b)
                            P.op("dve", lambda e, pg=pg, s1=s1, j=j: e.scalar_tensor_tensor(
                                out=s1[:], in0=pg[:, 0:TA], scalar=vec[:, 5, l, j:j + 1], in1=s1[:], op0=ALU.mult, op1=ALU.mult),
                                reads=pg.b + s1.b + vec.b, writes=s1.b)
                            P.op("dve", lambda e, s1=s1, s2=s2, go=go: e.tensor_tensor(out=go[:], in0=s1[:], in1=s2[:], op=ALU.mult),
                                 reads=s1.b + s2.b, writes=go.b)
                            P.dma(lambda e, go=go, j=j: e.dma_start(out=gate_d[j * 128:(j + 1) * 128, c0:c0 + TA], in_=go[:]),
                                  reads=go.b)
                            P.dma(lambda e, po=po, j=j: e.dma_start(out=sgp_d[j * 128:(j + 1) * 128, c0:c0 + TA], in_=po[:]),
                                  reads=po.b)

                    def tm_chunk(t, c):
                        h = hT[t % 2]
                        gc = t * CPT + c
                        tok0 = gc * CH
                        cs = slice(c * CH, (c + 1) * CH)
                        rp = rope[gc % 2]
                        P.dma(lambda e, rp=rp: e.dma_start(out=rp[:], in_=rope_t[tok0:tok0 + CH, :]), writes=rp.b)
                        banks = []
                        for col in (0, 512, 1024, 1536, 3072, 3584):
                            pb = bank()
                            banks.append(pb)
                            for k in range(8):
                                P.op("pe", lambda e, pb=pb, col=col, k=k: e.matmul(
                                    pb[:], lhsT=h[:, k, cs], rhs=wA[:, k, col:col + 512], start=(k == 0), stop=(k == 7)),
                                    reads=[wA.b[k], h.b[k]], writes=pb.b, inc=(k == 7))
                        pq, pk, pv0, pv1, pu0, pu1 = banks
                        v_, u_, kv_ = vbf[0], ubf[0], kvs[0]
                        P.op("act", lambda e: e.copy(out=q32[:], in_=pq[:]), reads=pq.b, writes=q32.b)
                        P.op("act", lambda e: e.mul(out=k32[:], in_=pk[:], mul=DK ** -0.5), reads=pk.b, writes=k32.b)
                        P.op("act", lambda e: e.copy(out=v_[:, 0:512], in_=pv0[:]), reads=pv0.b, writes=v_.b)
                        P.op("act", lambda e: e.copy(out=v_[:, 512:1024], in_=pv1[:]), reads=pv1.b + v_.b, writes=v_.b)
                        P.op("act", lambda e: e.copy(out=u_[:, 0:512], in_=pu0[:]), reads=pu0.b, writes=u_.b)
                        P.op("act", lambda e: e.copy(out=u_[:, 512:1024], in_=pu1[:]), reads=pu1.b + u_.b, writes=u_.b)
                        P.dma(lambda e: e.dma_start(out=v_d[tok0:tok0 + CH, :], in_=v_[:]), reads=v_.b)
                        P.dma(lambda e: e.dma_start(out=u_d[8 + tok0:8 + tok0 + CH, :], in_=u_[:]), reads=u_.b)
                        cosb = rp[:, 0:32].unsqueeze(1).to_broadcast([128, 8, 32])
                        sinb = rp[:, 32:64].unsqueeze(1).to_broadcast([128, 8, 32])
                        for (src, dst, eng) in ((q32, qbf, "dve"), (k32, kbf, "dve")):
                            s4 = src[:].rearrange("p (h two d) -> p h two d", h=8, two=2)
                            x1, x2 = s4[:, :, 0, :], s4[:, :, 1, :]
                            rd = src.b + rp.b
                            P.op(eng, lambda e, x1=x1: e.tensor_tensor(out=rt1[:], in0=x1, in1=cosb, op=ALU.mult),
                                 reads=rd, writes=rt1.b)
                            P.op(eng, lambda e, x2=x2: e.tensor_tensor(out=rt2[:], in0=x2, in1=sinb, op=ALU.mult),
                                 reads=rd, writes=rt2.b)
                            P.op(eng, lambda e, dst=dst: e.tensor_tensor(out=dst[:, :, 0:32], in0=rt1[:], in1=rt2[:], op=ALU.subtract),
                                 reads=rt1.b + rt2.b, writes=dst.b)
                            P.op(eng, lambda e, x1=x1: e.tensor_tensor(out=rt1[:], in0=x1, in1=sinb, op=ALU.mult),
                                 reads=rd, writes=rt1.b)
                            P.op(eng, lambda e, x2=x2: e.tensor_tensor(out=rt2[:], in0=x2, in1=cosb, op=ALU.mult),
                                 reads=rd, writes=rt2.b)
                            P.op(eng, lambda e, dst=dst: e.tensor_tensor(out=dst[:, :, 32:64], in0=rt1[:], in1=rt2[:], op=ALU.add),
                                 reads=rt1.b + rt2.b + dst.b, writes=dst.b)
                        for (src, dst, zi, eng) in ((qbf, qs, 2, "dve"), (kbf, kz, 0, "dve")):
                            for d in range(2):
                                P.op(eng, lambda e, src=src, dst=dst, zi=zi, d=d: e.tensor_tensor(
                                    out=dst[:, :, d, :], in0=src[:], in1=zx[:, zi + d, :].unsqueeze(2).to_broadcast([128, 8, 64]),
                                    op=ALU.mult), reads=src.b + zx.b + dst.b, writes=dst.b)
                        pbq = bank()
                        pq16 = pbq[:].bitcast(BF16)
                        for pr_ in range(4):
                            P.op("pe", lambda e, pr_=pr_: e.transpose(pq16[:, pr_ * 128:(pr_ + 1) * 128],
                                                                       qbf[:].rearrange("p h d -> p (h d)")[:, pr_ * 128:(pr_ + 1) * 128], ident_b[:]),
                                 reads=qbf.b + ident_b.b, writes=pbq.b, inc=(pr_ == 3))
                        P.op("act", lambda e: e.copy(out=qT[:, :, cs], in_=pq16[:, 0:512].rearrange("p (a b) -> p a b", a=4)),
                             reads=pbq.b, writes=[qT.b[c]])
                        kp4 = kpad[:].rearrange("p (pr two) x -> p pr two x", two=2)
                        kb4 = kbf[:].rearrange("p (pr two) d -> p pr two d", two=2)
                        P.op("dve", lambda e: e.tensor_copy(out=kp4[:, :, 0, 0:64], in_=kb4[:, :, 0, :]),
                             reads=kbf.b + kpad.b, writes=kpad.b)
                        P.op("dve", lambda e: e.tensor_copy(out=kp4[:, :, 1, 64:128], in_=kb4[:, :, 1, :]),
                             reads=kbf.b + kpad.b, writes=kpad.b)
                        pbk = bank()
                        pk16 = pbk[:].bitcast(BF16)
                        for hh in range(8):
                            P.op("pe", lambda e, hh=hh: e.transpose(pk16[:, hh * 128:(hh + 1) * 128], kpad[:, hh, :], ident_b[:]),
                                 reads=kpad.b + ident_b.b, writes=pbk.b, inc=(hh == 7))
                        P.op("act", lambda e: e.copy(out=kT[:, :, cs], in_=pk16[:].rearrange("p (a b) -> p a b", a=8)),
                             reads=pbk.b, writes=[kT.b[c]])
                        pbs = bank()
                        ps16 = pbs[:].bitcast(BF16)
                        for hh in range(8):
                            P.op("pe", lambda e, hh=hh: e.transpose(ps16[:, hh * 128:(hh + 1) * 128], qs[:].rearrange("p h t d -> p (h t d)")[:, hh * 128:(hh + 1) * 128], ident_b[:]),
                                 reads=qs.b + ident_b.b, writes=pbs.b, inc=(hh == 7))
                        P.op("act", lambda e: e.copy(out=qsT[:, :, cs], in_=ps16[:].rearrange("p (a b) -> p a b", a=8)),
                             reads=pbs.b, writes=[qsT.b[c]])
                        for half in range(2):
                            pb = bank()
                            for hh in range(4):
                                hd = half * 4 + hh
                                P.op("pe", lambda e, hd=hd, hh=hh, pb=pb: e.matmul(
                                    pb[:, hh * 128:(hh + 1) * 128], lhsT=kz[:].rearrange("p h t d -> p (h t d)")[:, hd * 128:(hd + 1) * 128], rhs=v_[:, hd * 128:(hd + 1) * 128],
                                    start=True, stop=True), reads=kz.b + v_.b, writes=pb.b, inc=(hh == 3))
                            P.op("dve", lambda e, pb=pb, half=half: e.tensor_scalar(
                                out=kv_[:, half * 512:(half + 1) * 512], in0=pb[:], scalar1=fkv[:, gc:gc + 1], scalar2=None,
                                op0=ALU.mult), reads=pb.b + fkv.b + kv_.b, writes=kv_.b)
                        P.dma(lambda e: e.dma_start(out=kv_d[gc], in_=kv_[:]), reads=kv_.b)

                    def store_T(t):
                        c0 = t * TA
                        for pr_ in range(4):
                            P.dma(lambda e, pr_=pr_: e.dma_start(out=qT_d[pr_ * 128:(pr_ + 1) * 128, c0:c0 + TA], in_=qT[:, pr_, :]),
                                  reads=qT.b)
                        for hh in range(8):
                            P.dma(lambda e, hh=hh: e.dma_start(out=qsT_d[hh * 128:(hh + 1) * 128, c0:c0 + TA], in_=qsT[:, hh, :]),
                                  reads=qsT.b)
                            P.dma(lambda e, hh=hh: e.dma_start(out=kT_d[hh * 128:(hh + 1) * 128, c0:c0 + TA], in_=kT[:, hh, :]),
                                  reads=kT.b)

                    KSUB = int(_os.environ.get("KSUB", "99"))
                    if KSUB >= 2:
                        load_x(0)
                    if KSUB >= 3:
                        norm_x(0)
                    if KSUB == 4:
                        fm_group(0)
                    if KSUB == 5:
                        tm_chunk(0, 0)
                    for t in range(NTA if KSUB >= 99 else 0):
                        fm_group(t)
                        tm_chunk(t, 0)
                        if t + 1 < NTA:
                            load_x(t + 1)
                            norm_x(t + 1)
                        for c in range(1, CPT):
                            tm_chunk(t, c)
                        store_T(t)
                    P.barrier()

            if KSTOP >= 1:
                _pass0()
            def _pass1():
                with ExitStack() as es:
                    S = sb(es, "b_S", [128, 1024], F32)
                    kvl = [sb(es, "b_kv%d" % i, [128, 1024], F32) for i in range(4)]
                    so = [sb(es, "b_so%d" % i, [128, 1024], BF16) for i in range(3)]
                    P.op("pool", lambda e: e.memset(S[:], 0.0), writes=S.b)
                    P.op("pool", lambda e: e.memset(so[2][:], 0.0), writes=so[2].b)
                    P.dma(lambda e: e.dma_start(out=st_d[0, 0:64, :], in_=so[2][0:64, :]), reads=so[2].b)
                    P.dma(lambda e: e.dma_start(out=st_d[NCH - 1, 64:128, :], in_=so[2][64:128, :]), reads=so[2].b)

                    def ld(s):
                        kk = kvl[s % 4]
                        P.dma(lambda e: e.dma_start(out=kk[0:64, :], in_=kv_d[s, 0:64, :]), writes=kk.b)
                        P.dma(lambda e: e.dma_start(out=kk[64:128, :], in_=kv_d[NCH - 1 - s, 64:128, :]), writes=kk.b)

                    for s in range(min(3, NCH - 1)):
                        ld(s)
                    for s in range(NCH - 1):
                        if s + 3 < NCH - 1:
                            ld(s + 3)
                        kk = kvl[s % 4]
                        o = so[s % 2]
                        for hh in range(8):
                            P.op("dve", lambda e, hh=hh, kk=kk, s=s: e.scalar_tensor_tensor(
                                out=S[:, hh * 128:(hh + 1) * 128], in0=S[:, hh * 128:(hh + 1) * 128], scalar=atab[:, s, hh:hh + 1],
                                in1=kk[:, hh * 128:(hh + 1) * 128], op0=ALU.mult, op1=ALU.add),
                                reads=S.b + kk.b + atab.b, writes=S.b)
                        P.op("act", lambda e, o=o: e.copy(out=o[:], in_=S[:]), reads=S.b, writes=o.b)
                        P.dma(lambda e, o=o, s=s: e.dma_start(out=st_d[s + 1, 0:64, :], in_=o[0:64, :]), reads=o.b)
                        P.dma(lambda e, o=o, s=s: e.dma_start(out=st_d[NCH - 2 - s, 64:128, :], in_=o[64:128, :]), reads=o.b)
                    P.barrier()

            if KSTOP >= 2:
                _pass1()
            def _pass2():
                with ExitStack() as es:
                    NTC = NT // TC
                    CPT = TC // CH
                    wO = sb(es, "wO", [128, 8, D], BF16, nb=8)
                    wP = sb(es, "wP", [128, 4, 2, 256], BF16, nb=8)
                    for k in range(8):
                        P.dma(lambda e, k=k: e.dma_start(out=wO[:, k, :], in_=w_out[l, k * 128:(k + 1) * 128, :]),
                              writes=[wO.b[k]], eng="pool")
                    for g in range(4):
                        for cc in range(2):
                            P.dma(lambda e, g=g, cc=cc: e.dma_start(out=wP[:, g, cc, :], in_=pool_w[l, g, cc * 128:(cc + 1) * 128, :]),
                                  writes=[wP.b[2 * g + cc]], eng="pool")
                    NBUF = 2
                    qT = [sb(es, "c_qT%d" % i, [128, 4, TC], BF16) for i in range(NBUF)]
                    kT = [sb(es, "c_kT%d" % i, [128, 8, TC], BF16) for i in range(NBUF)]
                    qsT = [sb(es, "c_qsT%d" % i, [128, 8, TC], BF16) for i in range(NBUF)]
                    gat = [sb(es, "c_gat%d" % i, [128, 8, TC], BF16) for i in range(NBUF)]
                    sgp = [sb(es, "c_sgp%d" % i, [128, 8, TC], BF16) for i in range(NBUF)]
                    xT = [sb(es, "c_xT%d" % i, [128, 8, TC], F32, nb=8) for i in range(1)]
                    bM = [sb(es, "c_bM%d" % i, [128, CPT * 4 * 128], BF16) for i in range(NBUF)]
                    bH = [sb(es, "c_bH%d" % i, [128, CPT * 4 * 128], BF16) for i in range(NBUF)]
                    uh = [sb(es, "c_uh%d" % i, [128, 1024], BF16) for i in range(2)]
                    for tz in bH + uh:
                        P.op("pool", lambda e, tz=tz: e.memset(tz[:], 0.0), writes=tz.b)
                    vv = [sb(es, "c_v%d" % i, [128, 1024], BF16) for i in range(2)]
                    st = [sb(es, "c_st%d" % i, [128, 1024], BF16) for i in range(2)]
                    uu = [sb(es, "c_u%d" % i, [128, 1024], BF16) for i in range(2)]
                    PT = [sb(es, "c_PT%d" % i, [128, 8, 128], BF16) for i in range(2)]
                    on = [sb(es, "c_on%d" % i, [128, 8, 128], BF16) for i in range(2)]
                    stats2 = [sb(es, "c_stats%d" % i, [128, 8, 6], F32) for i in range(2)]
                    mv2 = [sb(es, "c_mv%d" % i, [128, 8, 2], F32) for i in range(2)]
                    rs2 = [sb(es, "c_rs%d" % i, [128, 8], F32) for i in range(2)]
                    nmr2 = [sb(es, "c_nmr%d" % i, [128, 8], F32) for i in range(2)]
                    mT = sb(es, "c_mT", [128, 8, TC], BF16, nb=8)
                    pT = sb(es, "c_pT", [128, 8, TC], BF16, nb=8)
                    tmp = [sb(es, "c_tmp%d" % i, [128, TC], BF16) for i in range(2)]
                    yy = sb(es, "c_y", [128, 8, TC], F32, nb=8)
                    rstd = sb(es, "c_rstd", [128, TC], F32)
                    g_post = vec[:, 1, l, :]
                    pscale = vec[:, 4, l, :]

                    def load_tile(t):
                        i = t % NBUF
                        c0 = t * TC
                        for pr_ in range(4):
                            P.dma(lambda e, pr_=pr_: e.dma_start(out=qT[i][:, pr_, :], in_=qT_d[pr_ * 128:(pr_ + 1) * 128, c0:c0 + TC]),
                                  writes=qT[i].b)
                        for hh in range(8):
                            P.dma(lambda e, hh=hh: e.dma_start(out=qsT[i][:, hh, :], in_=qsT_d[hh * 128:(hh + 1) * 128, c0:c0 + TC]),
                                  writes=qsT[i].b)
                            P.dma(lambda e, hh=hh: e.dma_start(out=kT[i][:, hh, :], in_=kT_d[hh * 128:(hh + 1) * 128, c0:c0 + TC]),
                                  writes=kT[i].b)
                            P.dma(lambda e, hh=hh: e.dma_start(out=gat[i][:, hh, :], in_=gate_d[hh * 128:(hh + 1) * 128, c0:c0 + TC]),
                                  writes=gat[i].b)
                            P.dma(lambda e, hh=hh: e.dma_start(out=sgp[i][:, hh, :], in_=sgp_d[hh * 128:(hh + 1) * 128, c0:c0 + TC]),
                                  writes=sgp[i].b)
                        P.dma(lambda e: e.dma_start(out=bM[i][:], in_=bandM[t]), writes=bM[i].b)
                        P.dma(lambda e: e.dma_start(out=bH[i][0:16, :], in_=bandH[t]), writes=bH[i].b)

                    KC = int(_os.environ.get("KC", "99"))

                    def chunk(t, c):
                        i = t % NBUF
                        gc = t * CPT + c
                        tok0 = gc * CH
                        cs = slice(c * CH, (c + 1) * CH)
                        v_, s_, u_ = vv[gc % 2], st[gc % 2], uu[gc % 2]
                        pt_, on_ = PT[gc % 2], on[gc % 2]
                        stats, mv, rs, nmr = stats2[gc % 2], mv2[gc % 2], rs2[gc % 2], nmr2[gc % 2]
                        P.dma(lambda e: e.dma_start(out=v_[:], in_=v_d[tok0:tok0 + CH, :]), writes=v_.b)
                        P.dma(lambda e: e.dma_start(out=s_[:], in_=st_d[gc]), writes=s_.b)
                        P.dma(lambda e: e.dma_start(out=u_[:], in_=u_d[8 + tok0:8 + tok0 + CH, :]), writes=u_.b)
                        uh_ = uh[gc % 2]
                        P.dma(lambda e: e.dma_start(out=uh_[0:8, :], in_=u_d[tok0:tok0 + 8, :]), writes=uh_.b)
                        P.dma(lambda e: e.dma_start(out=uh_[8:16, :], in_=u_d[tok0 + CH + 8:tok0 + CH + 16, :]), writes=uh_.b)
                        if KC <= 0:
                            return
                        for half in range(2):
                            pb = bank()
                            for hh in range(4):
                                hd = half * 4 + hh
                                pr_, base = hd // 2, (hd % 2) * 64
                                P.op("pe", lambda e, pb=pb, hh=hh, pr_=pr_, hd=hd: e.matmul(
                                    pb[:, hh * 128:(hh + 1) * 128], lhsT=kT[i][:, hd, cs], rhs=qT[i][:, pr_, cs],
                                    start=True, stop=True), reads=kT[i].b + qT[i].b, writes=pb.b, inc=(hh == 3))
                            if _os.environ.get("KD") == "nomask":
                                continue
                            P.op("dve", lambda e, pb=pb, half=half: e.tensor_tensor(
                                out=pt_[:, half * 4:(half + 1) * 4, :], in0=pb[:].rearrange("p (a b) -> p a b", a=4),
                                in1=dmask[:, half * 4:(half + 1) * 4, :], op=ALU.mult),
                                reads=pb.b + dmask.b + pt_.b, writes=pt_.b)
                        if KC <= 1:
                            return
                        obanks = []
                        for half in range(2):
                            pb = bank()
                            obanks.append(pb)
                            for hh in range(4):
                                hd = half * 4 + hh
                                P.op("pe", lambda e, pb=pb, hh=hh, hd=hd: e.matmul(
                                    pb[:, hh * 128:(hh + 1) * 128], lhsT=pt_[:, hd, :], rhs=v_[:, hd * 128:(hd + 1) * 128],
                                    start=True, stop=False), reads=pt_.b + v_.b, writes=pb.b, inc=False)
                                P.op("pe", lambda e, pb=pb, hh=hh, hd=hd: e.matmul(
                                    pb[:, hh * 128:(hh + 1) * 128], lhsT=qsT[i][:, hd, cs], rhs=s_[:, hd * 128:(hd + 1) * 128],
                                    start=False, stop=True), reads=qsT[i].b + s_.b, writes=pb.b, inc=(hh == 3))
                        if KC <= 2:
                            return
                        for half in range(2):
                            pb = obanks[half]
                            for hh in range(4):
                                hd = half * 4 + hh
                                P.op("dve", lambda e, pb=pb, hh=hh, hd=hd: e.bn_stats(out=stats[:, hd, :], in_=pb[:, hh * 128:(hh + 1) * 128]),
                                     reads=pb.b + stats.b, writes=stats.b)
                        for hd in range(8):
                            P.op("dve", lambda e, hd=hd: e.bn_aggr(out=mv[:, hd, :], in_=stats[:, hd, :]),
                                 reads=stats.b + mv.b, writes=mv.b)
                        P.op("act", lambda e: e.activation(out=rs[:], in_=mv[:, :, 1], func=AF.Ln, scale=1.0, bias=EPS),
                             reads=mv.b, writes=rs.b)
                        P.op("act", lambda e: e.activation(out=rs[:], in_=rs[:], func=AF.Exp, scale=-0.5),
                             reads=rs.b, writes=rs.b)
                        P.op("dve", lambda e: e.scalar_tensor_tensor(out=nmr[:], in0=mv[:, :, 0], scalar=-1.0, in1=rs[:],
                                                                      op0=ALU.mult, op1=ALU.mult), reads=mv.b + rs.b, writes=nmr.b)
                        if KC <= 3:
                            return
                        for half in range(2):
                            pb = obanks[half]
                            for hh in range(4):
                                hd = half * 4 + hh
                                P.op("act", lambda e, pb=pb, hh=hh, hd=hd: e.activation(
                                    out=on_[:, hd, :], in_=pb[:, hh * 128:(hh + 1) * 128], func=AF.Identity,
                                    scale=rs[:, hd:hd + 1], bias=nmr[:, hd:hd + 1]),
                                    reads=pb.b + rs.b + nmr.b + on_.b, writes=on_.b)
                        if KC <= 4:
                            return
                        pbt = bank()
                        p16 = pbt[:].bitcast(BF16)
                        for hd in range(8):
                            P.op("pe", lambda e, hd=hd: e.transpose(p16[:, hd * 128:(hd + 1) * 128], on_[:, hd, :], ident_b[:]),
                                 reads=on_.b + ident_b.b, writes=pbt.b, inc=(hd == 7))
                        for half in range(2):
                            P.op("dve" if half == 0 else "pool" if False else "dve", lambda e, half=half: e.tensor_tensor(
                                out=mT[:, half * 4:(half + 1) * 4, cs],
                                in0=p16[:, half * 512:(half + 1) * 512].rearrange("p (a b) -> p a b", a=4),
                                in1=gat[i][:, half * 4:(half + 1) * 4, cs], op=ALU.mult),
                                reads=pbt.b + gat[i].b, writes=[mT.b[k] for k in range(half * 4, half * 4 + 4)])
                        if KC <= 5:
                            return
                        for half in range(2):
                            pb = bank()
                            for ff in range(4):
                                f = half * 4 + ff
                                g = f // 2
                                bo = (c * 4 + g) * 128
                                P.op("pe", lambda e, pb=pb, ff=ff, f=f, bo=bo: e.matmul(
                                    pb[:, ff * 128:(ff + 1) * 128], lhsT=u_[:, f * 128:(f + 1) * 128], rhs=bM[i][:, bo:bo + 128],
                                    start=True, stop=False), reads=u_.b + bM[i].b, writes=pb.b, inc=False)
                                P.op("pe", lambda e, pb=pb, ff=ff, f=f, bo=bo: e.matmul(
                                    pb[:, ff * 128:(ff + 1) * 128], lhsT=uh_[:, f * 128:(f + 1) * 128], rhs=bH[i][:, bo:bo + 128],
                                    start=False, stop=True), reads=uh_.b + bH[i].b, writes=pb.b, inc=(ff == 3))
                            P.op("act", lambda e, pb=pb, half=half: e.copy(
                                out=pT[:, half * 4:(half + 1) * 4, cs], in_=pb[:].rearrange("p (a b) -> p a b", a=4)),
                                reads=pb.b, writes=[pT.b[k] for k in range(half * 4, half * 4 + 4)])

                    def tile_tail(t):
                        i = t % NBUF
                        c0 = t * TC
                        for hh in range(8):
                            P.dma(lambda e, hh=hh: e.dma_start(out=xT[0][:, hh, :], in_=xT_d[hh * 128:(hh + 1) * 128, c0:c0 + TC]),
                                  writes=[xT[0].b[hh]])
                        for dchunk in range(8):
                            g, dc = dchunk // 2, dchunk % 2
                            pb = bank()
                            for cc in range(2):
                                P.op("pe", lambda e, pb=pb, g=g, dc=dc, cc=cc: e.matmul(
                                    pb[:, 0:TC], lhsT=wP[:, g, cc, dc * 128:(dc + 1) * 128], rhs=pT[:, 2 * g + cc, :],
                                    start=(cc == 0), stop=(cc == 1)), reads=[wP.b[2 * g + cc], pT.b[2 * g + cc]], writes=pb.b, inc=(cc == 1))
                            tm_ = tmp[dchunk % 2]
                            P.op("dve", lambda e, pb=pb, dchunk=dchunk, tm_=tm_: e.scalar_tensor_tensor(
                                out=tm_[:], in0=pb[:, 0:TC], scalar=pscale[:, dchunk:dchunk + 1], in1=sgp[i][:, dchunk, :],
                                op0=ALU.mult, op1=ALU.mult), reads=pb.b + sgp[i].b + vec.b, writes=tm_.b)
                            P.op("dve", lambda e, dchunk=dchunk, tm_=tm_: e.tensor_tensor(
                                out=mT[:, dchunk, :], in0=mT[:, dchunk, :], in1=tm_[:], op=ALU.add),
                                reads=tm_.b + [mT.b[dchunk]], writes=[mT.b[dchunk]])
                        for n in range(8):
                            pb = bank()
                            for k in range(8):
                                P.op("pe", lambda e, pb=pb, n=n, k=k: e.matmul(
                                    pb[:, 0:TC], lhsT=wO[:, k, n * 128:(n + 1) * 128], rhs=mT[:, k, :], start=(k == 0), stop=(k == 7)),
                                    reads=[wO.b[k], mT.b[k]], writes=pb.b, inc=(k == 7))
                            P.op("act", lambda e, pb=pb, n=n: e.copy(out=yy[:, n, :], in_=pb[:, 0:TC]), reads=pb.b, writes=[yy.b[n]])
                        rms_stats(None, yy, TC, pT, rstd)
                        for k in range(8):
                            eng = "dve"
                            if eng == "dve":
                                P.op(eng, lambda e, k=k: e.scalar_tensor_tensor(
                                    out=yy[:, k, :], in0=yy[:, k, :], scalar=g_post[:, k:k + 1], in1=rstd[:], op0=ALU.mult, op1=ALU.mult),
                                    reads=[yy.b[k]] + rstd.b + vec.b, writes=[yy.b[k]])
                            else:
                                P.op(eng, lambda e, k=k: e.tensor_tensor(out=yy[:, k, :], in0=yy[:, k, :], in1=rstd[:], op=ALU.mult),
                                     reads=[yy.b[k]] + rstd.b, writes=[yy.b[k]])
                                P.op(eng, lambda e, k=k: e.tensor_scalar(out=yy[:, k, :], in0=yy[:, k, :], scalar1=g_post[:, k:k + 1],
                                                                         scalar2=None, op0=ALU.mult),
                                     reads=[yy.b[k]] + vec.b, writes=[yy.b[k]])
                            P.op(eng, lambda e, k=k: e.tensor_tensor(out=xT[0][:, k, :], in0=xT[0][:, k, :], in1=yy[:, k, :], op=ALU.add),
                                 reads=[yy.b[k], xT[0].b[k]], writes=[xT[0].b[k]])
                            P.dma(lambda e, k=k: e.dma_start(out=xm_d[k * 128:(k + 1) * 128, c0:c0 + TC], in_=xT[0][:, k, :]),
                                  reads=[xT[0].b[k]])

                    load_tile(0)
                    for t in range(NTC):
                        if t + 1 < NTC:
                            load_tile(t + 1)
                        for c in range(CPT):
                            chunk(t, c)
                        if KC >= 99:
                            tile_tail(t)
                    P.barrier()

            if KSTOP >= 3:
                _pass2()
            def _pass3():
                with ExitStack() as es:
                    NTM = NT // TM
                    w1 = sb(es, "w1", [128, 8, DFF], BF16, nb=8)
                    w2 = sb(es, "w2", [128, 32, D], BF16, nb=32)
                    for k in range(8):
                        P.dma(lambda e, k=k: e.dma_start(out=w1[:, k, :], in_=w_mlp1[l, k * 128:(k + 1) * 128, :]),
                              writes=[w1.b[k]], eng="pool")
                    for k in range(32):
                        P.dma(lambda e, k=k: e.dma_start(out=w2[:, k, :], in_=w_mlp2[l, k * 128:(k + 1) * 128, :]),
                              writes=[w2.b[k]], eng="pool")
                    xT = [sb(es, "m_xT%d" % i, [128, 8, TM], F32, nb=8) for i in range(2)]
                    hT = [sb(es, "m_hT%d" % i, [128, 8, TM], BF16, nb=8) for i in range(2)]
                    fT = sb(es, "m_fT", [128, 32, TM], BF16, nb=32)
                    rl = [sb(es, "m_rl%d" % i, [128, TM], F32) for i in range(3)]
                    yy = sb(es, "m_y", [128, 8, TM], F32, nb=8)
                    rstd = sb(es, "m_rstd", [128, TM], F32)
                    rstd2 = sb(es, "m_rstd2", [128, TM], F32)
                    otok = [sb(es, "m_otok%d" % i, [128, D], F32) for i in range(1)] if l == depth - 1 else None
                    g_pre = vec[:, 2, l, :]
                    g_post = vec[:, 3, l, :]

                    def load_x(t):
                        c0 = t * TM
                        for k in range(8):
                            P.dma(lambda e, k=k: e.dma_start(out=xT[t % 2][:, k, :], in_=xm_d[k * 128:(k + 1) * 128, c0:c0 + TM]),
                                  writes=[xT[t % 2].b[k]])

                    def norm_x(t):
                        h, x = hT[t % 2], xT[t % 2]
                        rms_stats(None, x, TM, h, rstd)
                        for k in range(8):
                            eng = "dve"
                            P.op(eng, lambda e, k=k: e.scalar_tensor_tensor(
                                out=h[:, k, :], in0=x[:, k, :], scalar=g_pre[:, k:k + 1], in1=rstd[:], op0=ALU.mult, op1=ALU.mult),
                                reads=[x.b[k]] + rstd.b + vec.b, writes=[h.b[k]])

                    def stage1(t, lo_, hi_):
                        h = hT[t % 2]
                        for cidx in range(lo_, hi_):
                            pb = bank()
                            for k in range(8):
                                P.op("pe", lambda e, pb=pb, cidx=cidx, k=k: e.matmul(
                                    pb[:, 0:TM], lhsT=w1[:, k, cidx * 128:(cidx + 1) * 128], rhs=h[:, k, :], start=(k == 0), stop=(k == 7)),
                                    reads=[w1.b[k], h.b[k]], writes=pb.b, inc=(k == 7))
                            r = rl[cidx % 3]
                            P.op("act", lambda e, pb=pb, r=r: e.activation(out=r[:], in_=pb[:, 0:TM], func=AF.Relu),
                                 reads=pb.b, writes=r.b)
                            if cidx % 2 == 0:
                                P.op("dve", lambda e, r=r, cidx=cidx: e.tensor_tensor(out=fT[:, cidx, :], in0=r[:], in1=r[:], op=ALU.mult),
                                     reads=r.b, writes=[fT.b[cidx]])
                            else:
                                P.op("act", lambda e, r=r, cidx=cidx: e.activation(out=fT[:, cidx, :], in_=r[:], func=AF.Square),
                                     reads=r.b, writes=[fT.b[cidx]])

                    def stage2(t):
                        x = xT[t % 2]
                        c0 = t * TM
                        for n in range(8):
                            pb = bank()
                            for cidx in range(32):
                                P.op("pe", lambda e, pb=pb, n=n, cidx=cidx: e.matmul(
                                    pb[:, 0:TM], lhsT=w2[:, cidx, n * 128:(n + 1) * 128], rhs=fT[:, cidx, :],
                                    start=(cidx == 0), stop=(cidx == 31)), reads=[w2.b[cidx], fT.b[cidx]], writes=pb.b, inc=(cidx == 31))
                            P.op("act", lambda e, pb=pb, n=n: e.copy(out=yy[:, n, :], in_=pb[:, 0:TM]), reads=pb.b, writes=[yy.b[n]])
                        rms_stats(None, yy, TM, hT[t % 2], rstd2)
                        for k in range(8):
                            eng = "dve"
                            if eng == "dve":
                                P.op(eng, lambda e, k=k: e.scalar_tensor_tensor(
                                    out=yy[:, k, :], in0=yy[:, k, :], scalar=g_post[:, k:k + 1], in1=rstd2[:], op0=ALU.mult, op1=ALU.mult),
                                    reads=[yy.b[k]] + rstd2.b + vec.b, writes=[yy.b[k]])
                            else:
                                P.op(eng, lambda e, k=k: e.tensor_tensor(out=yy[:, k, :], in0=yy[:, k, :], in1=rstd2[:], op=ALU.mult),
                                     reads=[yy.b[k]] + rstd2.b, writes=[yy.b[k]])
                                P.op(eng, lambda e, k=k: e.tensor_scalar(out=yy[:, k, :], in0=yy[:, k, :], scalar1=g_post[:, k:k + 1],
                                                                         scalar2=None, op0=ALU.mult),
                                     reads=[yy.b[k]] + vec.b, writes=[yy.b[k]])
                            P.op(eng, lambda e, k=k: e.tensor_tensor(out=x[:, k, :], in0=x[:, k, :], in1=yy[:, k, :], op=ALU.add),
                                 reads=[yy.b[k], x.b[k]], writes=[x.b[k]])
                            if l < depth - 1:
                                P.dma(lambda e, k=k: e.dma_start(out=xT_d[k * 128:(k + 1) * 128, c0:c0 + TM], in_=x[:, k, :]),
                                      reads=[x.b[k]])
                        if l == depth - 1:
                            for c in range(TM // CH):
                                ot = otok[0]
                                for half in range(2):
                                    pb = bank()
                                    for kk in range(4):
                                        k = half * 4 + kk
                                        P.op("pe", lambda e, pb=pb, k=k, kk=kk, c=c: e.transpose(
                                            pb[:, kk * 128:(kk + 1) * 128], x[:, k, c * CH:(c + 1) * CH], ident_f[:]),
                                            reads=[x.b[k]] + ident_f.b, writes=pb.b, inc=(kk == 3))
                                    P.op("act" if half else "dve",
                                         (lambda e, pb=pb, half=half, ot=ot: e.copy(out=ot[:, half * 512:(half + 1) * 512], in_=pb[:]))
                                         if half else
                                         (lambda e, pb=pb, half=half, ot=ot: e.tensor_copy(out=ot[:, half * 512:(half + 1) * 512], in_=pb[:])),
                                         reads=pb.b + ot.b, writes=ot.b)
                                P.dma(lambda e, c=c, ot=ot: e.dma_start(out=y_out[c0 + c * CH:c0 + (c + 1) * CH, :], in_=ot[:]),
                                      reads=ot.b)

                    load_x(0)
                    norm_x(0)
                    for t in range(NTM):
                        stage1(t, 0, 16)
                        if t + 1 < NTM:
                            load_x(t + 1)
                        stage1(t, 16, 32)
                        if t + 1 < NTM:
                            norm_x(t + 1)
                        stage2(t)
                    P.barrier()

            if KSTOP >= 4:
                _pass3()
        for l_ in range(depth):
            do_layer(l_)
        P.barrier()
        P.emit()
    return nc, P


def host_tables(seq_lens, NT, TC=512):
    NCH = NT // CH
    pos = np.zeros(NT, np.int64)
    seq_id = np.full(NT, -1, np.int64)
    seq_start = np.zeros(NT, np.int64)
    seq_len = np.ones(NT, np.int64) * CH
    o = 0
    for si, L in enumerate(seq_lens):
        pos[o:o + L] = np.arange(L)
        seq_id[o:o + L] = si
        seq_start[o:o + L] = o
        seq_len[o:o + L] = L
        o += L
    for c in range(o // CH, NCH):
        pos[c * CH:(c + 1) * CH] = np.arange(CH)
        seq_id[c * CH:(c + 1) * CH] = 1000 + c
        seq_start[c * CH:(c + 1) * CH] = c * CH
    half = DK // 2
    inv = (ROPE_BASE ** (-np.arange(half, dtype=np.float32) / np.float32(half))).astype(np.float32)
    ang = pos.astype(np.float32)[:, None] * inv[None, :]
    rope_t = np.concatenate([np.cos(ang), np.sin(ang)], axis=1).astype(np.float32)
    cid = seq_id[::CH]
    cont = (cid[:-1] == cid[1:]).astype(np.float32)
    flagkv = np.zeros((128, NCH), np.float32)
    flagkv[0:64, :NCH - 1] = cont[None, :]
    flagkv[64:128, 1:] = cont[None, :]
    scanflag = np.zeros((128, NCH), np.float32)
    scanflag[0:64, :NCH - 1] = cont[None, :]
    scanflag[64:128, :NCH - 1] = cont[::-1][None, :]
    CPT = TC // CH
    bandM = np.zeros((NT // TC, 128, CPT, 4, 128), np.float32)
    bandH = np.zeros((NT // TC, 16, CPT, 4, 128), np.float32)
    i_idx = np.arange(CH)
    for c in range(NCH):
        t, cc = c // CPT, c % CPT
        tok = c * CH + i_idx
        s0 = seq_start[tok]
        s1 = s0 + seq_len[tok]
        for g, w in enumerate(POOL_WINDOWS):
            hw = w // 2
            lo = np.maximum(tok - hw, s0)
            hi = np.minimum(tok + hw, s1)
            cntv = (hi - lo).astype(np.float32)
            for r in range(-8, CH + 8):
                j = c * CH + r
                val = ((j >= lo) & (j < hi)).astype(np.float32) / cntv
                if 0 <= r < CH:
                    val = val - (i_idx == r).astype(np.float32)
                    bandM[t, r, cc, g, :] = val
                elif r < 0:
                    bandH[t, r + 8, cc, g, :] = val
                else:
                    bandH[t, 8 + (r - CH), cc, g, :] = val
    bandM = bandM.reshape(NT // TC, 128, -1).astype(ml_dtypes.bfloat16)
    bandH = bandH.reshape(NT // TC, 16, -1).astype(ml_dtypes.bfloat16)
    return dict(rope_t=rope_t, flagkv=flagkv, scanflag=scanflag, bandM=bandM, bandH=bandH)


def const_table():
    j = np.arange(128, dtype=np.float32)[:, None]
    i = np.arange(128, dtype=np.float32)[None, :]
    c = np.zeros((128, 5 * 128 + 4), np.float32)
    c[:, 516:644] = np.eye(128, dtype=np.float32)
    c[:, 0:128] = np.maximum(i - j, 0)
    c[:, 128:256] = np.maximum(j - i, 0)
    c[:, 256:384] = (i >= j)
    c[:, 384:512] = (j > i)
    c[:, 512] = 127 - j[:, 0]
    c[:, 513] = j[:, 0]
    c[:, 514] = j[:, 0] + 1
    c[:, 515] = 128 - j[:, 0]
    return c


def shared_inputs(inp, depth):
    f = lambda a: np.ascontiguousarray(np.asarray(a, dtype=np.float32))
    vecs = np.zeros((128, 6, depth, 8), np.float32)
    for idx, name in enumerate(["norm_mix_pre", "norm_mix_post", "norm_mlp_pre", "norm_mlp_post", "pool_scale", "ret_gn"]):
        a = f(inp[name])[:depth]
        vecs[:, idx] = a.reshape(depth, 8, 128).transpose(2, 0, 1)
    dec = np.concatenate([f(inp["ret_decay_fwd"])[:depth].reshape(-1), f(inp["ret_decay_bwd"])[:depth].reshape(-1)])
    decays = np.ascontiguousarray(np.broadcast_to(dec[None, :], (128, dec.size)))
    return dict(w_in=f(inp["w_in"])[:depth], w_out=f(inp["w_out"])[:depth], w_mlp1=f(inp["w_mlp1"])[:depth],
                w_mlp2=f(inp["w_mlp2"])[:depth], pool_w=f(inp["pool_w"])[:depth], vecs=vecs, decays=decays,
                consts=const_table())


_CACHE = {}


def run_cores(core_tokens, core_seqlens, NT, depth, inp, n_cores=8):
    key = (NT, depth)
    if key not in _CACHE:
        _CACHE[key] = build(NT, depth)[0]
    nc = _CACHE[key]
    sh = shared_inputs(inp, depth)
    in_maps = []
    for ci in range(n_cores):
        xt = np.zeros((NT, D), np.float32)
        a = core_tokens[ci]
        xt[:a.shape[0]] = a
        m = dict(sh)
        m["x_in"] = xt
        m.update(host_tables(core_seqlens[ci], NT))
        in_maps.append(m)
    res = run_bass_kernel_spmd(nc, in_maps, core_ids=list(range(n_cores)))
    return [r["y_out"] for r in res.results]


def kernel(**inputs):
    xp = np.asarray(inputs["x_prompt"], dtype=np.float32)
    xs = np.asarray(inputs["x_sample"], dtype=np.float32)
    depth = 4
    NT = 16384
    toks, lens = [xp[0]], [[16384]]
    for ci in range(4):
        toks.append(xs[ci * 4:(ci + 1) * 4].reshape(-1, D))
        lens.append([4096] * 4)
    for ci in range(3):
        toks.append(np.zeros((0, D), np.float32))
        lens.append([])
    outs = run_cores(toks, lens, NT, depth, inputs)
    y_prompt = outs[0].reshape(1, 16384, D).astype(np.float32)
    y_sample = np.concatenate([outs[1 + ci].reshape(4, 4096, D) for ci in range(4)], axis=0).astype(np.float32)
    return (y_prompt, y_sample)
```

```python
import numpy as np
import ml_dtypes
from contextlib import ExitStack
import concourse.bass as bass
import concourse.mybir as mybir
from concourse.bass_utils import run_bass_kernel_spmd

F32 = mybir.dt.float32
BF16 = mybir.dt.bfloat16
ALU = mybir.AluOpType
AF = mybir.ActivationFunctionType

D = 1024
HEADS = 8
DK = 64
DV = 128
DIN = 6144
DFF = 4096
EPS = 1e-6
CH = 128
ROPE_BASE = 10000.0
POOL_WINDOWS = (2, 4, 8, 16)

ENGS = ["pe", "act", "dve", "pool", "sp"]
import os as _os
KSTOP = int(_os.environ.get("KSTOP", "4"))


class Buf:
    __slots__ = ("name", "w", "r")

    def __init__(self, name=""):
        self.name = name
        self.w = None
        self.r = {}


class Prog:
    def __init__(self, nc, n_dma_sems=40):
        self.nc = nc
        self.q = {e: [] for e in ENGS}
        self.cnt = {e: 0 for e in ENGS}
        self.waited = {e: {} for e in ENGS}
        self.n_dma = n_dma_sems
        self.dma_i = 0
        self.n_sw = 8
        self.sw_i = 0
        self.dma_cnt = [0] * n_dma_sems
        self.dma_last = [None] * n_dma_sems
        self.ninstr = 0

    def _need(self, eng, tok, waits):
        if tok is None:
            return
        key = (tok[0], tok[1])
        val = tok[2]
        if self.waited[eng].get(key, 0) >= val:
            return
        if tok[0] == "e" and tok[1] == eng and (val > self.cnt[eng] or eng == "pe" or _os.environ.get("NOSELF")):
            return
        self.waited[eng][key] = val
        waits.append((key, val))

    def _deps(self, eng, reads, writes, waits):
        for b in reads:
            self._need(eng, b.w, waits)
        for b in writes:
            self._need(eng, b.w, waits)
            for k, v in b.r.items():
                self._need(eng, (k[0], k[1], v), waits)

    def _mark(self, tok, reads, writes):
        k = (tok[0], tok[1])
        for b in reads:
            if b.r.get(k, 0) < tok[2]:
                b.r[k] = tok[2]
        for b in writes:
            b.w = tok
            b.r = {}

    def op(self, eng, fn, reads=(), writes=(), inc=True):
        waits = []
        self._deps(eng, reads, writes, waits)
        c = self.cnt[eng] + 1
        tok = ("e", eng, c)
        if inc:
            self.cnt[eng] = c
        self.q[eng].append((waits, fn, ("e", inc)))
        self._mark(tok, reads, writes)
        self.ninstr += 1
        return tok

    def dma(self, fn, reads=(), writes=(), eng="sp"):
        waits = []
        self._deps(eng, reads, writes, waits)
        if eng == "pool":
            i = self.n_dma - self.n_sw + self.sw_i
            self.sw_i = (self.sw_i + 1) % self.n_sw
        else:
            i = self.dma_i
            self.dma_i = (i + 1) % (self.n_dma - self.n_sw)
        self._need(eng, self.dma_last[i], waits)
        self.dma_cnt[i] += 16
        tok = ("d", i, self.dma_cnt[i])
        self.dma_last[i] = tok
        self.q[eng].append((waits, fn, ("d", i)))
        self._mark(tok, reads, writes)
        self.ninstr += 1
        return tok

    def barrier(self):
        for e in ENGS:
            waits = []
            for i in range(self.n_dma):
                self._need(e, self.dma_last[i], waits)
            for o in ENGS:
                if o != "sp" and o != e and self.cnt[o] > 0:
                    self._need(e, ("e", o, self.cnt[o]), waits)
            if e != "sp" and self.cnt[e] > 0:
                self._need(e, ("e", e, self.cnt[e]), waits)
            self.q[e].append((waits, None, None))

    def emit(self):
        nc = self.nc
        with ExitStack() as es:
            esem = {e: es.enter_context(nc.semaphore("s_" + e)) for e in ENGS if e != "sp"}
            dsem = [es.enter_context(nc.semaphore("d_%d" % i)) for i in range(self.n_dma)]
            block = es.enter_context(nc.Block())

            def semof(key):
                return esem[key[1]] if key[0] == "e" else dsem[key[1]]

            def run(engname):
                def body(engine):
                    for waits, fn, kind in self.q[engname]:
                        for key, val in waits:
                            engine.wait_ge(semof(key), val)
                        if fn is None:
                            continue
                        ins = fn(engine)
                        if kind[0] == "e":
                            if kind[1]:
                                ins.then_inc(esem[engname], 1)
                        else:
                            ins.then_inc(dsem[kind[1]], 16)
                return body

            block.tensor(run("pe"))
            block.scalar(run("act"))
            block.vector(run("dve"))
            block.gpsimd(run("pool"))
            block.sync(run("sp"))


class T:
    def __init__(self, t, nb=1):
        self.t = t
        self.b = [Buf() for _ in range(nb)]

    def __getitem__(self, k):
        return self.t[k]


def build(NT, depth, TA=512, TC=512, TM=256):
    NCH = NT // CH
    nc = bass.Bass("TRN2", target_bir_lowering=False)
    P = Prog(nc)

    def din(name, shape, dt=F32):
        return nc.dram_tensor(name, list(shape), dt, kind="ExternalInput").ap()

    def dscr(name, shape, dt):
        return nc.dram_tensor(name, list(shape), dt, kind="Internal").ap()

    x_in = din("x_in", [NT, D])
    y_out = nc.dram_tensor("y_out", [NT, D], F32, kind="ExternalOutput").ap()
    w_in = din("w_in", [depth, D, DIN])
    w_out = din("w_out", [depth, D, D])
    w_mlp1 = din("w_mlp1", [depth, D, DFF])
    w_mlp2 = din("w_mlp2", [depth, DFF, D])
    pool_w = din("pool_w", [depth, 4, 256, 256])
    vecs = din("vecs", [128, 6, depth, 8])
    decays = din("decays", [128, 2 * depth * 8])
    rope_t = din("rope_t", [NT, 64])
    flagkv = din("flagkv", [128, NCH])
    scanflag = din("scanflag", [128, NCH])
    bandM = din("bandM", [NT // TC, 128, (TC // CH) * 4 * 128], BF16)
    bandH = din("bandH", [NT // TC, 16, (TC // CH) * 4 * 128], BF16)
    consts = din("consts", [128, 5 * 128 + 4])

    xT_d = dscr("xT_d", [D, NT], F32)
    xm_d = dscr("xm_d", [D, NT], F32)
    qT_d = dscr("qT_d", [512, NT], BF16)
    kT_d = dscr("kT_d", [1024, NT], BF16)
    qsT_d = dscr("qsT_d", [1024, NT], BF16)
    gate_d = dscr("gate_d", [1024, NT], BF16)
    sgp_d = dscr("sgp_d", [1024, NT], BF16)
    v_d = dscr("v_d", [NT, 1024], BF16)
    u_d = dscr("u_d", [NT + 16, 1024], BF16)
    kv_d = dscr("kv_d", [NCH, 128, 1024], F32)
    st_d = dscr("st_d", [NCH, 128, 1024], BF16)

    top = ExitStack()

    _uid = [0]

    def sb(es, name, shape, dt, nb=1):
        _uid[0] += 1
        return T(es.enter_context(nc.sbuf_tensor("%s_%d" % (name, _uid[0]), list(shape), dt)), nb)

    with top:
        psb = [T(top.enter_context(nc.psum_tensor("ps%d" % i, [128, 512], F32))) for i in range(8)]
        ps_i = [0]
        NBANK = 7 if _os.environ.get("K4") else 8

        def bank():
            b = psb[ps_i[0]]
            ps_i[0] = (ps_i[0] + 1) % NBANK
            return b

        ident_b = sb(top, "ident_b", [128, 128], BF16)
        ident_f = sb(top, "ident_f", [128, 128], F32)
        ones_b = sb(top, "ones_b", [128, 128], BF16)
        nhalf = sb(top, "nhalf", [128, 512], F32)
        cst = sb(top, "cst", [128, 5 * 128 + 4], F32)
        vec = sb(top, "vec", [128, 6, depth, 8], F32)
        dcy = sb(top, "dcy", [128, 2 * depth * 8], F32)
        lg = sb(top, "lg", [128, 2 * depth * 8], F32)
        fkv = sb(top, "fkv", [128, NCH], F32)
        sfl = sb(top, "sfl", [128, NCH], F32)
        dmask = sb(top, "dmask", [128, 8, 128], F32)
        dtmp = sb(top, "dtmp", [128, 128], F32)
        zx = sb(top, "zx", [128, 4, 8], F32)
        dec = sb(top, "dec", [128, 8], F32)
        atab = sb(top, "atab", [128, NCH, 8], F32)
        tmpes = ExitStack()
        zero_b = sb(tmpes, "zero_b", [8, 1024], BF16)

        P.op("pool", lambda e: e.memset(ident_f[:], 0.0), writes=ident_f.b)
        P.dma(lambda e: e.dma_start(out=cst[:], in_=consts), writes=cst.b)
        P.dma(lambda e: e.dma_start(out=vec[:], in_=vecs), writes=vec.b)
        P.dma(lambda e: e.dma_start(out=dcy[:], in_=decays), writes=dcy.b)
        P.dma(lambda e: e.dma_start(out=fkv[:], in_=flagkv), writes=fkv.b)
        P.dma(lambda e: e.dma_start(out=sfl[:], in_=scanflag), writes=sfl.b)
        P.op("dve", lambda e: e.tensor_copy(out=ident_f[:], in_=cst[:, 516:644]), reads=cst.b, writes=ident_f.b)
        P.op("dve", lambda e: e.tensor_copy(out=ident_b[:], in_=ident_f[:]), reads=ident_f.b, writes=ident_b.b)
        P.op("pool", lambda e: e.memset(ones_b[:], 1.0), writes=ones_b.b)
        P.op("pool", lambda e: e.memset(nhalf[:], -0.5), writes=nhalf.b)
        P.op("pool", lambda e: e.memset(zero_b[:], 0.0), writes=zero_b.b)
        P.dma(lambda e: e.dma_start(out=u_d[0:8, :], in_=zero_b[:]), reads=zero_b.b)
        P.dma(lambda e: e.dma_start(out=u_d[NT + 8:NT + 16, :], in_=zero_b[:]), reads=zero_b.b)
        P.op("act", lambda e: e.activation(out=lg[:], in_=dcy[:], func=AF.Exp, scale=-1.0), reads=dcy.b, writes=lg.b)
        P.op("act", lambda e: e.activation(out=lg[:], in_=lg[:], func=AF.Ln, bias=1.0, scale=1.0), reads=lg.b, writes=lg.b)
        P.op("dve", lambda e: e.tensor_scalar(out=lg[:], in0=lg[:], scalar1=-1.0, scalar2=None, op0=ALU.mult),
             reads=lg.b, writes=lg.b)
        P.barrier()
        tmpes.close()

        def rms_stats(es_tiles, xT, width, sqr, rstd, src_is_psum=None, msb_bank=None):
            msb = msb_bank if msb_bank is not None else bank()
            K5 = _os.environ.get("K5", "")
            for k in range(8):
                if K5 == "d":
                    P.op("dve", lambda e, k=k: e.tensor_tensor(out=sqr[:, k, 0:width], in0=xT[:, k, 0:width], in1=xT[:, k, 0:width], op=ALU.mult),
                         reads=[xT.b[k]], writes=[sqr.b[k]])
                else:
                    P.op("act", lambda e, k=k: e.activation(out=sqr[:, k, 0:width], in_=xT[:, k, 0:width], func=AF.Square),
                         reads=[xT.b[k]], writes=[sqr.b[k]])
            for k in range(8 if K5 != "s" else 0):
                P.op("pe", lambda e, k=k: e.matmul(msb[:, 0:width], lhsT=ones_b[:], rhs=sqr[:, k, 0:width],
                                                   start=(k == 0), stop=(k == 7)),
                     reads=[sqr.b[k]] + ones_b.b, writes=msb.b, inc=(k == 7))
            P.op("act", lambda e: e.activation(out=rstd[:, 0:width], in_=msb[:, 0:width], func=AF.Ln, scale=1.0 / D, bias=EPS),
                 reads=msb.b, writes=rstd.b)
            P.op("act", lambda e: e.activation(out=rstd[:, 0:width], in_=rstd[:, 0:width], func=AF.Exp, scale=-0.5),
                 reads=rstd.b, writes=rstd.b)

        def do_layer(l):
            lgf = lambda h: lg[:, l * 8 + h:l * 8 + h + 1]
            lgb = lambda h: lg[:, depth * 8 + l * 8 + h:depth * 8 + l * 8 + h + 1]
            for h in range(HEADS):
                P.op("act", lambda e, h=h: e.activation(out=dmask[:, h, :], in_=cst[:, 0:128], func=AF.Exp, scale=lgf(h)),
                     reads=cst.b + lg.b, writes=dmask.b)
                P.op("dve", lambda e, h=h: e.tensor_tensor(out=dmask[:, h, :], in0=dmask[:, h, :], in1=cst[:, 256:384],
                                                           op=ALU.mult), reads=dmask.b + cst.b, writes=dmask.b)
                P.op("act", lambda e, h=h: e.activation(out=dtmp[:], in_=cst[:, 128:256], func=AF.Exp, scale=lgb(h)),
                     reads=cst.b + lg.b, writes=dtmp.b)
                P.op("dve", lambda e, h=h: e.tensor_tensor(out=dtmp[:], in0=dtmp[:], in1=cst[:, 384:512], op=ALU.mult),
                     reads=dtmp.b + cst.b, writes=dtmp.b)
                P.op("dve", lambda e, h=h: e.tensor_tensor(out=dmask[:, h, :], in0=dmask[:, h, :], in1=dtmp[:], op=ALU.add),
                     reads=dmask.b + dtmp.b, writes=dmask.b)
            lo = l * 8
            hb = depth * 8 + l * 8
            for idx, (col, base) in enumerate([(512, lo), (513, hb), (514, lo), (515, hb)]):
                P.op("dve", lambda e, idx=idx, col=col, base=base: e.tensor_scalar(
                    out=zx[:, idx, :], in0=lg[:, base:base + 8], scalar1=cst[:, col:col + 1], scalar2=None, op0=ALU.mult),
                    reads=lg.b + cst.b, writes=zx.b)
            P.op("act", lambda e: e.activation(out=zx[:], in_=zx[:], func=AF.Exp), reads=zx.b, writes=zx.b)
            P.op("act", lambda e: e.activation(out=dec[0:64, :], in_=lg[0:64, lo:lo + 8], func=AF.Exp, scale=128.0),
                 reads=lg.b, writes=dec.b)
            P.op("act", lambda e: e.activation(out=dec[64:128, :], in_=lg[64:128, hb:hb + 8], func=AF.Exp, scale=128.0),
                 reads=lg.b + dec.b, writes=dec.b)
            P.op("dve", lambda e: e.tensor_tensor(out=atab[:], in0=dec[:].unsqueeze(1).to_broadcast([128, NCH, 8]),
                                                  in1=sfl[:].unsqueeze(2).to_broadcast([128, NCH, 8]), op=ALU.mult),
                 reads=dec.b + sfl.b, writes=atab.b)
            P.barrier()

            def _pass0():
                with ExitStack() as es:
                    NTA = NT // TA
                    CPT = TA // CH
                    wA = sb(es, "wA", [128, 8, DIN], BF16, nb=8)
                    for k in range(8 if not _os.environ.get("KNOW") else 0):
                        P.dma(lambda e, k=k: e.dma_start(out=wA[:, k, :], in_=w_in[l, k * 128:(k + 1) * 128, :]),
                              writes=[wA.b[k]], eng="pool")
                    xT = sb(es, "a_xT", [128, 8, TA], F32, nb=8)
                    xtok = [sb(es, "a_xtok%d" % i, [128, D], F32) for i in range(1)] if l == 0 else None
                    hT = [sb(es, "a_hT%d" % i, [128, 8, TA], BF16, nb=8) for i in range(2)]
                    rstd = sb(es, "a_rstd", [128, TA], F32)
                    sig = [sb(es, "a_sig%d" % i, [128, TA], F32) for i in range(3)]
                    gst = [sb(es, "a_gst%d" % i, [128, TA], BF16) for i in range(2)]
                    pst = [sb(es, "a_pst%d" % i, [128, TA], BF16) for i in range(2)]
                    q32 = sb(es, "a_q32", [128, 512], F32)
                    k32 = sb(es, "a_k32", [128, 512], F32)
                    rt1 = sb(es, "a_rt1", [128, 8, 32], F32)
                    rt2 = sb(es, "a_rt2", [128, 8, 32], F32)
                    qbf = sb(es, "a_qbf", [128, 8, 64], BF16)
                    kbf = sb(es, "a_kbf", [128, 8, 64], BF16)
                    qs = sb(es, "a_qs", [128, 8, 2, 64], BF16)
                    kz = sb(es, "a_kz", [128, 8, 2, 64], BF16)
                    vbf = [sb(es, "a_vbf%d" % i, [128, 1024], BF16) for i in range(1)]
                    ubf = [sb(es, "a_ubf%d" % i, [128, 1024], BF16) for i in range(1)]
                    kvs = [sb(es, "a_kvs%d" % i, [128, 1024], F32) for i in range(1)]
                    rope = [sb(es, "a_rope%d" % i, [128, 64], F32) for i in range(2)]
                    qT = sb(es, "a_qT", [128, 4, TA], BF16, nb=CPT)
                    kT = sb(es, "a_kT", [128, 8, TA], BF16, nb=CPT)
                    kpad = sb(es, "a_kpad", [128, 8, 128], BF16)
                    P.op("pool", lambda e: e.memset(kpad[:], 0.0), writes=kpad.b)
                    qsT = sb(es, "a_qsT", [128, 8, TA], BF16, nb=CPT)
                    g_pre = vec[:, 0, l, :]

                    def load_x(t):
                        c0 = t * TA
                        if l > 0:
                            for k in range(8):
                                P.dma(lambda e, k=k: e.dma_start(out=xT[:, k, :], in_=xT_d[k * 128:(k + 1) * 128, c0:c0 + TA]),
                                      writes=[xT.b[k]])
                        else:
                            for c in range(CPT if not _os.environ.get("K6") else int(_os.environ.get("K6"))):
                                xt = xtok[0]
                                P.dma(lambda e, c=c, xt=xt: e.dma_start(out=xt[:], in_=x_in[c0 + c * CH:c0 + (c + 1) * CH, :]),
                                      writes=xt.b)
                                for half in range(2):
                                    pb = bank()
                                    for kk in range(4 if _os.environ.get("K2", "") != "c" else 0):
                                        k = half * 4 + kk
                                        P.op("pe", lambda e, k=k, kk=kk, pb=pb, xt=xt: e.transpose(
                                            pb[:, kk * 128:(kk + 1) * 128], xt[:, k * 128:(k + 1) * 128], ident_f[:]),
                                            reads=xt.b + ident_f.b, writes=pb.b, inc=(kk == 3))
                                    for kk in range(4 if _os.environ.get("K2", "") not in ("b", "c") else 0):
                                        k = half * 4 + kk
                                        K7 = _os.environ.get("K7", "")
                                        useact = (half == 1)
                                        P.op("act" if useact else "dve",
                                             (lambda e, k=k, kk=kk, pb=pb, c=c: e.copy(out=xT[:, k, c * CH:(c + 1) * CH],
                                                                                       in_=pb[:, kk * 128:(kk + 1) * 128]))
                                             if useact else
                                             (lambda e, k=k, kk=kk, pb=pb, c=c: e.tensor_copy(out=xT[:, k, c * CH:(c + 1) * CH],
                                                                                              in_=pb[:, kk * 128:(kk + 1) * 128])),
                                             reads=pb.b, writes=[xT.b[k]])
                            for k in range(8 if _os.environ.get("K2", "") not in ("a", "b", "c") else 0):
                                P.dma(lambda e, k=k: e.dma_start(out=xT_d[k * 128:(k + 1) * 128, c0:c0 + TA], in_=xT[:, k, :]),
                                      reads=[xT.b[k]])

                    def norm_x(t):
                        h = hT[t % 2]
                        rms_stats(None, xT, TA, h, rstd)
                        for k in range(8 if _os.environ.get("K3", "z") >= "d" else 0):
                            eng = "dve"
                            P.op(eng, lambda e, k=k, h=h: e.scalar_tensor_tensor(
                                out=h[:, k, :], in0=xT[:, k, :], scalar=g_pre[:, k:k + 1], in1=rstd[:], op0=ALU.mult, op1=ALU.mult),
                                reads=[xT.b[k]] + rstd.b + vec.b, writes=[h.b[k]])

                    def fm_group(t):
                        h = hT[t % 2]
                        c0 = t * TA
                        for j in range(8):
                            pg, pr, pp = bank(), bank(), bank()
                            for (pb, col) in ((pg, 2048 + j * 128), (pr, 4096 + j * 128), (pp, 5120 + j * 128)):
                                for k in range(8):
                                    P.op("pe", lambda e, pb=pb, col=col, k=k: e.matmul(
                                        pb[:, 0:TA], lhsT=wA[:, k, col:col + 128], rhs=h[:, k, :], start=(k == 0), stop=(k == 7)),
                                        reads=[wA.b[k], h.b[k]], writes=pb.b, inc=(k == 7))
                            s1, s2 = sig[j % 2], sig[2]
                            go, po = gst[j % 2], pst[j % 2]
                            P.op("act", lambda e, pg=pg, s1=s1: e.activation(out=s1[:], in_=pg[:, 0:TA], func=AF.Sigmoid),
                                 reads=pg.b, writes=s1.b)
                            P.op("act", lambda e, pr=pr, s2=s2: e.activation(out=s2[:], in_=pr[:, 0:TA], func=AF.Sigmoid),
                                 reads=pr.b, writes=s2.b)
                            P.op("act", lambda e, pp=pp, po=po: e.activation(out=po[:], in_=pp[:, 0:TA], func=AF.Sigmoid),
                                 reads=pp.b, writes=po.b)
                            P.op("dve", lambda e, pg=pg, s1=s1, j=j: e.scalar_tensor_tensor(
                                out=s1[:], in0=pg[:, 0:TA], scalar=vec[:, 5, l, j:j + 1], in1=s1[:], op0=ALU.mult, op1=ALU.mult),
                                reads=pg.b + s1.b + vec.b, writes=s1.b)
                            P.op("dve", lambda e, s1=s1, s2=s2, go=go: e.tensor_tensor(out=go[:], in0=s1[:], in1=s2[:], op=ALU.mult),
                                 reads=s1.b + s2.b, writes=go.b)
                            P.dma(lambda e, go=go, j=j: e.dma_start(out=gate_d[j * 128:(j + 1) * 128, c0:c0 + TA], in_=go[:]),
                                  reads=go.b)
                            P.dma(lambda e, po=po, j=j: e.dma_start(out=sgp_d[j * 128:(j + 1) * 128, c0:c0 + TA], in_=po[:]),
                                  reads=po.b)

                    def tm_chunk(t, c):
                        h = hT[t % 2]
                        gc = t * CPT + c
                        tok0 = gc * CH
                        cs = slice(c * CH, (c + 1) * CH)
                        rp = rope[gc % 2]
                        P.dma(lambda e, rp=rp: e.dma_start(out=rp[:], in_=rope_t[tok0:tok0 + CH, :]), writes=rp.b)
                        banks = []
                        for col in (0, 512, 1024, 1536, 3072, 3584):
                            pb = bank()
                            banks.append(pb)
                            for k in range(8):
                                P.op("pe", lambda e, pb=pb, col=col, k=k: e.matmul(
                                    pb[:], lhsT=h[:, k, cs], rhs=wA[:, k, col:col + 512], start=(k == 0), stop=(k == 7)),
                                    reads=[wA.b[k], h.b[k]], writes=pb.b, inc=(k == 7))
                        pq, pk, pv0, pv1, pu0, pu1 = banks
                        v_, u_, kv_ = vbf[0], ubf[0], kvs[0]
                        P.op("act", lambda e: e.copy(out=q32[:], in_=pq[:]), reads=pq.b, writes=q32.b)
                        P.op("act", lambda e: e.mul(out=k32[:], in_=pk[:], mul=DK ** -0.5), reads=pk.b, writes=k32.b)
                        P.op("act", lambda e: e.copy(out=v_[:, 0:512], in_=pv0[:]), reads=pv0.b, writes=v_.b)
                        P.op("act", lambda e: e.copy(out=v_[:, 512:1024], in_=pv1[:]), reads=pv1.b + v_.b, writes=v_.b)
                        P.op("act", lambda e: e.copy(out=u_[:, 0:512], in_=pu0[:]), reads=pu0.b, writes=u_.b)
                        P.op("act", lambda e: e.copy(out=u_[:, 512:1024], in_=pu1[:]), reads=pu1.b + u_.b, writes=u_.b)
                        P.dma(lambda e: e.dma_start(out=v_d[tok0:tok0 + CH, :], in_=v_[:]), reads=v_.b)
                        P.dma(lambda e: e.dma_start(out=u_d[8 + tok0:8 + tok0 + CH, :], in_=u_[:]), reads=u_.b)
                        cosb = rp[:, 0:32].unsqueeze(1).to_broadcast([128, 8, 32])
                        sinb = rp[:, 32:64].unsqueeze(1).to_broadcast([128, 8, 32])
                        for (src, dst, eng) in ((q32, qbf, "dve"), (k32, kbf, "dve")):
                            s4 = src[:].rearrange("p (h two d) -> p h two d", h=8, two=2)
                            x1, x2 = s4[:, :, 0, :], s4[:, :, 1, :]
                            rd = src.b + rp.b
                            P.op(eng, lambda e, x1=x1: e.tensor_tensor(out=rt1[:], in0=x1, in1=cosb, op=ALU.mult),
                                 reads=rd, writes=rt1.b)
                            P.op(eng, lambda e, x2=x2: e.tensor_tensor(out=rt2[:], in0=x2, in1=sinb, op=ALU.mult),
                                 reads=rd, writes=rt2.b)
                            P.op(eng, lambda e, dst=dst: e.tensor_tensor(out=dst[:, :, 0:32], in0=rt1[:], in1=rt2[:], op=ALU.subtract),
                                 reads=rt1.b + rt2.b, writes=dst.b)
                            P.op(eng, lambda e, x1=x1: e.tensor_tensor(out=rt1[:], in0=x1, in1=sinb, op=ALU.mult),
                                 reads=rd, writes=rt1.b)
                            P.op(eng, lambda e, x2=x2: e.tensor_tensor(out=rt2[:], in0=x2, in1=cosb, op=ALU.mult),
                                 reads=rd, writes=rt2.b)
                            P.op(eng, lambda e, dst=dst: e.tensor_tensor(out=dst[:, :, 32:64], in0=rt1[:], in1=rt2[:], op=ALU.add),
                                 reads=rt1.b + rt2.b + dst.b, writes=dst.b)
                        for (src, dst, zi, eng) in ((qbf, qs, 2, "dve"), (kbf, kz, 0, "dve")):
                            for d in range(2):
                                P.op(eng, lambda e, src=src, dst=dst, zi=zi, d=d: e.tensor_tensor(
                                    out=dst[:, :, d, :], in0=src[:], in1=zx[:, zi + d, :].unsqueeze(2).to_broadcast([128, 8, 64]),
                                    op=ALU.mult), reads=src.b + zx.b + dst.b, writes=dst.b)
                        pbq = bank()
                        pq16 = pbq[:].bitcast(BF16)
                        for pr_ in range(4):
                            P.op("pe", lambda e, pr_=pr_: e.transpose(pq16[:, pr_ * 128:(pr_ + 1) * 128],
                                                                       qbf[:].rearrange("p h d -> p (h d)")[:, pr_ * 128:(pr_ + 1) * 128], ident_b[:]),
                                 reads=qbf.b + ident_b.b, writes=pbq.b, inc=(pr_ == 3))
                        P.op("act", lambda e: e.copy(out=qT[:, :, cs], in_=pq16[:, 0:512].rearrange("p (a b) -> p a b", a=4)),
                             reads=pbq.b, writes=[qT.b[c]])
                        kp4 = kpad[:].rearrange("p (pr two) x -> p pr two x", two=2)
                        kb4 = kbf[:].rearrange("p (pr two) d -> p pr two d", two=2)
                        P.op("dve", lambda e: e.tensor_copy(out=kp4[:, :, 0, 0:64], in_=kb4[:, :, 0, :]),
                             reads=kbf.b + kpad.b, writes=kpad.b)
                        P.op("dve", lambda e: e.tensor_copy(out=kp4[:, :, 1, 64:128], in_=kb4[:, :, 1, :]),
                             reads=kbf.b + kpad.b, writes=kpad.b)
                        pbk = bank()
                        pk16 = pbk[:].bitcast(BF16)
                        for hh in range(8):
                            P.op("pe", lambda e, hh=hh: e.transpose(pk16[:, hh * 128:(hh + 1) * 128], kpad[:, hh, :], ident_b[:]),
                                 reads=kpad.b + ident_b.b, writes=pbk.b, inc=(hh == 7))
                        P.op("act", lambda e: e.copy(out=kT[:, :, cs], in_=pk16[:].rearrange("p (a b) -> p a b", a=8)),
                             reads=pbk.b, writes=[kT.b[c]])
                        pbs = bank()
                        ps16 = pbs[:].bitcast(BF16)
                        for hh in range(8):
                            P.op("pe", lambda e, hh=hh: e.transpose(ps16[:, hh * 128:(hh + 1) * 128], qs[:].rearrange("p h t d -> p (h t d)")[:, hh * 128:(hh + 1) * 128], ident_b[:]),
                                 reads=qs.b + ident_b.b, writes=pbs.b, inc=(hh == 7))
                        P.op("act", lambda e: e.copy(out=qsT[:, :, cs], in_=ps16[:].rearrange("p (a b) -> p a b", a=8)),
                             reads=pbs.b, writes=[qsT.b[c]])
                        for half in range(2):
                            pb = bank()
                            for hh in range(4):
                                hd = half * 4 + hh
                                P.op("pe", lambda e, hd=hd, hh=hh, pb=pb: e.matmul(
                                    pb[:, hh * 128:(hh + 1) * 128], lhsT=kz[:].rearrange("p h t d -> p (h t d)")[:, hd * 128:(hd + 1) * 128], rhs=v_[:, hd * 128:(hd + 1) * 128],
                                    start=True, stop=True), reads=kz.b + v_.b, writes=pb.b, inc=(hh == 3))
                            P.op("dve", lambda e, pb=pb, half=half: e.tensor_scalar(
                                out=kv_[:, half * 512:(half + 1) * 512], in0=pb[:], scalar1=fkv[:, gc:gc + 1], scalar2=None,
                                op0=ALU.mult), reads=pb.b + fkv.b + kv_.b, writes=kv_.b)
                        P.dma(lambda e: e.dma_start(out=kv_d[gc], in_=kv_[:]), reads=kv_.b)

                    def store_T(t):
                        c0 = t * TA
                        for pr_ in range(4):
                            P.dma(lambda e, pr_=pr_: e.dma_start(out=qT_d[pr_ * 128:(pr_ + 1) * 128, c0:c0 + TA], in_=qT[:, pr_, :]),
                                  reads=qT.b)
                        for hh in range(8):
                            P.dma(lambda e, hh=hh: e.dma_start(out=qsT_d[hh * 128:(hh + 1) * 128, c0:c0 + TA], in_=qsT[:, hh, :]),
                                  reads=qsT.b)
                            P.dma(lambda e, hh=hh: e.dma_start(out=kT_d[hh * 128:(hh + 1) * 128, c0:c0 + TA], in_=kT[:, hh, :]),
                                  reads=kT.b)

                    KSUB = int(_os.environ.get("KSUB", "99"))
                    if KSUB >= 2:
                        load_x(0)
                    if KSUB >= 3:
                        norm_x(0)
                    if KSUB == 4:
                        fm_group(0)
                    if KSUB == 5:
                        tm_chunk(0, 0)
                    for t in range(NTA if KSUB >= 99 else 0):
                        fm_group(t)
                        tm_chunk(t, 0)
                        if t + 1 < NTA:
                            load_x(t + 1)
                            norm_x(t + 1)
                        for c in range(1, CPT):
                            tm_chunk(t, c)
                        store_T(t)
                    P.barrier()

            if KSTOP >= 1:
                _pass0()
            def _pass1():
                with ExitStack() as es:
                    S = sb(es, "b_S", [128, 1024], F32, nb=8)
                    kvl = [sb(es, "b_kv%d" % i, [128, 1024], F32) for i in range(4)]
                    so = [sb(es, "b_so%d" % i, [128, 1024], BF16) for i in range(3)]
                    P.op("pool", lambda e: e.memset(S[:], 0.0), writes=S.b)
                    P.op("pool", lambda e: e.memset(so[2][:], 0.0), writes=so[2].b)
                    P.dma(lambda e: e.dma_start(out=st_d[0, 0:64, :], in_=so[2][0:64, :]), reads=so[2].b)
                    P.dma(lambda e: e.dma_start(out=st_d[NCH - 1, 64:128, :], in_=so[2][64:128, :]), reads=so[2].b)

                    def ld(s):
                        kk = kvl[s % 4]
                        P.dma(lambda e: e.dma_start(out=kk[0:64, :], in_=kv_d[s, 0:64, :]), writes=kk.b)
                        P.dma(lambda e: e.dma_start(out=kk[64:128, :], in_=kv_d[NCH - 1 - s, 64:128, :]), writes=kk.b)

                    for s in range(min(3, NCH - 1)):
                        ld(s)
                    for s in range(NCH - 1):
                        if s + 3 < NCH - 1:
                            ld(s + 3)
                        kk = kvl[s % 4]
                        o = so[s % 2]
                        for hh in range(8):
                            P.op("dve", lambda e, hh=hh, kk=kk, s=s: e.scalar_tensor_tensor(
                                out=S[:, hh * 128:(hh + 1) * 128], in0=S[:, hh * 128:(hh + 1) * 128], scalar=atab[:, s, hh:hh + 1],
                                in1=kk[:, hh * 128:(hh + 1) * 128], op0=ALU.mult, op1=ALU.add),
                                reads=[S.b[hh]] + kk.b + atab.b, writes=[S.b[hh]])
                        P.op("act", lambda e, o=o: e.copy(out=o[:], in_=S[:]), reads=S.b, writes=o.b)
                        P.dma(lambda e, o=o, s=s: e.dma_start(out=st_d[s + 1, 0:64, :], in_=o[0:64, :]), reads=o.b)
                        P.dma(lambda e, o=o, s=s: e.dma_start(out=st_d[NCH - 2 - s, 64:128, :], in_=o[64:128, :]), reads=o.b)
                    P.barrier()

            if KSTOP >= 2:
                _pass1()
            def _pass2():
                with ExitStack() as es:
                    NTC = NT // TC
                    CPT = TC // CH
                    wO = sb(es, "wO", [128, 8, D], BF16, nb=8)
                    wP = sb(es, "wP", [128, 4, 2, 256], BF16, nb=8)
                    for k in range(8):
                        P.dma(lambda e, k=k: e.dma_start(out=wO[:, k, :], in_=w_out[l, k * 128:(k + 1) * 128, :]),
                              writes=[wO.b[k]], eng="pool")
                    for g in range(4):
                        for cc in range(2):
                            P.dma(lambda e, g=g, cc=cc: e.dma_start(out=wP[:, g, cc, :], in_=pool_w[l, g, cc * 128:(cc + 1) * 128, :]),
                                  writes=[wP.b[2 * g + cc]], eng="pool")
                    NBUF = 2
                    qT = [sb(es, "c_qT%d" % i, [128, 4, TC], BF16) for i in range(NBUF)]
                    kT = [sb(es, "c_kT%d" % i, [128, 8, TC], BF16) for i in range(NBUF)]
                    qsT = [sb(es, "c_qsT%d" % i, [128, 8, TC], BF16) for i in range(NBUF)]
                    gat = [sb(es, "c_gat%d" % i, [128, 8, TC], BF16) for i in range(NBUF)]
                    sgp = [sb(es, "c_sgp%d" % i, [128, 8, TC], BF16) for i in range(NBUF)]
                    xT = [sb(es, "c_xT%d" % i, [128, 8, TC], F32, nb=8) for i in range(1)]
                    bM = [sb(es, "c_bM%d" % i, [128, CPT * 4 * 128], BF16) for i in range(NBUF)]
                    bH = [sb(es, "c_bH%d" % i, [128, CPT * 4 * 128], BF16) for i in range(NBUF)]
                    uh = [sb(es, "c_uh%d" % i, [128, 1024], BF16) for i in range(2)]
                    for tz in bH + uh:
                        P.op("pool", lambda e, tz=tz: e.memset(tz[:], 0.0), writes=tz.b)
                    vv = [sb(es, "c_v%d" % i, [128, 1024], BF16) for i in range(2)]
                    st = [sb(es, "c_st%d" % i, [128, 1024], BF16) for i in range(2)]
                    uu = [sb(es, "c_u%d" % i, [128, 1024], BF16) for i in range(2)]
                    PT = [sb(es, "c_PT%d" % i, [128, 8, 128], BF16) for i in range(2)]
                    on = [sb(es, "c_on%d" % i, [128, 8, 128], BF16) for i in range(2)]
                    stats2 = [sb(es, "c_stats%d" % i, [128, 8, 6], F32) for i in range(2)]
                    mv2 = [sb(es, "c_mv%d" % i, [128, 8, 2], F32) for i in range(2)]
                    rs2 = [sb(es, "c_rs%d" % i, [128, 8], F32) for i in range(2)]
                    nmr2 = [sb(es, "c_nmr%d" % i, [128, 8], F32) for i in range(2)]
                    mT = sb(es, "c_mT", [128, 8, TC], BF16, nb=8)
                    pT = sb(es, "c_pT", [128, 8, TC], BF16, nb=8)
                    tmp = [sb(es, "c_tmp%d" % i, [128, TC], BF16) for i in range(2)]
                    yy = sb(es, "c_y", [128, 8, TC], F32, nb=8)
                    rstd = sb(es, "c_rstd", [128, TC], F32)
                    g_post = vec[:, 1, l, :]
                    pscale = vec[:, 4, l, :]

                    def load_tile(t):
                        i = t % NBUF
                        c0 = t * TC
                        for pr_ in range(4):
                            P.dma(lambda e, pr_=pr_: e.dma_start(out=qT[i][:, pr_, :], in_=qT_d[pr_ * 128:(pr_ + 1) * 128, c0:c0 + TC]),
                                  writes=qT[i].b)
                        for hh in range(8):
                            P.dma(lambda e, hh=hh: e.dma_start(out=qsT[i][:, hh, :], in_=qsT_d[hh * 128:(hh + 1) * 128, c0:c0 + TC]),
                                  writes=qsT[i].b)
                            P.dma(lambda e, hh=hh: e.dma_start(out=kT[i][:, hh, :], in_=kT_d[hh * 128:(hh + 1) * 128, c0:c0 + TC]),
                                  writes=kT[i].b)
                            P.dma(lambda e, hh=hh: e.dma_start(out=gat[i][:, hh, :], in_=gate_d[hh * 128:(hh + 1) * 128, c0:c0 + TC]),
                                  writes=gat[i].b)
                            P.dma(lambda e, hh=hh: e.dma_start(out=sgp[i][:, hh, :], in_=sgp_d[hh * 128:(hh + 1) * 128, c0:c0 + TC]),
                                  writes=sgp[i].b)
                        P.dma(lambda e: e.dma_start(out=bM[i][:], in_=bandM[t]), writes=bM[i].b)
                        P.dma(lambda e: e.dma_start(out=bH[i][0:16, :], in_=bandH[t]), writes=bH[i].b)

                    KC = int(_os.environ.get("KC", "99"))

                    def chunk(t, c):
                        i = t % NBUF
                        gc = t * CPT + c
                        tok0 = gc * CH
                        cs = slice(c * CH, (c + 1) * CH)
                        v_, s_, u_ = vv[gc % 2], st[gc % 2], uu[gc % 2]
                        pt_, on_ = PT[gc % 2], on[gc % 2]
                        stats, mv, rs, nmr = stats2[gc % 2], mv2[gc % 2], rs2[gc % 2], nmr2[gc % 2]
                        P.dma(lambda e: e.dma_start(out=v_[:], in_=v_d[tok0:tok0 + CH, :]), writes=v_.b)
                        P.dma(lambda e: e.dma_start(out=s_[:], in_=st_d[gc]), writes=s_.b)
                        P.dma(lambda e: e.dma_start(out=u_[:], in_=u_d[8 + tok0:8 + tok0 + CH, :]), writes=u_.b)
                        uh_ = uh[gc % 2]
                        P.dma(lambda e: e.dma_start(out=uh_[0:8, :], in_=u_d[tok0:tok0 + 8, :]), writes=uh_.b)
                        P.dma(lambda e: e.dma_start(out=uh_[8:16, :], in_=u_d[tok0 + CH + 8:tok0 + CH + 16, :]), writes=uh_.b)
                        for half in range(2):
                            pb = psb[half]
                            for hh in range(4):
                                hd = half * 4 + hh
                                pr_, base = hd // 2, (hd % 2) * 64
                                P.op("pe", lambda e, pb=pb, hh=hh, pr_=pr_, hd=hd: e.matmul(
                                    pb[:, hh * 128:(hh + 1) * 128], lhsT=kT[i][:, hd, cs], rhs=qT[i][:, pr_, cs],
                                    start=True, stop=True), reads=kT[i].b + qT[i].b, writes=pb.b, inc=(hh == 3))
                            if _os.environ.get("KD") == "nomask":
                                continue
                            P.op("dve", lambda e, pb=pb, half=half: e.tensor_tensor(
                                out=pt_[:, half * 4:(half + 1) * 4, :], in0=pb[:].rearrange("p (a b) -> p a b", a=4),
                                in1=dmask[:, half * 4:(half + 1) * 4, :], op=ALU.mult),
                                reads=pb.b + dmask.b + pt_.b, writes=pt_.b)
                        obanks = []
                        for half in range(2):
                            pb = psb[2 + 2 * (gc % 2) + half]
                            obanks.append(pb)
                            for hh in range(4):
                                hd = half * 4 + hh
                                P.op("pe", lambda e, pb=pb, hh=hh, hd=hd: e.matmul(
                                    pb[:, hh * 128:(hh + 1) * 128], lhsT=pt_[:, hd, :], rhs=v_[:, hd * 128:(hd + 1) * 128],
                                    start=True, stop=False), reads=pt_.b + v_.b, writes=pb.b, inc=False)
                                P.op("pe", lambda e, pb=pb, hh=hh, hd=hd: e.matmul(
                                    pb[:, hh * 128:(hh + 1) * 128], lhsT=qsT[i][:, hd, cs], rhs=s_[:, hd * 128:(hd + 1) * 128],
                                    start=False, stop=True), reads=qsT[i].b + s_.b, writes=pb.b, inc=(hh == 3))
                        yield
                        for half in range(2):
                            pb = obanks[half]
                            for hh in range(4):
                                hd = half * 4 + hh
                                P.op("dve", lambda e, pb=pb, hh=hh, hd=hd: e.bn_stats(out=stats[:, hd, :], in_=pb[:, hh * 128:(hh + 1) * 128]),
                                     reads=pb.b + stats.b, writes=stats.b)
                        for hd in range(8):
                            P.op("dve", lambda e, hd=hd: e.bn_aggr(out=mv[:, hd, :], in_=stats[:, hd, :]),
                                 reads=stats.b + mv.b, writes=mv.b)
                        P.op("act", lambda e: e.activation(out=rs[:], in_=mv[:, :, 1], func=AF.Ln, scale=1.0, bias=EPS),
                             reads=mv.b, writes=rs.b)
                        P.op("act", lambda e: e.activation(out=rs[:], in_=rs[:], func=AF.Exp, scale=-0.5),
                             reads=rs.b, writes=rs.b)
                        P.op("dve", lambda e: e.scalar_tensor_tensor(out=nmr[:], in0=mv[:, :, 0], scalar=-1.0, in1=rs[:],
                                                                      op0=ALU.mult, op1=ALU.mult), reads=mv.b + rs.b, writes=nmr.b)
                        for half in range(2):
                            pb = obanks[half]
                            for hh in range(4):
                                hd = half * 4 + hh
                                P.op("act", lambda e, pb=pb, hh=hh, hd=hd: e.activation(
                                    out=on_[:, hd, :], in_=pb[:, hh * 128:(hh + 1) * 128], func=AF.Identity,
                                    scale=rs[:, hd:hd + 1], bias=nmr[:, hd:hd + 1]),
                                    reads=pb.b + rs.b + nmr.b + on_.b, writes=on_.b)
                        pbt = psb[6]
                        p16 = pbt[:].bitcast(BF16)
                        for hd in range(8):
                            P.op("pe", lambda e, hd=hd: e.transpose(p16[:, hd * 128:(hd + 1) * 128], on_[:, hd, :], ident_b[:]),
                                 reads=on_.b + ident_b.b, writes=pbt.b, inc=(hd == 7))
                        for half in range(2):
                            P.op("dve" if half == 0 else "pool" if False else "dve", lambda e, half=half: e.tensor_tensor(
                                out=mT[:, half * 4:(half + 1) * 4, cs],
                                in0=p16[:, half * 512:(half + 1) * 512].rearrange("p (a b) -> p a b", a=4),
                                in1=gat[i][:, half * 4:(half + 1) * 4, cs], op=ALU.mult),
                                reads=pbt.b + gat[i].b, writes=[mT.b[k] for k in range(half * 4, half * 4 + 4)])
                        for half in range(2):
                            pb = psb[7 - half]
                            for ff in range(4):
                                f = half * 4 + ff
                                g = f // 2
                                bo = (c * 4 + g) * 128
                                P.op("pe", lambda e, pb=pb, ff=ff, f=f, bo=bo: e.matmul(
                                    pb[:, ff * 128:(ff + 1) * 128], lhsT=u_[:, f * 128:(f + 1) * 128], rhs=bM[i][:, bo:bo + 128],
                                    start=True, stop=False), reads=u_.b + bM[i].b, writes=pb.b, inc=False)
                                P.op("pe", lambda e, pb=pb, ff=ff, f=f, bo=bo: e.matmul(
                                    pb[:, ff * 128:(ff + 1) * 128], lhsT=uh_[:, f * 128:(f + 1) * 128], rhs=bH[i][:, bo:bo + 128],
                                    start=False, stop=True), reads=uh_.b + bH[i].b, writes=pb.b, inc=(ff == 3))
                            P.op("act", lambda e, pb=pb, half=half: e.copy(
                                out=pT[:, half * 4:(half + 1) * 4, cs], in_=pb[:].rearrange("p (a b) -> p a b", a=4)),
                                reads=pb.b, writes=[pT.b[k] for k in range(half * 4, half * 4 + 4)])

                    tb_i = [0]

                    def tail_bank():
                        b_ = psb[(0, 1, 6, 7)[tb_i[0] % 4]]
                        tb_i[0] += 1
                        return b_

                    def tile_tail(t):
                        i = t % NBUF
                        c0 = t * TC
                        for hh in range(8):
                            P.dma(lambda e, hh=hh: e.dma_start(out=xT[0][:, hh, :], in_=xT_d[hh * 128:(hh + 1) * 128, c0:c0 + TC]),
                                  writes=[xT[0].b[hh]])
                        for dchunk in range(8):
                            g, dc = dchunk // 2, dchunk % 2
                            pb = tail_bank()
                            for cc in range(2):
                                P.op("pe", lambda e, pb=pb, g=g, dc=dc, cc=cc: e.matmul(
                                    pb[:, 0:TC], lhsT=wP[:, g, cc, dc * 128:(dc + 1) * 128], rhs=pT[:, 2 * g + cc, :],
                                    start=(cc == 0), stop=(cc == 1)), reads=[wP.b[2 * g + cc], pT.b[2 * g + cc]], writes=pb.b, inc=(cc == 1))
                            tm_ = tmp[dchunk % 2]
                            P.op("dve", lambda e, pb=pb, dchunk=dchunk, tm_=tm_: e.scalar_tensor_tensor(
                                out=tm_[:], in0=pb[:, 0:TC], scalar=pscale[:, dchunk:dchunk + 1], in1=sgp[i][:, dchunk, :],
                                op0=ALU.mult, op1=ALU.mult), reads=pb.b + sgp[i].b + vec.b, writes=tm_.b)
                            P.op("dve", lambda e, dchunk=dchunk, tm_=tm_: e.tensor_tensor(
                                out=mT[:, dchunk, :], in0=mT[:, dchunk, :], in1=tm_[:], op=ALU.add),
                                reads=tm_.b + [mT.b[dchunk]], writes=[mT.b[dchunk]])
                        for n in range(8):
                            pb = tail_bank()
                            for k in range(8):
                                P.op("pe", lambda e, pb=pb, n=n, k=k: e.matmul(
                                    pb[:, 0:TC], lhsT=wO[:, k, n * 128:(n + 1) * 128], rhs=mT[:, k, :], start=(k == 0), stop=(k == 7)),
                                    reads=[wO.b[k], mT.b[k]], writes=pb.b, inc=(k == 7))
                            P.op("act", lambda e, pb=pb, n=n: e.copy(out=yy[:, n, :], in_=pb[:, 0:TC]), reads=pb.b, writes=[yy.b[n]])
                        rms_stats(None, yy, TC, pT, rstd, msb_bank=tail_bank())
                        for k in range(8):
                            eng = "dve"
                            if eng == "dve":
                                P.op(eng, lambda e, k=k: e.scalar_tensor_tensor(
                                    out=yy[:, k, :], in0=yy[:, k, :], scalar=g_post[:, k:k + 1], in1=rstd[:], op0=ALU.mult, op1=ALU.mult),
                                    reads=[yy.b[k]] + rstd.b + vec.b, writes=[yy.b[k]])
                            else:
                                P.op(eng, lambda e, k=k: e.tensor_tensor(out=yy[:, k, :], in0=yy[:, k, :], in1=rstd[:], op=ALU.mult),
                                     reads=[yy.b[k]] + rstd.b, writes=[yy.b[k]])
                                P.op(eng, lambda e, k=k: e.tensor_scalar(out=yy[:, k, :], in0=yy[:, k, :], scalar1=g_post[:, k:k + 1],
                                                                         scalar2=None, op0=ALU.mult),
                                     reads=[yy.b[k]] + vec.b, writes=[yy.b[k]])
                            P.op(eng, lambda e, k=k: e.tensor_tensor(out=xT[0][:, k, :], in0=xT[0][:, k, :], in1=yy[:, k, :], op=ALU.add),
                                 reads=[yy.b[k], xT[0].b[k]], writes=[xT[0].b[k]])
                            P.dma(lambda e, k=k: e.dma_start(out=xm_d[k * 128:(k + 1) * 128, c0:c0 + TC], in_=xT[0][:, k, :]),
                                  reads=[xT[0].b[k]])

                    gens = {}
                    load_tile(0)
                    for g in range(NCH + 1):
                        if g < NCH:
                            t, c = divmod(g, CPT)
                            if c == 1 and t + 1 < NTC:
                                load_tile(t + 1)
                            gens[g] = chunk(t, c)
                            next(gens[g])
                        if g >= 1:
                            for _ in gens.pop(g - 1):
                                pass
                            tp, cp = divmod(g - 1, CPT)
                            if cp == CPT - 1:
                                tile_tail(tp)
                    P.barrier()

            if KSTOP >= 3:
                _pass2()
            def _pass3():
                with ExitStack() as es:
                    NTM = NT // TM
                    w1 = sb(es, "w1", [128, 8, DFF], BF16, nb=8)
                    w2 = sb(es, "w2", [128, 32, D], BF16, nb=32)
                    for k in range(8):
                        P.dma(lambda e, k=k: e.dma_start(out=w1[:, k, :], in_=w_mlp1[l, k * 128:(k + 1) * 128, :]),
                              writes=[w1.b[k]], eng="pool")
                    for k in range(32):
                        P.dma(lambda e, k=k: e.dma_start(out=w2[:, k, :], in_=w_mlp2[l, k * 128:(k + 1) * 128, :]),
                              writes=[w2.b[k]], eng="pool")
                    xT = [sb(es, "m_xT%d" % i, [128, 8, TM], F32, nb=8) for i in range(2)]
                    hT = [sb(es, "m_hT%d" % i, [128, 8, TM], BF16, nb=8) for i in range(2)]
                    fT = sb(es, "m_fT", [128, 32, TM], BF16, nb=32)
                    rl = [sb(es, "m_rl%d" % i, [128, TM], F32) for i in range(3)]
                    yy = sb(es, "m_y", [128, 8, TM], F32, nb=8)
                    rstd = sb(es, "m_rstd", [128, TM], F32)
                    rstd2 = sb(es, "m_rstd2", [128, TM], F32)
                    otok = [sb(es, "m_otok%d" % i, [128, D], F32) for i in range(1)] if l == depth - 1 else None
                    g_pre = vec[:, 2, l, :]
                    g_post = vec[:, 3, l, :]

                    def load_x(t):
                        c0 = t * TM
                        for k in range(8):
                            P.dma(lambda e, k=k: e.dma_start(out=xT[t % 2][:, k, :], in_=xm_d[k * 128:(k + 1) * 128, c0:c0 + TM]),
                                  writes=[xT[t % 2].b[k]])

                    def norm_x(t):
                        h, x = hT[t % 2], xT[t % 2]
                        rms_stats(None, x, TM, h, rstd)
                        for k in range(8):
                            eng = "dve"
                            P.op(eng, lambda e, k=k: e.scalar_tensor_tensor(
                                out=h[:, k, :], in0=x[:, k, :], scalar=g_pre[:, k:k + 1], in1=rstd[:], op0=ALU.mult, op1=ALU.mult),
                                reads=[x.b[k]] + rstd.b + vec.b, writes=[h.b[k]])

                    def stage1(t, lo_, hi_):
                        h = hT[t % 2]
                        for cidx in range(lo_, hi_):
                            pb = bank()
                            for k in range(8):
                                P.op("pe", lambda e, pb=pb, cidx=cidx, k=k: e.matmul(
                                    pb[:, 0:TM], lhsT=w1[:, k, cidx * 128:(cidx + 1) * 128], rhs=h[:, k, :], start=(k == 0), stop=(k == 7)),
                                    reads=[w1.b[k], h.b[k]], writes=pb.b, inc=(k == 7))
                            r = rl[cidx % 3]
                            P.op("act", lambda e, pb=pb, r=r: e.activation(out=r[:], in_=pb[:, 0:TM], func=AF.Relu),
                                 reads=pb.b, writes=r.b)
                            if cidx % 2 == 0:
                                P.op("dve", lambda e, r=r, cidx=cidx: e.tensor_tensor(out=fT[:, cidx, :], in0=r[:], in1=r[:], op=ALU.mult),
                                     reads=r.b, writes=[fT.b[cidx]])
                            else:
                                P.op("act", lambda e, r=r, cidx=cidx: e.activation(out=fT[:, cidx, :], in_=r[:], func=AF.Square),
                                     reads=r.b, writes=[fT.b[cidx]])

                    def stage2(t):
                        x = xT[t % 2]
                        c0 = t * TM
                        for n in range(8):
                            pb = bank()
                            for cidx in range(32):
                                P.op("pe", lambda e, pb=pb, n=n, cidx=cidx: e.matmul(
                                    pb[:, 0:TM], lhsT=w2[:, cidx, n * 128:(n + 1) * 128], rhs=fT[:, cidx, :],
                                    start=(cidx == 0), stop=(cidx == 31)), reads=[w2.b[cidx], fT.b[cidx]], writes=pb.b, inc=(cidx == 31))
                            P.op("act", lambda e, pb=pb, n=n: e.copy(out=yy[:, n, :], in_=pb[:, 0:TM]), reads=pb.b, writes=[yy.b[n]])
                        rms_stats(None, yy, TM, hT[t % 2], rstd2)
                        for k in range(8):
                            eng = "dve"
                            if eng == "dve":
                                P.op(eng, lambda e, k=k: e.scalar_tensor_tensor(
                                    out=yy[:, k, :], in0=yy[:, k, :], scalar=g_post[:, k:k + 1], in1=rstd2[:], op0=ALU.mult, op1=ALU.mult),
                                    reads=[yy.b[k]] + rstd2.b + vec.b, writes=[yy.b[k]])
                            else:
                                P.op(eng, lambda e, k=k: e.tensor_tensor(out=yy[:, k, :], in0=yy[:, k, :], in1=rstd2[:], op=ALU.mult),
                                     reads=[yy.b[k]] + rstd2.b, writes=[yy.b[k]])
                                P.op(eng, lambda e, k=k: e.tensor_scalar(out=yy[:, k, :], in0=yy[:, k, :], scalar1=g_post[:, k:k + 1],
                                                                         scalar2=None, op0=ALU.mult),
                                     reads=[yy.b[k]] + vec.b, writes=[yy.b[k]])
                            P.op(eng, lambda e, k=k: e.tensor_tensor(out=x[:, k, :], in0=x[:, k, :], in1=yy[:, k, :], op=ALU.add),
                                 reads=[yy.b[k], x.b[k]], writes=[x.b[k]])
                            if l < depth - 1:
                                P.dma(lambda e, k=k: e.dma_start(out=xT_d[k * 128:(k + 1) * 128, c0:c0 + TM], in_=x[:, k, :]),
                                      reads=[x.b[k]])
                        if l == depth - 1:
                            for c in range(TM // CH):
                                ot = otok[0]
                                for half in range(2):
                                    pb = bank()
                                    for kk in range(4):
                                        k = half * 4 + kk
                                        P.op("pe", lambda e, pb=pb, k=k, kk=kk, c=c: e.transpose(
                                            pb[:, kk * 128:(kk + 1) * 128], x[:, k, c * CH:(c + 1) * CH], ident_f[:]),
                                            reads=[x.b[k]] + ident_f.b, writes=pb.b, inc=(kk == 3))
                                    P.op("act" if half else "dve",
                                         (lambda e, pb=pb, half=half, ot=ot: e.copy(out=ot[:, half * 512:(half + 1) * 512], in_=pb[:]))
                                         if half else
                                         (lambda e, pb=pb, half=half, ot=ot: e.tensor_copy(out=ot[:, half * 512:(half + 1) * 512], in_=pb[:])),
                                         reads=pb.b + ot.b, writes=ot.b)
                                P.dma(lambda e, c=c, ot=ot: e.dma_start(out=y_out[c0 + c * CH:c0 + (c + 1) * CH, :], in_=ot[:]),
                                      reads=ot.b)

                    load_x(0)
                    norm_x(0)
                    for t in range(NTM):
                        stage1(t, 0, 16)
                        if t + 1 < NTM:
                            load_x(t + 1)
                        stage1(t, 16, 32)
                        if t + 1 < NTM:
                            norm_x(t + 1)
                        stage2(t)
                    P.barrier()

            if KSTOP >= 4:
                _pass3()
        for l_ in range(depth):
            do_layer(l_)
        P.barrier()
        P.emit()
    return nc, P


def host_tables(seq_lens, NT, TC=512):
    NCH = NT // CH
    pos = np.zeros(NT, np.int64)
    seq_id = np.full(NT, -1, np.int64)
    seq_start = np.zeros(NT, np.int64)
    seq_len = np.ones(NT, np.int64) * CH
    o = 0
    for si, L in enumerate(seq_lens):
        pos[o:o + L] = np.arange(L)
        seq_id[o:o + L] = si
        seq_start[o:o + L] = o
        seq_len[o:o + L] = L
        o += L
    for c in range(o // CH, NCH):
        pos[c * CH:(c + 1) * CH] = np.arange(CH)
        seq_id[c * CH:(c + 1) * CH] = 1000 + c
        seq_start[c * CH:(c + 1) * CH] = c * CH
    half = DK // 2
    inv = (ROPE_BASE ** (-np.arange(half, dtype=np.float32) / np.float32(half))).astype(np.float32)
    ang = pos.astype(np.float32)[:, None] * inv[None, :]
    rope_t = np.concatenate([np.cos(ang), np.sin(ang)], axis=1).astype(np.float32)
    cid = seq_id[::CH]
    cont = (cid[:-1] == cid[1:]).astype(np.float32)
    flagkv = np.zeros((128, NCH), np.float32)
    flagkv[0:64, :NCH - 1] = cont[None, :]
    flagkv[64:128, 1:] = cont[None, :]
    scanflag = np.zeros((128, NCH), np.float32)
    scanflag[0:64, :NCH - 1] = cont[None, :]
    scanflag[64:128, :NCH - 1] = cont[::-1][None, :]
    CPT = TC // CH
    bandM = np.zeros((NT // TC, 128, CPT, 4, 128), np.float32)
    bandH = np.zeros((NT // TC, 16, CPT, 4, 128), np.float32)
    i_idx = np.arange(CH)
    for c in range(NCH):
        t, cc = c // CPT, c % CPT
        tok = c * CH + i_idx
        s0 = seq_start[tok]
        s1 = s0 + seq_len[tok]
        for g, w in enumerate(POOL_WINDOWS):
            hw = w // 2
            lo = np.maximum(tok - hw, s0)
            hi = np.minimum(tok + hw, s1)
            cntv = (hi - lo).astype(np.float32)
            for r in range(-8, CH + 8):
                j = c * CH + r
                val = ((j >= lo) & (j < hi)).astype(np.float32) / cntv
                if 0 <= r < CH:
                    val = val - (i_idx == r).astype(np.float32)
                    bandM[t, r, cc, g, :] = val
                elif r < 0:
                    bandH[t, r + 8, cc, g, :] = val
                else:
                    bandH[t, 8 + (r - CH), cc, g, :] = val
    bandM = bandM.reshape(NT // TC, 128, -1).astype(ml_dtypes.bfloat16)
    bandH = bandH.reshape(NT // TC, 16, -1).astype(ml_dtypes.bfloat16)
    return dict(rope_t=rope_t, flagkv=flagkv, scanflag=scanflag, bandM=bandM, bandH=bandH)


def const_table():
    j = np.arange(128, dtype=np.float32)[:, None]
    i = np.arange(128, dtype=np.float32)[None, :]
    c = np.zeros((128, 5 * 128 + 4), np.float32)
    c[:, 516:644] = np.eye(128, dtype=np.float32)
    c[:, 0:128] = np.maximum(i - j, 0)
    c[:, 128:256] = np.maximum(j - i, 0)
    c[:, 256:384] = (i >= j)
    c[:, 384:512] = (j > i)
    c[:, 512] = 127 - j[:, 0]
    c[:, 513] = j[:, 0]
    c[:, 514] = j[:, 0] + 1
    c[:, 515] = 128 - j[:, 0]
    return c


def shared_inputs(inp, depth):
    f = lambda a: np.ascontiguousarray(np.asarray(a, dtype=np.float32))
    vecs = np.zeros((128, 6, depth, 8), np.float32)
    for idx, name in enumerate(["norm_mix_pre", "norm_mix_post", "norm_mlp_pre", "norm_mlp_post", "pool_scale", "ret_gn"]):
        a = f(inp[name])[:depth]
        vecs[:, idx] = a.reshape(depth, 8, 128).transpose(2, 0, 1)
    dec = np.concatenate([f(inp["ret_decay_fwd"])[:depth].reshape(-1), f(inp["ret_decay_bwd"])[:depth].reshape(-1)])
    decays = np.ascontiguousarray(np.broadcast_to(dec[None, :], (128, dec.size)))
    return dict(w_in=f(inp["w_in"])[:depth], w_out=f(inp["w_out"])[:depth], w_mlp1=f(inp["w_mlp1"])[:depth],
                w_mlp2=f(inp["w_mlp2"])[:depth], pool_w=f(inp["pool_w"])[:depth], vecs=vecs, decays=decays,
                consts=const_table())


_CACHE = {}


def run_cores(core_tokens, core_seqlens, NT, depth, inp, n_cores=8):
    key = (NT, depth)
    if key not in _CACHE:
        _CACHE[key] = build(NT, depth)[0]
    nc = _CACHE[key]
    sh = shared_inputs(inp, depth)
    in_maps = []
    for ci in range(n_cores):
        xt = np.zeros((NT, D), np.float32)
        a = core_tokens[ci]
        xt[:a.shape[0]] = a
        m = dict(sh)
        m["x_in"] = xt
        m.update(host_tables(core_seqlens[ci], NT))
        in_maps.append(m)
    res = run_bass_kernel_spmd(nc, in_maps, core_ids=list(range(n_cores)))
    return [r["y_out"] for r in res.results]


def kernel(**inputs):
    xp = np.asarray(inputs["x_prompt"], dtype=np.float32)
    xs = np.asarray(inputs["x_sample"], dtype=np.float32)
    depth = 4
    NT = 16384
    toks, lens = [xp[0]], [[16384]]
    for ci in range(4):
        toks.append(xs[ci * 4:(ci + 1) * 4].reshape(-1, D))
        lens.append([4096] * 4)
    for ci in range(3):
        toks.append(np.zeros((0, D), np.float32))
        lens.append([])
    outs = run_cores(toks, lens, NT, depth, inputs)
    y_prompt = outs[0].reshape(1, 16384, D).astype(np.float32)
    y_sample = np.concatenate([outs[1 + ci].reshape(4, 4096, D) for ci in range(4)], axis=0).astype(np.float32)
    return (y_prompt, y_sample)
```

```python
import numpy as np
import ml_dtypes
from contextlib import ExitStack
import concourse.bass as bass
import concourse.mybir as mybir
from concourse.bass_utils import run_bass_kernel_spmd

F32 = mybir.dt.float32
BF16 = mybir.dt.bfloat16
ALU = mybir.AluOpType
AF = mybir.ActivationFunctionType

D = 1024
HEADS = 8
DK = 64
DV = 128
DIN = 6144
DFF = 4096
EPS = 1e-6
CH = 128
ROPE_BASE = 10000.0
POOL_WINDOWS = (2, 4, 8, 16)

ENGS = ["pe", "act", "dve", "pool", "sp"]
import os as _os
KSTOP = int(_os.environ.get("KSTOP", "4"))


class Buf:
    __slots__ = ("name", "w", "r")

    def __init__(self, name=""):
        self.name = name
        self.w = None
        self.r = {}


class Prog:
    def __init__(self, nc, n_dma_sems=40):
        self.nc = nc
        self.q = {e: [] for e in ENGS}
        self.cnt = {e: 0 for e in ENGS}
        self.waited = {e: {} for e in ENGS}
        self.n_dma = n_dma_sems
        self.dma_i = 0
        self.n_sw = 8
        self.sw_i = 0
        self.dma_cnt = [0] * n_dma_sems
        self.dma_last = [None] * n_dma_sems
        self.ninstr = 0

    def _need(self, eng, tok, waits):
        if tok is None:
            return
        key = (tok[0], tok[1])
        val = tok[2]
        if self.waited[eng].get(key, 0) >= val:
            return
        if tok[0] == "e" and tok[1] == eng and (val > self.cnt[eng] or eng == "pe" or _os.environ.get("NOSELF")):
            return
        self.waited[eng][key] = val
        waits.append((key, val))

    def _deps(self, eng, reads, writes, waits):
        for b in reads:
            self._need(eng, b.w, waits)
        for b in writes:
            self._need(eng, b.w, waits)
            for k, v in b.r.items():
                self._need(eng, (k[0], k[1], v), waits)

    def _mark(self, tok, reads, writes):
        k = (tok[0], tok[1])
        for b in reads:
            if b.r.get(k, 0) < tok[2]:
                b.r[k] = tok[2]
        for b in writes:
            b.w = tok
            b.r = {}

    def op(self, eng, fn, reads=(), writes=(), inc=True):
        waits = []
        self._deps(eng, reads, writes, waits)
        c = self.cnt[eng] + 1
        tok = ("e", eng, c)
        if inc:
            self.cnt[eng] = c
        self.q[eng].append((waits, fn, ("e", inc)))
        self._mark(tok, reads, writes)
        self.ninstr += 1
        return tok

    def dma(self, fn, reads=(), writes=(), eng="sp"):
        waits = []
        self._deps(eng, reads, writes, waits)
        if eng == "pool":
            i = self.n_dma - self.n_sw + self.sw_i
            self.sw_i = (self.sw_i + 1) % self.n_sw
        else:
            i = self.dma_i
            self.dma_i = (i + 1) % (self.n_dma - self.n_sw)
        self._need(eng, self.dma_last[i], waits)
        self.dma_cnt[i] += 16
        tok = ("d", i, self.dma_cnt[i])
        self.dma_last[i] = tok
        self.q[eng].append((waits, fn, ("d", i)))
        self._mark(tok, reads, writes)
        self.ninstr += 1
        return tok

    def barrier(self):
        for e in ENGS:
            waits = []
            for i in range(self.n_dma):
                self._need(e, self.dma_last[i], waits)
            for o in ENGS:
                if o != "sp" and o != e and self.cnt[o] > 0:
                    self._need(e, ("e", o, self.cnt[o]), waits)
            if e != "sp" and self.cnt[e] > 0:
                self._need(e, ("e", e, self.cnt[e]), waits)
            self.q[e].append((waits, None, None))

    def emit(self):
        nc = self.nc
        with ExitStack() as es:
            esem = {e: es.enter_context(nc.semaphore("s_" + e)) for e in ENGS if e != "sp"}
            dsem = [es.enter_context(nc.semaphore("d_%d" % i)) for i in range(self.n_dma)]
            block = es.enter_context(nc.Block())

            def semof(key):
                return esem[key[1]] if key[0] == "e" else dsem[key[1]]

            def run(engname):
                def body(engine):
                    for waits, fn, kind in self.q[engname]:
                        for key, val in waits:
                            engine.wait_ge(semof(key), val)
                        if fn is None:
                            continue
                        ins = fn(engine)
                        if kind[0] == "e":
                            if kind[1]:
                                ins.then_inc(esem[engname], 1)
                        else:
                            ins.then_inc(dsem[kind[1]], 16)
                return body

            block.tensor(run("pe"))
            block.scalar(run("act"))
            block.vector(run("dve"))
            block.gpsimd(run("pool"))
            block.sync(run("sp"))


class T:
    def __init__(self, t, nb=1):
        self.t = t
        self.b = [Buf() for _ in range(nb)]

    def __getitem__(self, k):
        return self.t[k]


def build(NT, depth, TA=512, TC=512, TM=256):
    NCH = NT // CH
    nc = bass.Bass("TRN2", target_bir_lowering=False)
    P = Prog(nc)

    def din(name, shape, dt=F32):
        return nc.dram_tensor(name, list(shape), dt, kind="ExternalInput").ap()

    def dscr(name, shape, dt):
        return nc.dram_tensor(name, list(shape), dt, kind="Internal").ap()

    x_in = din("x_in", [NT, D])
    y_out = nc.dram_tensor("y_out", [NT, D], F32, kind="ExternalOutput").ap()
    w_in = din("w_in", [depth, D, DIN])
    w_out = din("w_out", [depth, D, D])
    w_mlp1 = din("w_mlp1", [depth, D, DFF])
    w_mlp2 = din("w_mlp2", [depth, DFF, D])
    pool_w = din("pool_w", [depth, 4, 256, 256])
    vecs = din("vecs", [128, 6, depth, 8])
    decays = din("decays", [128, 2 * depth * 8])
    rope_t = din("rope_t", [NT, 64])
    flagkv = din("flagkv", [128, NCH])
    scanflag = din("scanflag", [128, NCH])
    bandM = din("bandM", [NT // TC, 128, (TC // CH) * 4 * 128], BF16)
    bandH = din("bandH", [NT // TC, 16, (TC // CH) * 4 * 128], BF16)
    consts = din("consts", [128, 5 * 128 + 4])

    xT_d = dscr("xT_d", [D, NT], F32)
    xm_d = dscr("xm_d", [D, NT], F32)
    qT_d = dscr("qT_d", [512, NT], BF16)
    kT_d = dscr("kT_d", [1024, NT], BF16)
    qsT_d = dscr("qsT_d", [1024, NT], BF16)
    gate_d = dscr("gate_d", [1024, NT], BF16)
    sgp_d = dscr("sgp_d", [1024, NT], BF16)
    v_d = dscr("v_d", [NT, 1024], BF16)
    u_d = dscr("u_d", [NT + 16, 1024], BF16)
    kv_d = dscr("kv_d", [NCH, 128, 1024], F32)
    st_d = dscr("st_d", [NCH, 128, 1024], BF16)

    top = ExitStack()

    _uid = [0]

    def sb(es, name, shape, dt, nb=1):
        _uid[0] += 1
        return T(es.enter_context(nc.sbuf_tensor("%s_%d" % (name, _uid[0]), list(shape), dt)), nb)

    with top:
        psb = [T(top.enter_context(nc.psum_tensor("ps%d" % i, [128, 512], F32))) for i in range(8)]
        ps_i = [0]
        NBANK = 7 if _os.environ.get("K4") else 8

        def bank():
            b = psb[ps_i[0]]
            ps_i[0] = (ps_i[0] + 1) % NBANK
            return b

        ident_b = sb(top, "ident_b", [128, 128], BF16)
        ident_f = sb(top, "ident_f", [128, 128], F32)
        ones_b = sb(top, "ones_b", [128, 128], BF16)
        nhalf = sb(top, "nhalf", [128, 512], F32)
        cst = sb(top, "cst", [128, 5 * 128 + 4], F32)
        vec = sb(top, "vec", [128, 6, depth, 8], F32)
        dcy = sb(top, "dcy", [128, 2 * depth * 8], F32)
        lg = sb(top, "lg", [128, 2 * depth * 8], F32)
        fkv = sb(top, "fkv", [128, NCH], F32)
        sfl = sb(top, "sfl", [128, NCH], F32)
        dmask = sb(top, "dmask", [128, 8, 128], F32)
        dtmp = sb(top, "dtmp", [128, 128], F32)
        zx = sb(top, "zx", [128, 4, 8], F32)
        dec = sb(top, "dec", [128, 8], F32)
        atab = sb(top, "atab", [128, NCH, 8], F32)
        tmpes = ExitStack()
        zero_b = sb(tmpes, "zero_b", [8, 1024], BF16)

        P.op("pool", lambda e: e.memset(ident_f[:], 0.0), writes=ident_f.b)
        P.dma(lambda e: e.dma_start(out=cst[:], in_=consts), writes=cst.b)
        P.dma(lambda e: e.dma_start(out=vec[:], in_=vecs), writes=vec.b)
        P.dma(lambda e: e.dma_start(out=dcy[:], in_=decays), writes=dcy.b)
        P.dma(lambda e: e.dma_start(out=fkv[:], in_=flagkv), writes=fkv.b)
        P.dma(lambda e: e.dma_start(out=sfl[:], in_=scanflag), writes=sfl.b)
        P.op("dve", lambda e: e.tensor_copy(out=ident_f[:], in_=cst[:, 516:644]), reads=cst.b, writes=ident_f.b)
        P.op("dve", lambda e: e.tensor_copy(out=ident_b[:], in_=ident_f[:]), reads=ident_f.b, writes=ident_b.b)
        P.op("pool", lambda e: e.memset(ones_b[:], 1.0), writes=ones_b.b)
        P.op("pool", lambda e: e.memset(nhalf[:], -0.5), writes=nhalf.b)
        P.op("pool", lambda e: e.memset(zero_b[:], 0.0), writes=zero_b.b)
        P.dma(lambda e: e.dma_start(out=u_d[0:8, :], in_=zero_b[:]), reads=zero_b.b)
        P.dma(lambda e: e.dma_start(out=u_d[NT + 8:NT + 16, :], in_=zero_b[:]), reads=zero_b.b)
        P.op("act", lambda e: e.activation(out=lg[:], in_=dcy[:], func=AF.Exp, scale=-1.0), reads=dcy.b, writes=lg.b)
        P.op("act", lambda e: e.activation(out=lg[:], in_=lg[:], func=AF.Ln, bias=1.0, scale=1.0), reads=lg.b, writes=lg.b)
        P.op("dve", lambda e: e.tensor_scalar(out=lg[:], in0=lg[:], scalar1=-1.0, scalar2=None, op0=ALU.mult),
             reads=lg.b, writes=lg.b)
        P.barrier()
        tmpes.close()

        def rms_stats(es_tiles, xT, width, sqr, rstd, src_is_psum=None, msb_bank=None):
            msb = msb_bank if msb_bank is not None else bank()
            K5 = _os.environ.get("K5", "")
            for k in range(8):
                if K5 == "d":
                    P.op("dve", lambda e, k=k: e.tensor_tensor(out=sqr[:, k, 0:width], in0=xT[:, k, 0:width], in1=xT[:, k, 0:width], op=ALU.mult),
                         reads=[xT.b[k]], writes=[sqr.b[k]])
                else:
                    P.op("act", lambda e, k=k: e.activation(out=sqr[:, k, 0:width], in_=xT[:, k, 0:width], func=AF.Square),
                         reads=[xT.b[k]], writes=[sqr.b[k]])
            for k in range(8 if K5 != "s" else 0):
                P.op("pe", lambda e, k=k: e.matmul(msb[:, 0:width], lhsT=ones_b[:], rhs=sqr[:, k, 0:width],
                                                   start=(k == 0), stop=(k == 7)),
                     reads=[sqr.b[k]] + ones_b.b, writes=msb.b, inc=(k == 7))
            P.op("act", lambda e: e.activation(out=rstd[:, 0:width], in_=msb[:, 0:width], func=AF.Ln, scale=1.0 / D, bias=EPS),
                 reads=msb.b, writes=rstd.b)
            P.op("act", lambda e: e.activation(out=rstd[:, 0:width], in_=rstd[:, 0:width], func=AF.Exp, scale=-0.5),
                 reads=rstd.b, writes=rstd.b)

        def do_layer(l):
            lgf = lambda h: lg[:, l * 8 + h:l * 8 + h + 1]
            lgb = lambda h: lg[:, depth * 8 + l * 8 + h:depth * 8 + l * 8 + h + 1]
            for h in range(HEADS):
                P.op("act", lambda e, h=h: e.activation(out=dmask[:, h, :], in_=cst[:, 0:128], func=AF.Exp, scale=lgf(h)),
                     reads=cst.b + lg.b, writes=dmask.b)
                P.op("dve", lambda e, h=h: e.tensor_tensor(out=dmask[:, h, :], in0=dmask[:, h, :], in1=cst[:, 256:384],
                                                           op=ALU.mult), reads=dmask.b + cst.b, writes=dmask.b)
                P.op("act", lambda e, h=h: e.activation(out=dtmp[:], in_=cst[:, 128:256], func=AF.Exp, scale=lgb(h)),
                     reads=cst.b + lg.b, writes=dtmp.b)
                P.op("dve", lambda e, h=h: e.tensor_tensor(out=dtmp[:], in0=dtmp[:], in1=cst[:, 384:512], op=ALU.mult),
                     reads=dtmp.b + cst.b, writes=dtmp.b)
                P.op("dve", lambda e, h=h: e.tensor_tensor(out=dmask[:, h, :], in0=dmask[:, h, :], in1=dtmp[:], op=ALU.add),
                     reads=dmask.b + dtmp.b, writes=dmask.b)
            lo = l * 8
            hb = depth * 8 + l * 8
            for idx, (col, base) in enumerate([(512, lo), (513, hb), (514, lo), (515, hb)]):
                P.op("dve", lambda e, idx=idx, col=col, base=base: e.tensor_scalar(
                    out=zx[:, idx, :], in0=lg[:, base:base + 8], scalar1=cst[:, col:col + 1], scalar2=None, op0=ALU.mult),
                    reads=lg.b + cst.b, writes=zx.b)
            P.op("act", lambda e: e.activation(out=zx[:], in_=zx[:], func=AF.Exp), reads=zx.b, writes=zx.b)
            P.op("act", lambda e: e.activation(out=dec[0:64, :], in_=lg[0:64, lo:lo + 8], func=AF.Exp, scale=128.0),
                 reads=lg.b, writes=dec.b)
            P.op("act", lambda e: e.activation(out=dec[64:128, :], in_=lg[64:128, hb:hb + 8], func=AF.Exp, scale=128.0),
                 reads=lg.b + dec.b, writes=dec.b)
            P.op("dve", lambda e: e.tensor_tensor(out=atab[:], in0=dec[:].unsqueeze(1).to_broadcast([128, NCH, 8]),
                                                  in1=sfl[:].unsqueeze(2).to_broadcast([128, NCH, 8]), op=ALU.mult),
                 reads=dec.b + sfl.b, writes=atab.b)
            P.barrier()

            def _pass0():
                with ExitStack() as es:
                    NTA = NT // TA
                    CPT = TA // CH
                    wA = sb(es, "wA", [128, 8, DIN], BF16, nb=8)
                    for k in range(8 if not _os.environ.get("KNOW") else 0):
                        P.dma(lambda e, k=k: e.dma_start(out=wA[:, k, :], in_=w_in[l, k * 128:(k + 1) * 128, :]),
                              writes=[wA.b[k]], eng="pool")
                    xT = sb(es, "a_xT", [128, 8, TA], F32, nb=8)
                    xtok = [sb(es, "a_xtok%d" % i, [128, D], F32) for i in range(1)] if l == 0 else None
                    hT = [sb(es, "a_hT%d" % i, [128, 8, TA], BF16, nb=8) for i in range(2)]
                    rstd = sb(es, "a_rstd", [128, TA], F32)
                    sig = [sb(es, "a_sig%d" % i, [128, TA], F32) for i in range(3)]
                    gst = [sb(es, "a_gst%d" % i, [128, TA], BF16) for i in range(2)]
                    pst = [sb(es, "a_pst%d" % i, [128, TA], BF16) for i in range(2)]
                    q32 = sb(es, "a_q32", [128, 512], F32)
                    k32 = sb(es, "a_k32", [128, 512], F32)
                    rt1 = sb(es, "a_rt1", [128, 8, 32], F32)
                    rt2 = sb(es, "a_rt2", [128, 8, 32], F32)
                    qbf = sb(es, "a_qbf", [128, 8, 64], BF16)
                    kbf = sb(es, "a_kbf", [128, 8, 64], BF16)
                    qs = sb(es, "a_qs", [128, 8, 2, 64], BF16)
                    kz = sb(es, "a_kz", [128, 8, 2, 64], BF16)
                    vbf = [sb(es, "a_vbf%d" % i, [128, 1024], BF16) for i in range(1)]
                    ubf = [sb(es, "a_ubf%d" % i, [128, 1024], BF16) for i in range(1)]
                    kvs = [sb(es, "a_kvs%d" % i, [128, 1024], F32) for i in range(1)]
                    rope = [sb(es, "a_rope%d" % i, [128, 64], F32) for i in range(2)]
                    qT = sb(es, "a_qT", [128, 4, TA], BF16, nb=CPT)
                    kT = sb(es, "a_kT", [128, 8, TA], BF16, nb=CPT)
                    kpad = sb(es, "a_kpad", [128, 8, 128], BF16)
                    P.op("pool", lambda e: e.memset(kpad[:], 0.0), writes=kpad.b)
                    qsT = sb(es, "a_qsT", [128, 8, TA], BF16, nb=CPT)
                    g_pre = vec[:, 0, l, :]

                    def load_x(t):
                        c0 = t * TA
                        if l > 0:
                            for k in range(8):
                                P.dma(lambda e, k=k: e.dma_start(out=xT[:, k, :], in_=xT_d[k * 128:(k + 1) * 128, c0:c0 + TA]),
                                      writes=[xT.b[k]])
                        else:
                            for c in range(CPT if not _os.environ.get("K6") else int(_os.environ.get("K6"))):
                                xt = xtok[0]
                                P.dma(lambda e, c=c, xt=xt: e.dma_start(out=xt[:], in_=x_in[c0 + c * CH:c0 + (c + 1) * CH, :]),
                                      writes=xt.b)
                                for half in range(2):
                                    pb = bank()
                                    for kk in range(4 if _os.environ.get("K2", "") != "c" else 0):
                                        k = half * 4 + kk
                                        P.op("pe", lambda e, k=k, kk=kk, pb=pb, xt=xt: e.transpose(
                                            pb[:, kk * 128:(kk + 1) * 128], xt[:, k * 128:(k + 1) * 128], ident_f[:]),
                                            reads=xt.b + ident_f.b, writes=pb.b, inc=(kk == 3))
                                    for kk in range(4 if _os.environ.get("K2", "") not in ("b", "c") else 0):
                                        k = half * 4 + kk
                                        K7 = _os.environ.get("K7", "")
                                        useact = (half == 1)
                                        P.op("act" if useact else "dve",
                                             (lambda e, k=k, kk=kk, pb=pb, c=c: e.copy(out=xT[:, k, c * CH:(c + 1) * CH],
                                                                                       in_=pb[:, kk * 128:(kk + 1) * 128]))
                                             if useact else
                                             (lambda e, k=k, kk=kk, pb=pb, c=c: e.tensor_copy(out=xT[:, k, c * CH:(c + 1) * CH],
                                                                                              in_=pb[:, kk * 128:(kk + 1) * 128])),
                                             reads=pb.b, writes=[xT.b[k]])
                            for k in range(8 if _os.environ.get("K2", "") not in ("a", "b", "c") else 0):
                                P.dma(lambda e, k=k: e.dma_start(out=xT_d[k * 128:(k + 1) * 128, c0:c0 + TA], in_=xT[:, k, :]),
                                      reads=[xT.b[k]])

                    def norm_x(t):
                        h = hT[t % 2]
                        rms_stats(None, xT, TA, h, rstd)
                        for k in range(8 if _os.environ.get("K3", "z") >= "d" else 0):
                            eng = "dve"
                            P.op(eng, lambda e, k=k, h=h: e.scalar_tensor_tensor(
                                out=h[:, k, :], in0=xT[:, k, :], scalar=g_pre[:, k:k + 1], in1=rstd[:], op0=ALU.mult, op1=ALU.mult),
                                reads=[xT.b[k]] + rstd.b + vec.b, writes=[h.b[k]])

                    def fm_group(t):
                        h = hT[t % 2]
                        c0 = t * TA
                        for j in range(8):
                            pg, pr, pp = bank(), bank(), bank()
                            for (pb, col) in ((pg, 2048 + j * 128), (pr, 4096 + j * 128), (pp, 5120 + j * 128)):
                                for k in range(8):
                                    P.op("pe", lambda e, pb=pb, col=col, k=k: e.matmul(
                                        pb[:, 0:TA], lhsT=wA[:, k, col:col + 128], rhs=h[:, k, :], start=(k == 0), stop=(k == 7)),
                                        reads=[wA.b[k], h.b[k]], writes=pb.b, inc=(k == 7))
                            s1, s2 = sig[j % 2], sig[2]
                            go, po = gst[j % 2], pst[j % 2]
                            P.op("act", lambda e, pg=pg, s1=s1: e.activation(out=s1[:], in_=pg[:, 0:TA], func=AF.Sigmoid),
                                 reads=pg.b, writes=s1.b)
                            P.op("act", lambda e, pr=pr, s2=s2: e.activation(out=s2[:], in_=pr[:, 0:TA], func=AF.Sigmoid),
                                 reads=pr.b, writes=s2.b)
                            P.op("act", lambda e, pp=pp, po=po: e.activation(out=po[:], in_=pp[:, 0:TA], func=AF.Sigmoid),
                                 reads=pp.b, writes=po.b)
                            P.op("dve", lambda e, pg=pg, s1=s1, j=j: e.scalar_tensor_tensor(
                                out=s1[:], in0=pg[:, 0:TA], scalar=vec[:, 5, l, j:j + 1], in1=s1[:], op0=ALU.mult, op1=ALU.mult),
                                reads=pg.b + s1.b + vec.b, writes=s1.b)
                            P.op("dve", lambda e, s1=s1, s2=s2, go=go: e.tensor_tensor(out=go[:], in0=s1[:], in1=s2[:], op=ALU.mult),
                                 reads=s1.b + s2.b, writes=go.b)
                            P.dma(lambda e, go=go, j=j: e.dma_start(out=gate_d[j * 128:(j + 1) * 128, c0:c0 + TA], in_=go[:]),
                                  reads=go.b)
                            P.dma(lambda e, po=po, j=j: e.dma_start(out=sgp_d[j * 128:(j + 1) * 128, c0:c0 + TA], in_=po[:]),
                                  reads=po.b)

                    def tm_chunk(t, c):
                        h = hT[t % 2]
                        gc = t * CPT + c
                        tok0 = gc * CH
                        cs = slice(c * CH, (c + 1) * CH)
                        rp = rope[gc % 2]
                        P.dma(lambda e, rp=rp: e.dma_start(out=rp[:], in_=rope_t[tok0:tok0 + CH, :]), writes=rp.b)
                        banks = []
                        for col in (0, 512, 1024, 1536, 3072, 3584):
                            pb = bank()
                            banks.append(pb)
                            for k in range(8):
                                P.op("pe", lambda e, pb=pb, col=col, k=k: e.matmul(
                                    pb[:], lhsT=h[:, k, cs], rhs=wA[:, k, col:col + 512], start=(k == 0), stop=(k == 7)),
                                    reads=[wA.b[k], h.b[k]], writes=pb.b, inc=(k == 7))
                        pq, pk, pv0, pv1, pu0, pu1 = banks
                        v_, u_, kv_ = vbf[0], ubf[0], kvs[0]
                        P.op("act", lambda e: e.copy(out=q32[:], in_=pq[:]), reads=pq.b, writes=q32.b)
                        P.op("act", lambda e: e.mul(out=k32[:], in_=pk[:], mul=DK ** -0.5), reads=pk.b, writes=k32.b)
                        P.op("act", lambda e: e.copy(out=v_[:, 0:512], in_=pv0[:]), reads=pv0.b, writes=v_.b)
                        P.op("act", lambda e: e.copy(out=v_[:, 512:1024], in_=pv1[:]), reads=pv1.b + v_.b, writes=v_.b)
                        P.op("act", lambda e: e.copy(out=u_[:, 0:512], in_=pu0[:]), reads=pu0.b, writes=u_.b)
                        P.op("act", lambda e: e.copy(out=u_[:, 512:1024], in_=pu1[:]), reads=pu1.b + u_.b, writes=u_.b)
                        P.dma(lambda e: e.dma_start(out=v_d[tok0:tok0 + CH, :], in_=v_[:]), reads=v_.b)
                        P.dma(lambda e: e.dma_start(out=u_d[8 + tok0:8 + tok0 + CH, :], in_=u_[:]), reads=u_.b)
                        cosb = rp[:, 0:32].unsqueeze(1).to_broadcast([128, 8, 32])
                        sinb = rp[:, 32:64].unsqueeze(1).to_broadcast([128, 8, 32])
                        for (src, dst, eng) in ((q32, qbf, "dve"), (k32, kbf, "dve")):
                            s4 = src[:].rearrange("p (h two d) -> p h two d", h=8, two=2)
                            x1, x2 = s4[:, :, 0, :], s4[:, :, 1, :]
                            rd = src.b + rp.b
                            P.op(eng, lambda e, x1=x1: e.tensor_tensor(out=rt1[:], in0=x1, in1=cosb, op=ALU.mult),
                                 reads=rd, writes=rt1.b)
                            P.op(eng, lambda e, x2=x2: e.tensor_tensor(out=rt2[:], in0=x2, in1=sinb, op=ALU.mult),
                                 reads=rd, writes=rt2.b)
                            P.op(eng, lambda e, dst=dst: e.tensor_tensor(out=dst[:, :, 0:32], in0=rt1[:], in1=rt2[:], op=ALU.subtract),
                                 reads=rt1.b + rt2.b, writes=dst.b)
                            P.op(eng, lambda e, x1=x1: e.tensor_tensor(out=rt1[:], in0=x1, in1=sinb, op=ALU.mult),
                                 reads=rd, writes=rt1.b)
                            P.op(eng, lambda e, x2=x2: e.tensor_tensor(out=rt2[:], in0=x2, in1=cosb, op=ALU.mult),
                                 reads=rd, writes=rt2.b)
                            P.op(eng, lambda e, dst=dst: e.tensor_tensor(out=dst[:, :, 32:64], in0=rt1[:], in1=rt2[:], op=ALU.add),
                                 reads=rt1.b + rt2.b + dst.b, writes=dst.b)
                        for (src, dst, zi, eng) in ((qbf, qs, 2, "dve"), (kbf, kz, 0, "dve")):
                            for d in range(2):
                                P.op(eng, lambda e, src=src, dst=dst, zi=zi, d=d: e.tensor_tensor(
                                    out=dst[:, :, d, :], in0=src[:], in1=zx[:, zi + d, :].unsqueeze(2).to_broadcast([128, 8, 64]),
                                    op=ALU.mult), reads=src.b + zx.b + dst.b, writes=dst.b)
                        pbq = bank()
                        pq16 = pbq[:].bitcast(BF16)
                        for pr_ in range(4):
                            P.op("pe", lambda e, pr_=pr_: e.transpose(pq16[:, pr_ * 128:(pr_ + 1) * 128],
                                                                       qbf[:].rearrange("p h d -> p (h d)")[:, pr_ * 128:(pr_ + 1) * 128], ident_b[:]),
                                 reads=qbf.b + ident_b.b, writes=pbq.b, inc=(pr_ == 3))
                        P.op("act", lambda e: e.copy(out=qT[:, :, cs], in_=pq16[:, 0:512].rearrange("p (a b) -> p a b", a=4)),
                             reads=pbq.b, writes=[qT.b[c]])
                        kp4 = kpad[:].rearrange("p (pr two) x -> p pr two x", two=2)
                        kb4 = kbf[:].rearrange("p (pr two) d -> p pr two d", two=2)
                        P.op("dve", lambda e: e.tensor_copy(out=kp4[:, :, 0, 0:64], in_=kb4[:, :, 0, :]),
                             reads=kbf.b + kpad.b, writes=kpad.b)
                        P.op("dve", lambda e: e.tensor_copy(out=kp4[:, :, 1, 64:128], in_=kb4[:, :, 1, :]),
                             reads=kbf.b + kpad.b, writes=kpad.b)
                        pbk = bank()
                        pk16 = pbk[:].bitcast(BF16)
                        for hh in range(8):
                            P.op("pe", lambda e, hh=hh: e.transpose(pk16[:, hh * 128:(hh + 1) * 128], kpad[:, hh, :], ident_b[:]),
                                 reads=kpad.b + ident_b.b, writes=pbk.b, inc=(hh == 7))
                        P.op("act", lambda e: e.copy(out=kT[:, :, cs], in_=pk16[:].rearrange("p (a b) -> p a b", a=8)),
                             reads=pbk.b, writes=[kT.b[c]])
                        pbs = bank()
                        ps16 = pbs[:].bitcast(BF16)
                        for hh in range(8):
                            P.op("pe", lambda e, hh=hh: e.transpose(ps16[:, hh * 128:(hh + 1) * 128], qs[:].rearrange("p h t d -> p (h t d)")[:, hh * 128:(hh + 1) * 128], ident_b[:]),
                                 reads=qs.b + ident_b.b, writes=pbs.b, inc=(hh == 7))
                        P.op("act", lambda e: e.copy(out=qsT[:, :, cs], in_=ps16[:].rearrange("p (a b) -> p a b", a=8)),
                             reads=pbs.b, writes=[qsT.b[c]])
                        for half in range(2):
                            pb = bank()
                            for hh in range(4):
                                hd = half * 4 + hh
                                P.op("pe", lambda e, hd=hd, hh=hh, pb=pb: e.matmul(
                                    pb[:, hh * 128:(hh + 1) * 128], lhsT=kz[:].rearrange("p h t d -> p (h t d)")[:, hd * 128:(hd + 1) * 128], rhs=v_[:, hd * 128:(hd + 1) * 128],
                                    start=True, stop=True), reads=kz.b + v_.b, writes=pb.b, inc=(hh == 3))
                            P.op("dve", lambda e, pb=pb, half=half: e.tensor_scalar(
                                out=kv_[:, half * 512:(half + 1) * 512], in0=pb[:], scalar1=fkv[:, gc:gc + 1], scalar2=None,
                                op0=ALU.mult), reads=pb.b + fkv.b + kv_.b, writes=kv_.b)
                        P.dma(lambda e: e.dma_start(out=kv_d[gc], in_=kv_[:]), reads=kv_.b)

                    def store_T(t):
                        c0 = t * TA
                        for pr_ in range(4):
                            P.dma(lambda e, pr_=pr_: e.dma_start(out=qT_d[pr_ * 128:(pr_ + 1) * 128, c0:c0 + TA], in_=qT[:, pr_, :]),
                                  reads=qT.b)
                        for hh in range(8):
                            P.dma(lambda e, hh=hh: e.dma_start(out=qsT_d[hh * 128:(hh + 1) * 128, c0:c0 + TA], in_=qsT[:, hh, :]),
                                  reads=qsT.b)
                            P.dma(lambda e, hh=hh: e.dma_start(out=kT_d[hh * 128:(hh + 1) * 128, c0:c0 + TA], in_=kT[:, hh, :]),
                                  reads=kT.b)

                    KSUB = int(_os.environ.get("KSUB", "99"))
                    if KSUB >= 2:
                        load_x(0)
                    if KSUB >= 3:
                        norm_x(0)
                    if KSUB == 4:
                        fm_group(0)
                    if KSUB == 5:
                        tm_chunk(0, 0)
                    for t in range(NTA if KSUB >= 99 else 0):
                        fm_group(t)
                        tm_chunk(t, 0)
                        if t + 1 < NTA:
                            load_x(t + 1)
                            norm_x(t + 1)
                        for c in range(1, CPT):
                            tm_chunk(t, c)
                        store_T(t)
                    P.barrier()

            if KSTOP >= 1:
                _pass0()
            def _pass1():
                with ExitStack() as es:
                    S = sb(es, "b_S", [128, 1024], F32, nb=8)
                    kvl = [sb(es, "b_kv%d" % i, [128, 1024], F32) for i in range(4)]
                    so = [sb(es, "b_so%d" % i, [128, 1024], BF16) for i in range(3)]
                    P.op("pool", lambda e: e.memset(S[:], 0.0), writes=S.b)
                    P.op("pool", lambda e: e.memset(so[2][:], 0.0), writes=so[2].b)
                    P.dma(lambda e: e.dma_start(out=st_d[0, 0:64, :], in_=so[2][0:64, :]), reads=so[2].b)
                    P.dma(lambda e: e.dma_start(out=st_d[NCH - 1, 64:128, :], in_=so[2][64:128, :]), reads=so[2].b)

                    def ld(s):
                        kk = kvl[s % 4]
                        P.dma(lambda e: e.dma_start(out=kk[0:64, :], in_=kv_d[s, 0:64, :]), writes=kk.b)
                        P.dma(lambda e: e.dma_start(out=kk[64:128, :], in_=kv_d[NCH - 1 - s, 64:128, :]), writes=kk.b)

                    for s in range(min(3, NCH - 1)):
                        ld(s)
                    for s in range(NCH - 1):
                        if s + 3 < NCH - 1:
                            ld(s + 3)
                        kk = kvl[s % 4]
                        o = so[s % 2]
                        for hh in range(8):
                            P.op("dve", lambda e, hh=hh, kk=kk, s=s: e.scalar_tensor_tensor(
                                out=S[:, hh * 128:(hh + 1) * 128], in0=S[:, hh * 128:(hh + 1) * 128], scalar=atab[:, s, hh:hh + 1],
                                in1=kk[:, hh * 128:(hh + 1) * 128], op0=ALU.mult, op1=ALU.add),
                                reads=[S.b[hh]] + kk.b + atab.b, writes=[S.b[hh]])
                        P.op("act", lambda e, o=o: e.copy(out=o[:], in_=S[:]), reads=S.b, writes=o.b)
                        P.dma(lambda e, o=o, s=s: e.dma_start(out=st_d[s + 1, 0:64, :], in_=o[0:64, :]), reads=o.b)
                        P.dma(lambda e, o=o, s=s: e.dma_start(out=st_d[NCH - 2 - s, 64:128, :], in_=o[64:128, :]), reads=o.b)
                    P.barrier()

            if KSTOP >= 2:
                _pass1()
            def _pass2():
                with ExitStack() as es:
                    NTC = NT // TC
                    CPT = TC // CH
                    wO = sb(es, "wO", [128, 8, D], BF16, nb=8)
                    wP = sb(es, "wP", [128, 4, 2, 256], BF16, nb=8)
                    for k in range(8):
                        P.dma(lambda e, k=k: e.dma_start(out=wO[:, k, :], in_=w_out[l, k * 128:(k + 1) * 128, :]),
                              writes=[wO.b[k]], eng="pool")
                    for g in range(4):
                        for cc in range(2):
                            P.dma(lambda e, g=g, cc=cc: e.dma_start(out=wP[:, g, cc, :], in_=pool_w[l, g, cc * 128:(cc + 1) * 128, :]),
                                  writes=[wP.b[2 * g + cc]], eng="pool")
                    NBUF = 2
                    qT = [sb(es, "c_qT%d" % i, [128, 4, TC], BF16) for i in range(NBUF)]
                    kT = [sb(es, "c_kT%d" % i, [128, 8, TC], BF16) for i in range(NBUF)]
                    qsT = [sb(es, "c_qsT%d" % i, [128, 8, TC], BF16) for i in range(NBUF)]
                    gat = [sb(es, "c_gat%d" % i, [128, 8, TC], BF16) for i in range(NBUF)]
                    sgp = [sb(es, "c_sgp%d" % i, [128, 8, TC], BF16) for i in range(NBUF)]
                    xT = [sb(es, "c_xT%d" % i, [128, 8, TC], F32, nb=8) for i in range(1)]
                    bM = [sb(es, "c_bM%d" % i, [128, CPT * 4 * 128], BF16) for i in range(NBUF)]
                    bH = [sb(es, "c_bH%d" % i, [128, CPT * 4 * 128], BF16) for i in range(NBUF)]
                    uh = [sb(es, "c_uh%d" % i, [128, 1024], BF16) for i in range(2)]
                    for tz in bH + uh:
                        P.op("pool", lambda e, tz=tz: e.memset(tz[:], 0.0), writes=tz.b)
                    vv = [sb(es, "c_v%d" % i, [128, 1024], BF16) for i in range(2)]
                    st = [sb(es, "c_st%d" % i, [128, 1024], BF16) for i in range(2)]
                    uu = [sb(es, "c_u%d" % i, [128, 1024], BF16) for i in range(2)]
                    PT = [sb(es, "c_PT%d" % i, [128, 8, 128], BF16) for i in range(2)]
                    on = [sb(es, "c_on%d" % i, [128, 8, 128], BF16) for i in range(2)]
                    stats2 = [sb(es, "c_stats%d" % i, [128, 8, 6], F32) for i in range(2)]
                    mv2 = [sb(es, "c_mv%d" % i, [128, 8, 2], F32) for i in range(2)]
                    rs2 = [sb(es, "c_rs%d" % i, [128, 8], F32) for i in range(2)]
                    nmr2 = [sb(es, "c_nmr%d" % i, [128, 8], F32) for i in range(2)]
                    mT = sb(es, "c_mT", [128, 8, TC], BF16, nb=8)
                    pT = sb(es, "c_pT", [128, 8, TC], BF16, nb=8)
                    tmp = [sb(es, "c_tmp%d" % i, [128, TC], BF16) for i in range(2)]
                    yy = sb(es, "c_y", [128, 8, TC], F32, nb=8)
                    rstd = sb(es, "c_rstd", [128, TC], F32)
                    g_post = vec[:, 1, l, :]
                    pscale = vec[:, 4, l, :]

                    def load_tile(t, parts=(0, 1, 2, 3)):
                        i = t % NBUF
                        c0 = t * TC
                        if 0 in parts:
                            for pr_ in range(4):
                                P.dma(lambda e, pr_=pr_: e.dma_start(out=qT[i][:, pr_, :], in_=qT_d[pr_ * 128:(pr_ + 1) * 128, c0:c0 + TC]),
                                      writes=qT[i].b)
                            P.dma(lambda e: e.dma_start(out=bM[i][:], in_=bandM[t]), writes=bM[i].b)
                            P.dma(lambda e: e.dma_start(out=bH[i][0:16, :], in_=bandH[t]), writes=bH[i].b)
                        for hh in range(8):
                            if 2 in parts:
                                P.dma(lambda e, hh=hh: e.dma_start(out=qsT[i][:, hh, :], in_=qsT_d[hh * 128:(hh + 1) * 128, c0:c0 + TC]),
                                      writes=qsT[i].b)
                            if 1 in parts:
                                P.dma(lambda e, hh=hh: e.dma_start(out=kT[i][:, hh, :], in_=kT_d[hh * 128:(hh + 1) * 128, c0:c0 + TC]),
                                      writes=kT[i].b)
                            if 3 in parts:
                                P.dma(lambda e, hh=hh: e.dma_start(out=gat[i][:, hh, :], in_=gate_d[hh * 128:(hh + 1) * 128, c0:c0 + TC]),
                                      writes=gat[i].b)
                                P.dma(lambda e, hh=hh: e.dma_start(out=sgp[i][:, hh, :], in_=sgp_d[hh * 128:(hh + 1) * 128, c0:c0 + TC]),
                                      writes=sgp[i].b)

                    KC = int(_os.environ.get("KC", "99"))

                    def chunk(t, c):
                        i = t % NBUF
                        gc = t * CPT + c
                        tok0 = gc * CH
                        cs = slice(c * CH, (c + 1) * CH)
                        v_, s_, u_ = vv[gc % 2], st[gc % 2], uu[gc % 2]
                        pt_, on_ = PT[gc % 2], on[gc % 2]
                        stats, mv, rs, nmr = stats2[gc % 2], mv2[gc % 2], rs2[gc % 2], nmr2[gc % 2]
                        P.dma(lambda e: e.dma_start(out=v_[:], in_=v_d[tok0:tok0 + CH, :]), writes=v_.b)
                        P.dma(lambda e: e.dma_start(out=s_[:], in_=st_d[gc]), writes=s_.b)
                        P.dma(lambda e: e.dma_start(out=u_[:], in_=u_d[8 + tok0:8 + tok0 + CH, :]), writes=u_.b)
                        uh_ = uh[gc % 2]
                        P.dma(lambda e: e.dma_start(out=uh_[0:8, :], in_=u_d[tok0:tok0 + 8, :]), writes=uh_.b)
                        P.dma(lambda e: e.dma_start(out=uh_[8:16, :], in_=u_d[tok0 + CH + 8:tok0 + CH + 16, :]), writes=uh_.b)
                        for half in range(2):
                            pb = psb[half]
                            for hh in range(4):
                                hd = half * 4 + hh
                                pr_, base = hd // 2, (hd % 2) * 64
                                P.op("pe", lambda e, pb=pb, hh=hh, pr_=pr_, hd=hd: e.matmul(
                                    pb[:, hh * 128:(hh + 1) * 128], lhsT=kT[i][:, hd, cs], rhs=qT[i][:, pr_, cs],
                                    start=True, stop=True), reads=kT[i].b + qT[i].b, writes=pb.b, inc=(hh == 3))
                            if _os.environ.get("KD") == "nomask":
                                continue
                            P.op("dve", lambda e, pb=pb, half=half: e.tensor_tensor(
                                out=pt_[:, half * 4:(half + 1) * 4, :], in0=pb[:].rearrange("p (a b) -> p a b", a=4),
                                in1=dmask[:, half * 4:(half + 1) * 4, :], op=ALU.mult),
                                reads=pb.b + dmask.b + pt_.b, writes=pt_.b)
                        obanks = []
                        for half in range(2):
                            pb = psb[2 + 2 * (gc % 2) + half]
                            obanks.append(pb)
                            for hh in range(4):
                                hd = half * 4 + hh
                                P.op("pe", lambda e, pb=pb, hh=hh, hd=hd: e.matmul(
                                    pb[:, hh * 128:(hh + 1) * 128], lhsT=pt_[:, hd, :], rhs=v_[:, hd * 128:(hd + 1) * 128],
                                    start=True, stop=False), reads=pt_.b + v_.b, writes=pb.b, inc=False)
                                P.op("pe", lambda e, pb=pb, hh=hh, hd=hd: e.matmul(
                                    pb[:, hh * 128:(hh + 1) * 128], lhsT=qsT[i][:, hd, cs], rhs=s_[:, hd * 128:(hd + 1) * 128],
                                    start=False, stop=True), reads=qsT[i].b + s_.b, writes=pb.b, inc=(hh == 3))
                        yield
                        for half in range(2):
                            pb = obanks[half]
                            for hh in range(4):
                                hd = half * 4 + hh
                                P.op("dve", lambda e, pb=pb, hh=hh, hd=hd: e.bn_stats(out=stats[:, hd, :], in_=pb[:, hh * 128:(hh + 1) * 128]),
                                     reads=pb.b + stats.b, writes=stats.b)
                        for hd in range(8):
                            P.op("dve", lambda e, hd=hd: e.bn_aggr(out=mv[:, hd, :], in_=stats[:, hd, :]),
                                 reads=stats.b + mv.b, writes=mv.b)
                        P.op("act", lambda e: e.activation(out=rs[:], in_=mv[:, :, 1], func=AF.Ln, scale=1.0, bias=EPS),
                             reads=mv.b, writes=rs.b)
                        P.op("act", lambda e: e.activation(out=rs[:], in_=rs[:], func=AF.Exp, scale=-0.5),
                             reads=rs.b, writes=rs.b)
                        P.op("dve", lambda e: e.scalar_tensor_tensor(out=nmr[:], in0=mv[:, :, 0], scalar=-1.0, in1=rs[:],
                                                                      op0=ALU.mult, op1=ALU.mult), reads=mv.b + rs.b, writes=nmr.b)
                        for half in range(2):
                            pb = obanks[half]
                            for hh in range(4):
                                hd = half * 4 + hh
                                P.op("act", lambda e, pb=pb, hh=hh, hd=hd: e.activation(
                                    out=on_[:, hd, :], in_=pb[:, hh * 128:(hh + 1) * 128], func=AF.Identity,
                                    scale=rs[:, hd:hd + 1], bias=nmr[:, hd:hd + 1]),
                                    reads=pb.b + rs.b + nmr.b + on_.b, writes=on_.b)
                        pbt = psb[6]
                        p16 = pbt[:].bitcast(BF16)
                        for hd in range(8):
                            P.op("pe", lambda e, hd=hd: e.transpose(p16[:, hd * 128:(hd + 1) * 128], on_[:, hd, :], ident_b[:]),
                                 reads=on_.b + ident_b.b, writes=pbt.b, inc=(hd == 7))
                        for half in range(2):
                            P.op("dve" if half == 0 else "pool" if False else "dve", lambda e, half=half: e.tensor_tensor(
                                out=mT[:, half * 4:(half + 1) * 4, cs],
                                in0=p16[:, half * 512:(half + 1) * 512].rearrange("p (a b) -> p a b", a=4),
                                in1=gat[i][:, half * 4:(half + 1) * 4, cs], op=ALU.mult),
                                reads=pbt.b + gat[i].b, writes=[mT.b[k] for k in range(half * 4, half * 4 + 4)])
                        for half in range(2):
                            pb = psb[7 - half]
                            for ff in range(4):
                                f = half * 4 + ff
                                g = f // 2
                                bo = (c * 4 + g) * 128
                                P.op("pe", lambda e, pb=pb, ff=ff, f=f, bo=bo: e.matmul(
                                    pb[:, ff * 128:(ff + 1) * 128], lhsT=u_[:, f * 128:(f + 1) * 128], rhs=bM[i][:, bo:bo + 128],
                                    start=True, stop=False), reads=u_.b + bM[i].b, writes=pb.b, inc=False)
                                P.op("pe", lambda e, pb=pb, ff=ff, f=f, bo=bo: e.matmul(
                                    pb[:, ff * 128:(ff + 1) * 128], lhsT=uh_[:, f * 128:(f + 1) * 128], rhs=bH[i][:, bo:bo + 128],
                                    start=False, stop=True), reads=uh_.b + bH[i].b, writes=pb.b, inc=(ff == 3))
                            P.op("act", lambda e, pb=pb, half=half: e.copy(
                                out=pT[:, half * 4:(half + 1) * 4, cs], in_=pb[:].rearrange("p (a b) -> p a b", a=4)),
                                reads=pb.b, writes=[pT.b[k] for k in range(half * 4, half * 4 + 4)])

                    tb_i = [0]

                    def tail_bank():
                        b_ = psb[(0, 1, 6, 7)[tb_i[0] % 4]]
                        tb_i[0] += 1
                        return b_

                    def tile_tail(t):
                        i = t % NBUF
                        c0 = t * TC
                        for hh in range(8):
                            P.dma(lambda e, hh=hh: e.dma_start(out=xT[0][:, hh, :], in_=xT_d[hh * 128:(hh + 1) * 128, c0:c0 + TC]),
                                  writes=[xT[0].b[hh]])
                        for dchunk in range(8):
                            g, dc = dchunk // 2, dchunk % 2
                            pb = tail_bank()
                            for cc in range(2):
                                P.op("pe", lambda e, pb=pb, g=g, dc=dc, cc=cc: e.matmul(
                                    pb[:, 0:TC], lhsT=wP[:, g, cc, dc * 128:(dc + 1) * 128], rhs=pT[:, 2 * g + cc, :],
                                    start=(cc == 0), stop=(cc == 1)), reads=[wP.b[2 * g + cc], pT.b[2 * g + cc]], writes=pb.b, inc=(cc == 1))
                            tm_ = tmp[dchunk % 2]
                            P.op("dve", lambda e, pb=pb, dchunk=dchunk, tm_=tm_: e.scalar_tensor_tensor(
                                out=tm_[:], in0=pb[:, 0:TC], scalar=pscale[:, dchunk:dchunk + 1], in1=sgp[i][:, dchunk, :],
                                op0=ALU.mult, op1=ALU.mult), reads=pb.b + sgp[i].b + vec.b, writes=tm_.b)
                            P.op("dve", lambda e, dchunk=dchunk, tm_=tm_: e.tensor_tensor(
                                out=mT[:, dchunk, :], in0=mT[:, dchunk, :], in1=tm_[:], op=ALU.add),
                                reads=tm_.b + [mT.b[dchunk]], writes=[mT.b[dchunk]])
                        for n in range(8):
                            pb = tail_bank()
                            for k in range(8):
                                P.op("pe", lambda e, pb=pb, n=n, k=k: e.matmul(
                                    pb[:, 0:TC], lhsT=wO[:, k, n * 128:(n + 1) * 128], rhs=mT[:, k, :], start=(k == 0), stop=(k == 7)),
                                    reads=[wO.b[k], mT.b[k]], writes=pb.b, inc=(k == 7))
                            P.op("act", lambda e, pb=pb, n=n: e.copy(out=yy[:, n, :], in_=pb[:, 0:TC]), reads=pb.b, writes=[yy.b[n]])
                        rms_stats(None, yy, TC, pT, rstd, msb_bank=tail_bank())
                        for k in range(8):
                            eng = "dve"
                            if eng == "dve":
                                P.op(eng, lambda e, k=k: e.scalar_tensor_tensor(
                                    out=yy[:, k, :], in0=yy[:, k, :], scalar=g_post[:, k:k + 1], in1=rstd[:], op0=ALU.mult, op1=ALU.mult),
                                    reads=[yy.b[k]] + rstd.b + vec.b, writes=[yy.b[k]])
                            else:
                                P.op(eng, lambda e, k=k: e.tensor_tensor(out=yy[:, k, :], in0=yy[:, k, :], in1=rstd[:], op=ALU.mult),
                                     reads=[yy.b[k]] + rstd.b, writes=[yy.b[k]])
                                P.op(eng, lambda e, k=k: e.tensor_scalar(out=yy[:, k, :], in0=yy[:, k, :], scalar1=g_post[:, k:k + 1],
                                                                         scalar2=None, op0=ALU.mult),
                                     reads=[yy.b[k]] + vec.b, writes=[yy.b[k]])
                            P.op(eng, lambda e, k=k: e.tensor_tensor(out=xT[0][:, k, :], in0=xT[0][:, k, :], in1=yy[:, k, :], op=ALU.add),
                                 reads=[yy.b[k], xT[0].b[k]], writes=[xT[0].b[k]])
                            P.dma(lambda e, k=k: e.dma_start(out=xm_d[k * 128:(k + 1) * 128, c0:c0 + TC], in_=xT[0][:, k, :]),
                                  reads=[xT[0].b[k]])

                    gens = {}
                    load_tile(0)
                    for g in range(NCH + 1):
                        if g < NCH:
                            t, c = divmod(g, CPT)
                            gens[g] = chunk(t, c)
                            next(gens[g])
                        if g >= 1:
                            for _ in gens.pop(g - 1):
                                pass
                            tp, cp = divmod(g - 1, CPT)
                            if cp == CPT - 1:
                                tile_tail(tp)
                        if g < NCH and t + 1 < NTC:
                            assert CPT == 4
                            load_tile(t + 1, parts=(c,))
                    P.barrier()

            if KSTOP >= 3:
                _pass2()
            def _pass3():
                with ExitStack() as es:
                    NTM = NT // TM
                    w1 = sb(es, "w1", [128, 8, DFF], BF16, nb=8)
                    w2 = sb(es, "w2", [128, 32, D], BF16, nb=32)
                    for k in range(8):
                        P.dma(lambda e, k=k: e.dma_start(out=w1[:, k, :], in_=w_mlp1[l, k * 128:(k + 1) * 128, :]),
                              writes=[w1.b[k]], eng="pool")
                    for k in range(32):
                        P.dma(lambda e, k=k: e.dma_start(out=w2[:, k, :], in_=w_mlp2[l, k * 128:(k + 1) * 128, :]),
                              writes=[w2.b[k]], eng="pool")
                    xT = [sb(es, "m_xT%d" % i, [128, 8, TM], F32, nb=8) for i in range(2)]
                    hT = [sb(es, "m_hT%d" % i, [128, 8, TM], BF16, nb=8) for i in range(2)]
                    fT = sb(es, "m_fT", [128, 32, TM], BF16, nb=32)
                    rl = [sb(es, "m_rl%d" % i, [128, TM], F32) for i in range(3)]
                    yy = sb(es, "m_y", [128, 8, TM], F32, nb=8)
                    rstd = sb(es, "m_rstd", [128, TM], F32)
                    rstd2 = sb(es, "m_rstd2", [128, TM], F32)
                    otok = [sb(es, "m_otok%d" % i, [128, D], F32) for i in range(1)] if l == depth - 1 else None
                    g_pre = vec[:, 2, l, :]
                    g_post = vec[:, 3, l, :]

                    def load_x(t):
                        c0 = t * TM
                        for k in range(8):
                            P.dma(lambda e, k=k: e.dma_start(out=xT[t % 2][:, k, :], in_=xm_d[k * 128:(k + 1) * 128, c0:c0 + TM]),
                                  writes=[xT[t % 2].b[k]])

                    def norm_x(t):
                        h, x = hT[t % 2], xT[t % 2]
                        rms_stats(None, x, TM, h, rstd)
                        for k in range(8):
                            eng = "dve"
                            P.op(eng, lambda e, k=k: e.scalar_tensor_tensor(
                                out=h[:, k, :], in0=x[:, k, :], scalar=g_pre[:, k:k + 1], in1=rstd[:], op0=ALU.mult, op1=ALU.mult),
                                reads=[x.b[k]] + rstd.b + vec.b, writes=[h.b[k]])

                    def stage1(t, lo_, hi_):
                        h = hT[t % 2]
                        for cidx in range(lo_, hi_):
                            pb = bank()
                            for k in range(8):
                                P.op("pe", lambda e, pb=pb, cidx=cidx, k=k: e.matmul(
                                    pb[:, 0:TM], lhsT=w1[:, k, cidx * 128:(cidx + 1) * 128], rhs=h[:, k, :], start=(k == 0), stop=(k == 7)),
                                    reads=[w1.b[k], h.b[k]], writes=pb.b, inc=(k == 7))
                            r = rl[cidx % 3]
                            P.op("act", lambda e, pb=pb, r=r: e.activation(out=r[:], in_=pb[:, 0:TM], func=AF.Relu),
                                 reads=pb.b, writes=r.b)
                            if cidx % 2 == 0:
                                P.op("dve", lambda e, r=r, cidx=cidx: e.tensor_tensor(out=fT[:, cidx, :], in0=r[:], in1=r[:], op=ALU.mult),
                                     reads=r.b, writes=[fT.b[cidx]])
                            else:
                                P.op("act", lambda e, r=r, cidx=cidx: e.activation(out=fT[:, cidx, :], in_=r[:], func=AF.Square),
                                     reads=r.b, writes=[fT.b[cidx]])

                    def stage2(t):
                        x = xT[t % 2]
                        c0 = t * TM
                        for n in range(8):
                            pb = bank()
                            for cidx in range(32):
                                P.op("pe", lambda e, pb=pb, n=n, cidx=cidx: e.matmul(
                                    pb[:, 0:TM], lhsT=w2[:, cidx, n * 128:(n + 1) * 128], rhs=fT[:, cidx, :],
                                    start=(cidx == 0), stop=(cidx == 31)), reads=[w2.b[cidx], fT.b[cidx]], writes=pb.b, inc=(cidx == 31))
                            P.op("act", lambda e, pb=pb, n=n: e.copy(out=yy[:, n, :], in_=pb[:, 0:TM]), reads=pb.b, writes=[yy.b[n]])
                        rms_stats(None, yy, TM, hT[t % 2], rstd2)
                        for k in range(8):
                            eng = "dve"
                            if eng == "dve":
                                P.op(eng, lambda e, k=k: e.scalar_tensor_tensor(
                                    out=yy[:, k, :], in0=yy[:, k, :], scalar=g_post[:, k:k + 1], in1=rstd2[:], op0=ALU.mult, op1=ALU.mult),
                                    reads=[yy.b[k]] + rstd2.b + vec.b, writes=[yy.b[k]])
                            else:
                                P.op(eng, lambda e, k=k: e.tensor_tensor(out=yy[:, k, :], in0=yy[:, k, :], in1=rstd2[:], op=ALU.mult),
                                     reads=[yy.b[k]] + rstd2.b, writes=[yy.b[k]])
                                P.op(eng, lambda e, k=k: e.tensor_scalar(out=yy[:, k, :], in0=yy[:, k, :], scalar1=g_post[:, k:k + 1],
                                                                         scalar2=None, op0=ALU.mult),
                                     reads=[yy.b[k]] + vec.b, writes=[yy.b[k]])
                            P.op(eng, lambda e, k=k: e.tensor_tensor(out=x[:, k, :], in0=x[:, k, :], in1=yy[:, k, :], op=ALU.add),
                                 reads=[yy.b[k], x.b[k]], writes=[x.b[k]])
                            if l < depth - 1:
                                P.dma(lambda e, k=k: e.dma_start(out=xT_d[k * 128:(k + 1) * 128, c0:c0 + TM], in_=x[:, k, :]),
                                      reads=[x.b[k]])
                        if l == depth - 1:
                            for c in range(TM // CH):
                                ot = otok[0]
                                for half in range(2):
                                    pb = bank()
                                    for kk in range(4):
                                        k = half * 4 + kk
                                        P.op("pe", lambda e, pb=pb, k=k, kk=kk, c=c: e.transpose(
                                            pb[:, kk * 128:(kk + 1) * 128], x[:, k, c * CH:(c + 1) * CH], ident_f[:]),
                                            reads=[x.b[k]] + ident_f.b, writes=pb.b, inc=(kk == 3))
                                    P.op("act" if half else "dve",
                                         (lambda e, pb=pb, half=half, ot=ot: e.copy(out=ot[:, half * 512:(half + 1) * 512], in_=pb[:]))
                                         if half else
                                         (lambda e, pb=pb, half=half, ot=ot: e.tensor_copy(out=ot[:, half * 512:(half + 1) * 512], in_=pb[:])),
                                         reads=pb.b + ot.b, writes=ot.b)
                                P.dma(lambda e, c=c, ot=ot: e.dma_start(out=y_out[c0 + c * CH:c0 + (c + 1) * CH, :], in_=ot[:]),
                                      reads=ot.b)

                    load_x(0)
                    norm_x(0)
                    for t in range(NTM):
                        stage1(t, 0, 16)
                        if t + 1 < NTM:
                            load_x(t + 1)
                        stage1(t, 16, 32)
                        if t + 1 < NTM:
                            norm_x(t + 1)
                        stage2(t)
                    P.barrier()

            if KSTOP >= 4:
                _pass3()
        for l_ in range(depth):
            do_layer(l_)
        P.barrier()
        P.emit()
    return nc, P


def host_tables(seq_lens, NT, TC=512):
    NCH = NT // CH
    pos = np.zeros(NT, np.int64)
    seq_id = np.full(NT, -1, np.int64)
    seq_start = np.zeros(NT, np.int64)
    seq_len = np.ones(NT, np.int64) * CH
    o = 0
    for si, L in enumerate(seq_lens):
        pos[o:o + L] = np.arange(L)
        seq_id[o:o + L] = si
        seq_start[o:o + L] = o
        seq_len[o:o + L] = L
        o += L
    for c in range(o // CH, NCH):
        pos[c * CH:(c + 1) * CH] = np.arange(CH)
        seq_id[c * CH:(c + 1) * CH] = 1000 + c
        seq_start[c * CH:(c + 1) * CH] = c * CH
    half = DK // 2
    inv = (ROPE_BASE ** (-np.arange(half, dtype=np.float32) / np.float32(half))).astype(np.float32)
    ang = pos.astype(np.float32)[:, None] * inv[None, :]
    rope_t = np.concatenate([np.cos(ang), np.sin(ang)], axis=1).astype(np.float32)
    cid = seq_id[::CH]
    cont = (cid[:-1] == cid[1:]).astype(np.float32)
    flagkv = np.zeros((128, NCH), np.float32)
    flagkv[0:64, :NCH - 1] = cont[None, :]
    flagkv[64:128, 1:] = cont[None, :]
    scanflag = np.zeros((128, NCH), np.float32)
    scanflag[0:64, :NCH - 1] = cont[None, :]
    scanflag[64:128, :NCH - 1] = cont[::-1][None, :]
    CPT = TC // CH
    bandM = np.zeros((NT // TC, 128, CPT, 4, 128), np.float32)
    bandH = np.zeros((NT // TC, 16, CPT, 4, 128), np.float32)
    i_idx = np.arange(CH)
    for c in range(NCH):
        t, cc = c // CPT, c % CPT
        tok = c * CH + i_idx
        s0 = seq_start[tok]
        s1 = s0 + seq_len[tok]
        for g, w in enumerate(POOL_WINDOWS):
            hw = w // 2
            lo = np.maximum(tok - hw, s0)
            hi = np.minimum(tok + hw, s1)
            cntv = (hi - lo).astype(np.float32)
            for r in range(-8, CH + 8):
                j = c * CH + r
                val = ((j >= lo) & (j < hi)).astype(np.float32) / cntv
                if 0 <= r < CH:
                    val = val - (i_idx == r).astype(np.float32)
                    bandM[t, r, cc, g, :] = val
                elif r < 0:
                    bandH[t, r + 8, cc, g, :] = val
                else:
                    bandH[t, 8 + (r - CH), cc, g, :] = val
    bandM = bandM.reshape(NT // TC, 128, -1).astype(ml_dtypes.bfloat16)
    bandH = bandH.reshape(NT // TC, 16, -1).astype(ml_dtypes.bfloat16)
    return dict(rope_t=rope_t, flagkv=flagkv, scanflag=scanflag, bandM=bandM, bandH=bandH)


def const_table():
    j = np.arange(128, dtype=np.float32)[:, None]
    i = np.arange(128, dtype=np.float32)[None, :]
    c = np.zeros((128, 5 * 128 + 4), np.float32)
    c[:, 516:644] = np.eye(128, dtype=np.float32)
    c[:, 0:128] = np.maximum(i - j, 0)
    c[:, 128:256] = np.maximum(j - i, 0)
    c[:, 256:384] = (i >= j)
    c[:, 384:512] = (j > i)
    c[:, 512] = 127 - j[:, 0]
    c[:, 513] = j[:, 0]
    c[:, 514] = j[:, 0] + 1
    c[:, 515] = 128 - j[:, 0]
    return c


def shared_inputs(inp, depth):
    f = lambda a: np.ascontiguousarray(np.asarray(a, dtype=np.float32))
    vecs = np.zeros((128, 6, depth, 8), np.float32)
    for idx, name in enumerate(["norm_mix_pre", "norm_mix_post", "norm_mlp_pre", "norm_mlp_post", "pool_scale", "ret_gn"]):
        a = f(inp[name])[:depth]
        vecs[:, idx] = a.reshape(depth, 8, 128).transpose(2, 0, 1)
    dec = np.concatenate([f(inp["ret_decay_fwd"])[:depth].reshape(-1), f(inp["ret_decay_bwd"])[:depth].reshape(-1)])
    decays = np.ascontiguousarray(np.broadcast_to(dec[None, :], (128, dec.size)))
    return dict(w_in=f(inp["w_in"])[:depth], w_out=f(inp["w_out"])[:depth], w_mlp1=f(inp["w_mlp1"])[:depth],
                w_mlp2=f(inp["w_mlp2"])[:depth], pool_w=f(inp["pool_w"])[:depth], vecs=vecs, decays=decays,
                consts=const_table())


_CACHE = {}


def run_cores(core_tokens, core_seqlens, NT, depth, inp, n_cores=8):
    key = (NT, depth)
    if key not in _CACHE:
        _CACHE[key] = build(NT, depth)[0]
    nc = _CACHE[key]
    sh = shared_inputs(inp, depth)
    in_maps = []
    for ci in range(n_cores):
        xt = np.zeros((NT, D), np.float32)
        a = core_tokens[ci]
        xt[:a.shape[0]] = a
        m = dict(sh)
        m["x_in"] = xt
        m.update(host_tables(core_seqlens[ci], NT))
        in_maps.append(m)
    res = run_bass_kernel_spmd(nc, in_maps, core_ids=list(range(n_cores)))
    return [r["y_out"] for r in res.results]


def kernel(**inputs):
    xp = np.asarray(inputs["x_prompt"], dtype=np.float32)
    xs = np.asarray(inputs["x_sample"], dtype=np.float32)
    depth = 4
    NT = 16384
    toks, lens = [xp[0]], [[16384]]
    for ci in range(4):
        toks.append(xs[ci * 4:(ci + 1) * 4].reshape(-1, D))
        lens.append([4096] * 4)
    for ci in range(3):
        toks.append(np.zeros((0, D), np.float32))
        lens.append([])
    outs = run_cores(toks, lens, NT, depth, inputs)
    y_prompt = outs[0].reshape(1, 16384, D).astype(np.float32)
    y_sample = np.concatenate([outs[1 + ci].reshape(4, 4096, D) for ci in range(4)], axis=0).astype(np.float32)
    return (y_prompt, y_sample)
```

```python
import numpy as np
import ml_dtypes
from contextlib import ExitStack
import concourse.bass as bass
import concourse.mybir as mybir
from concourse.bass_utils import run_bass_kernel_spmd

F32 = mybir.dt.float32
BF16 = mybir.dt.bfloat16
ALU = mybir.AluOpType
AF = mybir.ActivationFunctionType

D = 1024
HEADS = 8
DK = 64
DV = 128
DIN = 6144
DFF = 4096
EPS = 1e-6
CH = 128
ROPE_BASE = 10000.0
POOL_WINDOWS = (2, 4, 8, 16)

ENGS = ["pe", "act", "dve", "pool", "sp"]
import os as _os
KSTOP = int(_os.environ.get("KSTOP", "4"))


class Buf:
    __slots__ = ("name", "w", "r")

    def __init__(self, name=""):
        self.name = name
        self.w = None
        self.r = {}


class Prog:
    def __init__(self, nc, n_dma_sems=40):
        self.nc = nc
        self.q = {e: [] for e in ENGS}
        self.cnt = {e: 0 for e in ENGS}
        self.waited = {e: {} for e in ENGS}
        self.n_dma = n_dma_sems
        self.dma_i = 0
        self.n_sw = 8
        self.sw_i = 0
        self.dma_cnt = [0] * n_dma_sems
        self.dma_last = [None] * n_dma_sems
        self.ninstr = 0

    def _need(self, eng, tok, waits):
        if tok is None:
            return
        key = (tok[0], tok[1])
        val = tok[2]
        if self.waited[eng].get(key, 0) >= val:
            return
        if tok[0] == "e" and tok[1] == eng and (val > self.cnt[eng] or eng == "pe" or _os.environ.get("NOSELF")):
            return
        self.waited[eng][key] = val
        waits.append((key, val))

    def _deps(self, eng, reads, writes, waits):
        for b in reads:
            self._need(eng, b.w, waits)
        for b in writes:
            self._need(eng, b.w, waits)
            for k, v in b.r.items():
                self._need(eng, (k[0], k[1], v), waits)

    def _mark(self, tok, reads, writes):
        k = (tok[0], tok[1])
        for b in reads:
            if b.r.get(k, 0) < tok[2]:
                b.r[k] = tok[2]
        for b in writes:
            b.w = tok
            b.r = {}

    def op(self, eng, fn, reads=(), writes=(), inc=True):
        waits = []
        self._deps(eng, reads, writes, waits)
        c = self.cnt[eng] + 1
        tok = ("e", eng, c)
        if inc:
            self.cnt[eng] = c
        self.q[eng].append((waits, fn, ("e", inc)))
        self._mark(tok, reads, writes)
        self.ninstr += 1
        return tok

    def dma(self, fn, reads=(), writes=(), eng="sp"):
        waits = []
        self._deps(eng, reads, writes, waits)
        if eng == "pool":
            i = self.n_dma - self.n_sw + self.sw_i
            self.sw_i = (self.sw_i + 1) % self.n_sw
        else:
            i = self.dma_i
            self.dma_i = (i + 1) % (self.n_dma - self.n_sw)
        self._need(eng, self.dma_last[i], waits)
        self.dma_cnt[i] += 16
        tok = ("d", i, self.dma_cnt[i])
        self.dma_last[i] = tok
        self.q[eng].append((waits, fn, ("d", i)))
        self._mark(tok, reads, writes)
        self.ninstr += 1
        return tok

    def barrier(self):
        for e in ENGS:
            waits = []
            for i in range(self.n_dma):
                self._need(e, self.dma_last[i], waits)
            for o in ENGS:
                if o != "sp" and o != e and self.cnt[o] > 0:
                    self._need(e, ("e", o, self.cnt[o]), waits)
            if e != "sp" and self.cnt[e] > 0:
                self._need(e, ("e", e, self.cnt[e]), waits)
            self.q[e].append((waits, None, None))

    def emit(self):
        nc = self.nc
        with ExitStack() as es:
            esem = {e: es.enter_context(nc.semaphore("s_" + e)) for e in ENGS if e != "sp"}
            dsem = [es.enter_context(nc.semaphore("d_%d" % i)) for i in range(self.n_dma)]
            block = es.enter_context(nc.Block())

            def semof(key):
                return esem[key[1]] if key[0] == "e" else dsem[key[1]]

            def run(engname):
                def body(engine):
                    for waits, fn, kind in self.q[engname]:
                        for key, val in waits:
                            engine.wait_ge(semof(key), val)
                        if fn is None:
                            continue
                        ins = fn(engine)
                        if kind[0] == "e":
                            if kind[1]:
                                ins.then_inc(esem[engname], 1)
                        else:
                            ins.then_inc(dsem[kind[1]], 16)
                return body

            block.tensor(run("pe"))
            block.scalar(run("act"))
            block.vector(run("dve"))
            block.gpsimd(run("pool"))
            block.sync(run("sp"))


class T:
    def __init__(self, t, nb=1):
        self.t = t
        self.b = [Buf() for _ in range(nb)]

    def __getitem__(self, k):
        return self.t[k]


def build(NT, depth, TA=512, TC=512, TM=256):
    NCH = NT // CH
    nc = bass.Bass("TRN2", target_bir_lowering=False)
    P = Prog(nc)

    def din(name, shape, dt=F32):
        return nc.dram_tensor(name, list(shape), dt, kind="ExternalInput").ap()

    def dscr(name, shape, dt):
        return nc.dram_tensor(name, list(shape), dt, kind="Internal").ap()

    x_in = din("x_in", [NT, D])
    y_out = nc.dram_tensor("y_out", [NT, D], F32, kind="ExternalOutput").ap()
    w_in = din("w_in", [depth, D, DIN])
    w_out = din("w_out", [depth, D, D])
    w_mlp1 = din("w_mlp1", [depth, D, DFF])
    w_mlp2 = din("w_mlp2", [depth, DFF, D])
    pool_w = din("pool_w", [depth, 4, 256, 256])
    vecs = din("vecs", [128, 6, depth, 8])
    decays = din("decays", [128, 2 * depth * 8])
    rope_t = din("rope_t", [NT, 64])
    flagkv = din("flagkv", [128, NCH])
    scanflag = din("scanflag", [128, NCH])
    bandM = din("bandM", [NT // TC, 128, (TC // CH) * 4 * 128], BF16)
    bandH = din("bandH", [NT // TC, 16, (TC // CH) * 4 * 128], BF16)
    consts = din("consts", [128, 5 * 128 + 4])

    xT_d = dscr("xT_d", [D, NT], F32)
    xm_d = dscr("xm_d", [D, NT], F32)
    qT_d = dscr("qT_d", [512, NT], BF16)
    kT_d = dscr("kT_d", [1024, NT], BF16)
    qsT_d = dscr("qsT_d", [1024, NT], BF16)
    gate_d = dscr("gate_d", [1024, NT], BF16)
    sgp_d = dscr("sgp_d", [1024, NT], BF16)
    v_d = dscr("v_d", [NT, 1024], BF16)
    u_d = dscr("u_d", [NT + 16, 1024], BF16)
    kv_d = dscr("kv_d", [NCH, 128, 1024], F32)
    st_d = dscr("st_d", [NCH, 128, 1024], BF16)

    top = ExitStack()

    _uid = [0]

    def sb(es, name, shape, dt, nb=1):
        _uid[0] += 1
        return T(es.enter_context(nc.sbuf_tensor("%s_%d" % (name, _uid[0]), list(shape), dt)), nb)

    with top:
        psb = [T(top.enter_context(nc.psum_tensor("ps%d" % i, [128, 512], F32))) for i in range(8)]
        ps_i = [0]
        NBANK = 7 if _os.environ.get("K4") else 8

        def bank():
            b = psb[ps_i[0]]
            ps_i[0] = (ps_i[0] + 1) % NBANK
            return b

        ident_b = sb(top, "ident_b", [128, 128], BF16)
        ident_f = sb(top, "ident_f", [128, 128], F32)
        ones_b = sb(top, "ones_b", [128, 128], BF16)
        nhalf = sb(top, "nhalf", [128, 512], F32)
        cst = sb(top, "cst", [128, 5 * 128 + 4], F32)
        vec = sb(top, "vec", [128, 6, depth, 8], F32)
        dcy = sb(top, "dcy", [128, 2 * depth * 8], F32)
        lg = sb(top, "lg", [128, 2 * depth * 8], F32)
        fkv = sb(top, "fkv", [128, NCH], F32)
        sfl = sb(top, "sfl", [128, NCH], F32)
        dmask = sb(top, "dmask", [128, 8, 128], F32)
        dtmp = sb(top, "dtmp", [128, 128], F32)
        zx = sb(top, "zx", [128, 4, 8], F32)
        dec = sb(top, "dec", [128, 8], F32)
        atab = sb(top, "atab", [128, NCH, 8], F32)
        tmpes = ExitStack()
        zero_b = sb(tmpes, "zero_b", [8, 1024], BF16)

        P.op("pool", lambda e: e.memset(ident_f[:], 0.0), writes=ident_f.b)
        P.dma(lambda e: e.dma_start(out=cst[:], in_=consts), writes=cst.b)
        P.dma(lambda e: e.dma_start(out=vec[:], in_=vecs), writes=vec.b)
        P.dma(lambda e: e.dma_start(out=dcy[:], in_=decays), writes=dcy.b)
        P.dma(lambda e: e.dma_start(out=fkv[:], in_=flagkv), writes=fkv.b)
        P.dma(lambda e: e.dma_start(out=sfl[:], in_=scanflag), writes=sfl.b)
        P.op("dve", lambda e: e.tensor_copy(out=ident_f[:], in_=cst[:, 516:644]), reads=cst.b, writes=ident_f.b)
        P.op("dve", lambda e: e.tensor_copy(out=ident_b[:], in_=ident_f[:]), reads=ident_f.b, writes=ident_b.b)
        P.op("pool", lambda e: e.memset(ones_b[:], 1.0), writes=ones_b.b)
        P.op("pool", lambda e: e.memset(nhalf[:], -0.5), writes=nhalf.b)
        P.op("pool", lambda e: e.memset(zero_b[:], 0.0), writes=zero_b.b)
        P.dma(lambda e: e.dma_start(out=u_d[0:8, :], in_=zero_b[:]), reads=zero_b.b)
        P.dma(lambda e: e.dma_start(out=u_d[NT + 8:NT + 16, :], in_=zero_b[:]), reads=zero_b.b)
        P.op("act", lambda e: e.activation(out=lg[:], in_=dcy[:], func=AF.Exp, scale=-1.0), reads=dcy.b, writes=lg.b)
        P.op("act", lambda e: e.activation(out=lg[:], in_=lg[:], func=AF.Ln, bias=1.0, scale=1.0), reads=lg.b, writes=lg.b)
        P.op("dve", lambda e: e.tensor_scalar(out=lg[:], in0=lg[:], scalar1=-1.0, scalar2=None, op0=ALU.mult),
             reads=lg.b, writes=lg.b)
        P.barrier()
        tmpes.close()

        def rms_stats(es_tiles, xT, width, sqr, rstd, src_is_psum=None, msb_bank=None):
            msb = msb_bank if msb_bank is not None else bank()
            K5 = _os.environ.get("K5", "")
            for k in range(8):
                if K5 == "d":
                    P.op("dve", lambda e, k=k: e.tensor_tensor(out=sqr[:, k, 0:width], in0=xT[:, k, 0:width], in1=xT[:, k, 0:width], op=ALU.mult),
                         reads=[xT.b[k]], writes=[sqr.b[k]])
                else:
                    P.op("act", lambda e, k=k: e.activation(out=sqr[:, k, 0:width], in_=xT[:, k, 0:width], func=AF.Square),
                         reads=[xT.b[k]], writes=[sqr.b[k]])
            for k in range(8 if K5 != "s" else 0):
                P.op("pe", lambda e, k=k: e.matmul(msb[:, 0:width], lhsT=ones_b[:], rhs=sqr[:, k, 0:width],
                                                   start=(k == 0), stop=(k == 7)),
                     reads=[sqr.b[k]] + ones_b.b, writes=msb.b, inc=(k == 7))
            P.op("act", lambda e: e.activation(out=rstd[:, 0:width], in_=msb[:, 0:width], func=AF.Ln, scale=1.0 / D, bias=EPS),
                 reads=msb.b, writes=rstd.b)
            P.op("act", lambda e: e.activation(out=rstd[:, 0:width], in_=rstd[:, 0:width], func=AF.Exp, scale=-0.5),
                 reads=rstd.b, writes=rstd.b)

        def do_layer(l):
            lgf = lambda h: lg[:, l * 8 + h:l * 8 + h + 1]
            lgb = lambda h: lg[:, depth * 8 + l * 8 + h:depth * 8 + l * 8 + h + 1]
            for h in range(HEADS):
                P.op("act", lambda e, h=h: e.activation(out=dmask[:, h, :], in_=cst[:, 0:128], func=AF.Exp, scale=lgf(h)),
                     reads=cst.b + lg.b, writes=dmask.b)
                P.op("dve", lambda e, h=h: e.tensor_tensor(out=dmask[:, h, :], in0=dmask[:, h, :], in1=cst[:, 256:384],
                                                           op=ALU.mult), reads=dmask.b + cst.b, writes=dmask.b)
                P.op("act", lambda e, h=h: e.activation(out=dtmp[:], in_=cst[:, 128:256], func=AF.Exp, scale=lgb(h)),
                     reads=cst.b + lg.b, writes=dtmp.b)
                P.op("dve", lambda e, h=h: e.tensor_tensor(out=dtmp[:], in0=dtmp[:], in1=cst[:, 384:512], op=ALU.mult),
                     reads=dtmp.b + cst.b, writes=dtmp.b)
                P.op("dve", lambda e, h=h: e.tensor_tensor(out=dmask[:, h, :], in0=dmask[:, h, :], in1=dtmp[:], op=ALU.add),
                     reads=dmask.b + dtmp.b, writes=dmask.b)
            lo = l * 8
            hb = depth * 8 + l * 8
            for idx, (col, base) in enumerate([(512, lo), (513, hb), (514, lo), (515, hb)]):
                P.op("dve", lambda e, idx=idx, col=col, base=base: e.tensor_scalar(
                    out=zx[:, idx, :], in0=lg[:, base:base + 8], scalar1=cst[:, col:col + 1], scalar2=None, op0=ALU.mult),
                    reads=lg.b + cst.b, writes=zx.b)
            P.op("act", lambda e: e.activation(out=zx[:], in_=zx[:], func=AF.Exp), reads=zx.b, writes=zx.b)
            P.op("act", lambda e: e.activation(out=dec[0:64, :], in_=lg[0:64, lo:lo + 8], func=AF.Exp, scale=128.0),
                 reads=lg.b, writes=dec.b)
            P.op("act", lambda e: e.activation(out=dec[64:128, :], in_=lg[64:128, hb:hb + 8], func=AF.Exp, scale=128.0),
                 reads=lg.b + dec.b, writes=dec.b)
            P.op("dve", lambda e: e.tensor_tensor(out=atab[:], in0=dec[:].unsqueeze(1).to_broadcast([128, NCH, 8]),
                                                  in1=sfl[:].unsqueeze(2).to_broadcast([128, NCH, 8]), op=ALU.mult),
                 reads=dec.b + sfl.b, writes=atab.b)
            P.barrier()

            def _pass0():
                with ExitStack() as es:
                    NTA = NT // TA
                    CPT = TA // CH
                    wA = sb(es, "wA", [128, 8, DIN], BF16, nb=8)
                    for k in range(8 if not _os.environ.get("KNOW") else 0):
                        P.dma(lambda e, k=k: e.dma_start(out=wA[:, k, :], in_=w_in[l, k * 128:(k + 1) * 128, :]),
                              writes=[wA.b[k]], eng="pool")
                    xT = sb(es, "a_xT", [128, 8, TA], F32, nb=8)
                    xtok = [sb(es, "a_xtok%d" % i, [128, D], F32) for i in range(1)] if l == 0 else None
                    hT = [sb(es, "a_hT%d" % i, [128, 8, TA], BF16, nb=8) for i in range(2)]
                    rstd = sb(es, "a_rstd", [128, TA], F32)
                    sig = [sb(es, "a_sig%d" % i, [128, TA], F32) for i in range(3)]
                    gst = [sb(es, "a_gst%d" % i, [128, TA], BF16) for i in range(2)]
                    pst = [sb(es, "a_pst%d" % i, [128, TA], BF16) for i in range(2)]
                    q32 = sb(es, "a_q32", [128, 512], F32)
                    k32 = sb(es, "a_k32", [128, 512], F32)
                    rt1 = sb(es, "a_rt1", [128, 8, 32], F32)
                    rt2 = sb(es, "a_rt2", [128, 8, 32], F32)
                    qbf = sb(es, "a_qbf", [128, 8, 64], BF16)
                    kbf = sb(es, "a_kbf", [128, 8, 64], BF16)
                    qs = sb(es, "a_qs", [128, 8, 2, 64], BF16)
                    kz = sb(es, "a_kz", [128, 8, 2, 64], BF16)
                    vbf = [sb(es, "a_vbf%d" % i, [128, 1024], BF16) for i in range(1)]
                    ubf = [sb(es, "a_ubf%d" % i, [128, 1024], BF16) for i in range(1)]
                    kvs = [sb(es, "a_kvs%d" % i, [128, 1024], F32) for i in range(1)]
                    rope = [sb(es, "a_rope%d" % i, [128, 64], F32) for i in range(2)]
                    qT = sb(es, "a_qT", [128, 4, TA], BF16, nb=CPT)
                    kT = sb(es, "a_kT", [128, 8, TA], BF16, nb=CPT)
                    kpad = sb(es, "a_kpad", [128, 8, 128], BF16)
                    P.op("pool", lambda e: e.memset(kpad[:], 0.0), writes=kpad.b)
                    qsT = sb(es, "a_qsT", [128, 8, TA], BF16, nb=CPT)
                    g_pre = vec[:, 0, l, :]

                    def load_x(t):
                        c0 = t * TA
                        if l > 0:
                            for k in range(8):
                                P.dma(lambda e, k=k: e.dma_start(out=xT[:, k, :], in_=xT_d[k * 128:(k + 1) * 128, c0:c0 + TA]),
                                      writes=[xT.b[k]])
                        else:
                            for c in range(CPT if not _os.environ.get("K6") else int(_os.environ.get("K6"))):
                                xt = xtok[0]
                                P.dma(lambda e, c=c, xt=xt: e.dma_start(out=xt[:], in_=x_in[c0 + c * CH:c0 + (c + 1) * CH, :]),
                                      writes=xt.b)
                                for half in range(2):
                                    pb = bank()
                                    for kk in range(4 if _os.environ.get("K2", "") != "c" else 0):
                                        k = half * 4 + kk
                                        P.op("pe", lambda e, k=k, kk=kk, pb=pb, xt=xt: e.transpose(
                                            pb[:, kk * 128:(kk + 1) * 128], xt[:, k * 128:(k + 1) * 128], ident_f[:]),
                                            reads=xt.b + ident_f.b, writes=pb.b, inc=(kk == 3))
                                    for kk in range(4 if _os.environ.get("K2", "") not in ("b", "c") else 0):
                                        k = half * 4 + kk
                                        K7 = _os.environ.get("K7", "")
                                        useact = (half == 1)
                                        P.op("act" if useact else "dve",
                                             (lambda e, k=k, kk=kk, pb=pb, c=c: e.copy(out=xT[:, k, c * CH:(c + 1) * CH],
                                                                                       in_=pb[:, kk * 128:(kk + 1) * 128]))
                                             if useact else
                                             (lambda e, k=k, kk=kk, pb=pb, c=c: e.tensor_copy(out=xT[:, k, c * CH:(c + 1) * CH],
                                                                                              in_=pb[:, kk * 128:(kk + 1) * 128])),
                                             reads=pb.b, writes=[xT.b[k]])
                            for k in range(8 if _os.environ.get("K2", "") not in ("a", "b", "c") else 0):
                                P.dma(lambda e, k=k: e.dma_start(out=xT_d[k * 128:(k + 1) * 128, c0:c0 + TA], in_=xT[:, k, :]),
                                      reads=[xT.b[k]])

                    def norm_x(t):
                        h = hT[t % 2]
                        rms_stats(None, xT, TA, h, rstd)
                        for k in range(8 if _os.environ.get("K3", "z") >= "d" else 0):
                            eng = "dve"
                            P.op(eng, lambda e, k=k, h=h: e.scalar_tensor_tensor(
                                out=h[:, k, :], in0=xT[:, k, :], scalar=g_pre[:, k:k + 1], in1=rstd[:], op0=ALU.mult, op1=ALU.mult),
                                reads=[xT.b[k]] + rstd.b + vec.b, writes=[h.b[k]])

                    def fm_group(t):
                        h = hT[t % 2]
                        c0 = t * TA
                        for j in range(8):
                            pg, pr, pp = bank(), bank(), bank()
                            for (pb, col) in ((pg, 2048 + j * 128), (pr, 4096 + j * 128), (pp, 5120 + j * 128)):
                                for k in range(8):
                                    P.op("pe", lambda e, pb=pb, col=col, k=k: e.matmul(
                                        pb[:, 0:TA], lhsT=wA[:, k, col:col + 128], rhs=h[:, k, :], start=(k == 0), stop=(k == 7)),
                                        reads=[wA.b[k], h.b[k]], writes=pb.b, inc=(k == 7))
                            s1, s2 = sig[j % 2], sig[2]
                            go, po = gst[j % 2], pst[j % 2]
                            P.op("act", lambda e, pg=pg, s1=s1: e.activation(out=s1[:], in_=pg[:, 0:TA], func=AF.Sigmoid),
                                 reads=pg.b, writes=s1.b)
                            P.op("act", lambda e, pr=pr, s2=s2: e.activation(out=s2[:], in_=pr[:, 0:TA], func=AF.Sigmoid),
                                 reads=pr.b, writes=s2.b)
                            P.op("act", lambda e, pp=pp, po=po: e.activation(out=po[:], in_=pp[:, 0:TA], func=AF.Sigmoid),
                                 reads=pp.b, writes=po.b)
                            P.op("dve", lambda e, pg=pg, s1=s1, j=j: e.scalar_tensor_tensor(
                                out=s1[:], in0=pg[:, 0:TA], scalar=vec[:, 5, l, j:j + 1], in1=s1[:], op0=ALU.mult, op1=ALU.mult),
                                reads=pg.b + s1.b + vec.b, writes=s1.b)
                            P.op("dve", lambda e, s1=s1, s2=s2, go=go: e.tensor_tensor(out=go[:], in0=s1[:], in1=s2[:], op=ALU.mult),
                                 reads=s1.b + s2.b, writes=go.b)
                            P.dma(lambda e, go=go, j=j: e.dma_start(out=gate_d[j * 128:(j + 1) * 128, c0:c0 + TA], in_=go[:]),
                                  reads=go.b)
                            P.dma(lambda e, po=po, j=j: e.dma_start(out=sgp_d[j * 128:(j + 1) * 128, c0:c0 + TA], in_=po[:]),
                                  reads=po.b)

                    def tm_chunk(t, c):
                        h = hT[t % 2]
                        gc = t * CPT + c
                        tok0 = gc * CH
                        cs = slice(c * CH, (c + 1) * CH)
                        rp = rope[gc % 2]
                        P.dma(lambda e, rp=rp: e.dma_start(out=rp[:], in_=rope_t[tok0:tok0 + CH, :]), writes=rp.b)
                        banks = []
                        for col in (0, 512, 1024, 1536, 3072, 3584):
                            pb = bank()
                            banks.append(pb)
                            for k in range(8):
                                P.op("pe", lambda e, pb=pb, col=col, k=k: e.matmul(
                                    pb[:], lhsT=h[:, k, cs], rhs=wA[:, k, col:col + 512], start=(k == 0), stop=(k == 7)),
                                    reads=[wA.b[k], h.b[k]], writes=pb.b, inc=(k == 7))
                        pq, pk, pv0, pv1, pu0, pu1 = banks
                        v_, u_, kv_ = vbf[0], ubf[0], kvs[0]
                        P.op("act", lambda e: e.copy(out=q32[:], in_=pq[:]), reads=pq.b, writes=q32.b)
                        P.op("act", lambda e: e.mul(out=k32[:], in_=pk[:], mul=DK ** -0.5), reads=pk.b, writes=k32.b)
                        P.op("act", lambda e: e.copy(out=v_[:, 0:512], in_=pv0[:]), reads=pv0.b, writes=v_.b)
                        P.op("act", lambda e: e.copy(out=v_[:, 512:1024], in_=pv1[:]), reads=pv1.b + v_.b, writes=v_.b)
                        P.op("act", lambda e: e.copy(out=u_[:, 0:512], in_=pu0[:]), reads=pu0.b, writes=u_.b)
                        P.op("act", lambda e: e.copy(out=u_[:, 512:1024], in_=pu1[:]), reads=pu1.b + u_.b, writes=u_.b)
                        P.dma(lambda e: e.dma_start(out=v_d[tok0:tok0 + CH, :], in_=v_[:]), reads=v_.b)
                        P.dma(lambda e: e.dma_start(out=u_d[8 + tok0:8 + tok0 + CH, :], in_=u_[:]), reads=u_.b)
                        cosb = rp[:, 0:32].unsqueeze(1).to_broadcast([128, 8, 32])
                        sinb = rp[:, 32:64].unsqueeze(1).to_broadcast([128, 8, 32])
                        for (src, dst, eng) in ((q32, qbf, "dve"), (k32, kbf, "dve")):
                            s4 = src[:].rearrange("p (h two d) -> p h two d", h=8, two=2)
                            x1, x2 = s4[:, :, 0, :], s4[:, :, 1, :]
                            rd = src.b + rp.b
                            P.op(eng, lambda e, x1=x1: e.tensor_tensor(out=rt1[:], in0=x1, in1=cosb, op=ALU.mult),
                                 reads=rd, writes=rt1.b)
                            P.op(eng, lambda e, x2=x2: e.tensor_tensor(out=rt2[:], in0=x2, in1=sinb, op=ALU.mult),
                                 reads=rd, writes=rt2.b)
                            P.op(eng, lambda e, dst=dst: e.tensor_tensor(out=dst[:, :, 0:32], in0=rt1[:], in1=rt2[:], op=ALU.subtract),
                                 reads=rt1.b + rt2.b, writes=dst.b)
                            P.op(eng, lambda e, x1=x1: e.tensor_tensor(out=rt1[:], in0=x1, in1=sinb, op=ALU.mult),
                                 reads=rd, writes=rt1.b)
                            P.op(eng, lambda e, x2=x2: e.tensor_tensor(out=rt2[:], in0=x2, in1=cosb, op=ALU.mult),
                                 reads=rd, writes=rt2.b)
                            P.op(eng, lambda e, dst=dst: e.tensor_tensor(out=dst[:, :, 32:64], in0=rt1[:], in1=rt2[:], op=ALU.add),
                                 reads=rt1.b + rt2.b + dst.b, writes=dst.b)
                        for (src, dst, zi, eng) in ((qbf, qs, 2, "dve"), (kbf, kz, 0, "dve")):
                            for d in range(2):
                                P.op(eng, lambda e, src=src, dst=dst, zi=zi, d=d: e.tensor_tensor(
                                    out=dst[:, :, d, :], in0=src[:], in1=zx[:, zi + d, :].unsqueeze(2).to_broadcast([128, 8, 64]),
                                    op=ALU.mult), reads=src.b + zx.b + dst.b, writes=dst.b)
                        pbq = bank()
                        pq16 = pbq[:].bitcast(BF16)
                        for pr_ in range(4):
                            P.op("pe", lambda e, pr_=pr_: e.transpose(pq16[:, pr_ * 128:(pr_ + 1) * 128],
                                                                       qbf[:].rearrange("p h d -> p (h d)")[:, pr_ * 128:(pr_ + 1) * 128], ident_b[:]),
                                 reads=qbf.b + ident_b.b, writes=pbq.b, inc=(pr_ == 3))
                        P.op("act", lambda e: e.copy(out=qT[:, :, cs], in_=pq16[:, 0:512].rearrange("p (a b) -> p a b", a=4)),
                             reads=pbq.b, writes=[qT.b[c]])
                        kp4 = kpad[:].rearrange("p (pr two) x -> p pr two x", two=2)
                        kb4 = kbf[:].rearrange("p (pr two) d -> p pr two d", two=2)
                        P.op("dve", lambda e: e.tensor_copy(out=kp4[:, :, 0, 0:64], in_=kb4[:, :, 0, :]),
                             reads=kbf.b + kpad.b, writes=kpad.b)
                        P.op("dve", lambda e: e.tensor_copy(out=kp4[:, :, 1, 64:128], in_=kb4[:, :, 1, :]),
                             reads=kbf.b + kpad.b, writes=kpad.b)
                        pbk = bank()
                        pk16 = pbk[:].bitcast(BF16)
                        for hh in range(8):
                            P.op("pe", lambda e, hh=hh: e.transpose(pk16[:, hh * 128:(hh + 1) * 128], kpad[:, hh, :], ident_b[:]),
                                 reads=kpad.b + ident_b.b, writes=pbk.b, inc=(hh == 7))
                        P.op("act", lambda e: e.copy(out=kT[:, :, cs], in_=pk16[:].rearrange("p (a b) -> p a b", a=8)),
                             reads=pbk.b, writes=[kT.b[c]])
                        pbs = bank()
                        ps16 = pbs[:].bitcast(BF16)
                        for hh in range(8):
                            P.op("pe", lambda e, hh=hh: e.transpose(ps16[:, hh * 128:(hh + 1) * 128], qs[:].rearrange("p h t d -> p (h t d)")[:, hh * 128:(hh + 1) * 128], ident_b[:]),
                                 reads=qs.b + ident_b.b, writes=pbs.b, inc=(hh == 7))
                        P.op("act", lambda e: e.copy(out=qsT[:, :, cs], in_=ps16[:].rearrange("p (a b) -> p a b", a=8)),
                             reads=pbs.b, writes=[qsT.b[c]])
                        for half in range(2):
                            pb = bank()
                            for hh in range(4):
                                hd = half * 4 + hh
                                P.op("pe", lambda e, hd=hd, hh=hh, pb=pb: e.matmul(
                                    pb[:, hh * 128:(hh + 1) * 128], lhsT=kz[:].rearrange("p h t d -> p (h t d)")[:, hd * 128:(hd + 1) * 128], rhs=v_[:, hd * 128:(hd + 1) * 128],
                                    start=True, stop=True), reads=kz.b + v_.b, writes=pb.b, inc=(hh == 3))
                            P.op("dve", lambda e, pb=pb, half=half: e.tensor_scalar(
                                out=kv_[:, half * 512:(half + 1) * 512], in0=pb[:], scalar1=fkv[:, gc:gc + 1], scalar2=None,
                                op0=ALU.mult), reads=pb.b + fkv.b + kv_.b, writes=kv_.b)
                        P.dma(lambda e: e.dma_start(out=kv_d[gc], in_=kv_[:]), reads=kv_.b)

                    def store_T(t):
                        c0 = t * TA
                        for pr_ in range(4):
                            P.dma(lambda e, pr_=pr_: e.dma_start(out=qT_d[pr_ * 128:(pr_ + 1) * 128, c0:c0 + TA], in_=qT[:, pr_, :]),
                                  reads=qT.b)
                        for hh in range(8):
                            P.dma(lambda e, hh=hh: e.dma_start(out=qsT_d[hh * 128:(hh + 1) * 128, c0:c0 + TA], in_=qsT[:, hh, :]),
                                  reads=qsT.b)
                            P.dma(lambda e, hh=hh: e.dma_start(out=kT_d[hh * 128:(hh + 1) * 128, c0:c0 + TA], in_=kT[:, hh, :]),
                                  reads=kT.b)

                    KSUB = int(_os.environ.get("KSUB", "99"))
                    if KSUB >= 2:
                        load_x(0)
                    if KSUB >= 3:
                        norm_x(0)
                    if KSUB == 4:
                        fm_group(0)
                    if KSUB == 5:
                        tm_chunk(0, 0)
                    for t in range(NTA if KSUB >= 99 else 0):
                        fm_group(t)
                        tm_chunk(t, 0)
                        if t + 1 < NTA:
                            load_x(t + 1)
                            norm_x(t + 1)
                        for c in range(1, CPT):
                            tm_chunk(t, c)
                        store_T(t)
                    P.barrier()

            if KSTOP >= 1:
                _pass0()
            def _pass1():
                with ExitStack() as es:
                    S = sb(es, "b_S", [128, 1024], F32, nb=8)
                    kvl = [sb(es, "b_kv%d" % i, [128, 1024], F32) for i in range(4)]
                    so = [sb(es, "b_so%d" % i, [128, 1024], BF16) for i in range(3)]
                    P.op("pool", lambda e: e.memset(S[:], 0.0), writes=S.b)
                    P.op("pool", lambda e: e.memset(so[2][:], 0.0), writes=so[2].b)
                    P.dma(lambda e: e.dma_start(out=st_d[0, 0:64, :], in_=so[2][0:64, :]), reads=so[2].b)
                    P.dma(lambda e: e.dma_start(out=st_d[NCH - 1, 64:128, :], in_=so[2][64:128, :]), reads=so[2].b)

                    def ld(s):
                        kk = kvl[s % 4]
                        P.dma(lambda e: e.dma_start(out=kk[0:64, :], in_=kv_d[s, 0:64, :]), writes=kk.b)
                        P.dma(lambda e: e.dma_start(out=kk[64:128, :], in_=kv_d[NCH - 1 - s, 64:128, :]), writes=kk.b)

                    for s in range(min(3, NCH - 1)):
                        ld(s)
                    for s in range(NCH - 1):
                        if s + 3 < NCH - 1:
                            ld(s + 3)
                        kk = kvl[s % 4]
                        o = so[s % 2]
                        for hh in range(8):
                            P.op("dve", lambda e, hh=hh, kk=kk, s=s: e.scalar_tensor_tensor(
                                out=S[:, hh * 128:(hh + 1) * 128], in0=S[:, hh * 128:(hh + 1) * 128], scalar=atab[:, s, hh:hh + 1],
                                in1=kk[:, hh * 128:(hh + 1) * 128], op0=ALU.mult, op1=ALU.add),
                                reads=[S.b[hh]] + kk.b + atab.b, writes=[S.b[hh]])
                        P.op("act", lambda e, o=o: e.copy(out=o[:], in_=S[:]), reads=S.b, writes=o.b)
                        P.dma(lambda e, o=o, s=s: e.dma_start(out=st_d[s + 1, 0:64, :], in_=o[0:64, :]), reads=o.b)
                        P.dma(lambda e, o=o, s=s: e.dma_start(out=st_d[NCH - 2 - s, 64:128, :], in_=o[64:128, :]), reads=o.b)
                    P.barrier()

            if KSTOP >= 2:
                _pass1()
            def _pass2():
                with ExitStack() as es:
                    NTC = NT // TC
                    CPT = TC // CH
                    wO = sb(es, "wO", [128, 8, D], BF16, nb=8)
                    wP = sb(es, "wP", [128, 4, 2, 256], BF16, nb=8)
                    for k in range(8):
                        P.dma(lambda e, k=k: e.dma_start(out=wO[:, k, :], in_=w_out[l, k * 128:(k + 1) * 128, :]),
                              writes=[wO.b[k]], eng="pool")
                    for g in range(4):
                        for cc in range(2):
                            P.dma(lambda e, g=g, cc=cc: e.dma_start(out=wP[:, g, cc, :], in_=pool_w[l, g, cc * 128:(cc + 1) * 128, :]),
                                  writes=[wP.b[2 * g + cc]], eng="pool")
                    NBUF = 2
                    qT = [sb(es, "c_qT%d" % i, [128, 4, TC], BF16) for i in range(NBUF)]
                    kT = [sb(es, "c_kT%d" % i, [128, 8, TC], BF16) for i in range(NBUF)]
                    qsT = [sb(es, "c_qsT%d" % i, [128, 8, TC], BF16) for i in range(NBUF)]
                    gat = [sb(es, "c_gat%d" % i, [128, 8, TC], BF16) for i in range(NBUF)]
                    sgp = [sb(es, "c_sgp%d" % i, [128, 8, TC], BF16) for i in range(NBUF)]
                    xT = [sb(es, "c_xT%d" % i, [128, 8, TC], F32, nb=8) for i in range(1)]
                    bM = [sb(es, "c_bM%d" % i, [128, CPT * 4 * 128], BF16) for i in range(NBUF)]
                    bH = [sb(es, "c_bH%d" % i, [128, CPT * 4 * 128], BF16) for i in range(NBUF)]
                    uh = [sb(es, "c_uh%d" % i, [128, 1024], BF16) for i in range(2)]
                    for tz in bH + uh:
                        P.op("pool", lambda e, tz=tz: e.memset(tz[:], 0.0), writes=tz.b)
                    vv = [sb(es, "c_v%d" % i, [128, 1024], BF16) for i in range(2)]
                    st = [sb(es, "c_st%d" % i, [128, 1024], BF16) for i in range(2)]
                    uu = [sb(es, "c_u%d" % i, [128, 1024], BF16) for i in range(2)]
                    PT = [sb(es, "c_PT%d" % i, [128, 8, 128], BF16) for i in range(2)]
                    on = [sb(es, "c_on%d" % i, [128, 8, 128], BF16) for i in range(2)]
                    stats2 = [sb(es, "c_stats%d" % i, [128, 8, 6], F32) for i in range(2)]
                    mv2 = [sb(es, "c_mv%d" % i, [128, 8, 2], F32) for i in range(2)]
                    rs2 = [sb(es, "c_rs%d" % i, [128, 8], F32) for i in range(2)]
                    nmr2 = [sb(es, "c_nmr%d" % i, [128, 8], F32) for i in range(2)]
                    mT = sb(es, "c_mT", [128, 8, TC], BF16, nb=8)
                    pT = sb(es, "c_pT", [128, 8, TC], BF16, nb=8)
                    tmp = [sb(es, "c_tmp%d" % i, [128, TC], BF16) for i in range(2)]
                    yy = sb(es, "c_y", [128, 8, TC], F32, nb=8)
                    rstd = sb(es, "c_rstd", [128, TC], F32)
                    g_post = vec[:, 1, l, :]
                    pscale = vec[:, 4, l, :]

                    def load_tile(t):
                        i = t % NBUF
                        c0 = t * TC
                        for pr_ in range(4):
                            P.dma(lambda e, pr_=pr_: e.dma_start(out=qT[i][:, pr_, :], in_=qT_d[pr_ * 128:(pr_ + 1) * 128, c0:c0 + TC]),
                                  writes=qT[i].b)
                        for hh in range(8):
                            P.dma(lambda e, hh=hh: e.dma_start(out=qsT[i][:, hh, :], in_=qsT_d[hh * 128:(hh + 1) * 128, c0:c0 + TC]),
                                  writes=qsT[i].b)
                            P.dma(lambda e, hh=hh: e.dma_start(out=kT[i][:, hh, :], in_=kT_d[hh * 128:(hh + 1) * 128, c0:c0 + TC]),
                                  writes=kT[i].b)
                            P.dma(lambda e, hh=hh: e.dma_start(out=gat[i][:, hh, :], in_=gate_d[hh * 128:(hh + 1) * 128, c0:c0 + TC]),
                                  writes=gat[i].b)
                            P.dma(lambda e, hh=hh: e.dma_start(out=sgp[i][:, hh, :], in_=sgp_d[hh * 128:(hh + 1) * 128, c0:c0 + TC]),
                                  writes=sgp[i].b)
                        P.dma(lambda e: e.dma_start(out=bM[i][:], in_=bandM[t]), writes=bM[i].b)
                        P.dma(lambda e: e.dma_start(out=bH[i][0:16, :], in_=bandH[t]), writes=bH[i].b)

                    KC = int(_os.environ.get("KC", "99"))

                    def chunk(t, c):
                        i = t % NBUF
                        gc = t * CPT + c
                        tok0 = gc * CH
                        cs = slice(c * CH, (c + 1) * CH)
                        v_, s_, u_ = vv[gc % 2], st[gc % 2], uu[gc % 2]
                        pt_, on_ = PT[gc % 2], on[gc % 2]
                        stats, mv, rs, nmr = stats2[gc % 2], mv2[gc % 2], rs2[gc % 2], nmr2[gc % 2]
                        P.dma(lambda e: e.dma_start(out=v_[:], in_=v_d[tok0:tok0 + CH, :]), writes=v_.b)
                        P.dma(lambda e: e.dma_start(out=s_[:], in_=st_d[gc]), writes=s_.b)
                        uh_ = uh[gc % 2]
                        for half in range(2):
                            pb = psb[half]
                            for hh in range(4):
                                hd = half * 4 + hh
                                pr_, base = hd // 2, (hd % 2) * 64
                                P.op("pe", lambda e, pb=pb, hh=hh, pr_=pr_, hd=hd: e.matmul(
                                    pb[:, hh * 128:(hh + 1) * 128], lhsT=kT[i][:, hd, cs], rhs=qT[i][:, pr_, cs],
                                    start=True, stop=True), reads=kT[i].b + qT[i].b, writes=pb.b, inc=(hh == 3))
                            if _os.environ.get("KD") == "nomask":
                                continue
                            P.op("dve", lambda e, pb=pb, half=half: e.tensor_tensor(
                                out=pt_[:, half * 4:(half + 1) * 4, :], in0=pb[:].rearrange("p (a b) -> p a b", a=4),
                                in1=dmask[:, half * 4:(half + 1) * 4, :], op=ALU.mult),
                                reads=pb.b + dmask.b + pt_.b, writes=pt_.b)
                        obanks = []
                        for half in range(2):
                            pb = psb[2 + 2 * (gc % 2) + half]
                            obanks.append(pb)
                            for hh in range(4):
                                hd = half * 4 + hh
                                P.op("pe", lambda e, pb=pb, hh=hh, hd=hd: e.matmul(
                                    pb[:, hh * 128:(hh + 1) * 128], lhsT=pt_[:, hd, :], rhs=v_[:, hd * 128:(hd + 1) * 128],
                                    start=True, stop=False), reads=pt_.b + v_.b, writes=pb.b, inc=False)
                                P.op("pe", lambda e, pb=pb, hh=hh, hd=hd: e.matmul(
                                    pb[:, hh * 128:(hh + 1) * 128], lhsT=qsT[i][:, hd, cs], rhs=s_[:, hd * 128:(hd + 1) * 128],
                                    start=False, stop=True), reads=qsT[i].b + s_.b, writes=pb.b, inc=(hh == 3))
                        yield
                        P.dma(lambda e: e.dma_start(out=u_[:], in_=u_d[8 + tok0:8 + tok0 + CH, :]), writes=u_.b)
                        P.dma(lambda e: e.dma_start(out=uh_[0:8, :], in_=u_d[tok0:tok0 + 8, :]), writes=uh_.b)
                        P.dma(lambda e: e.dma_start(out=uh_[8:16, :], in_=u_d[tok0 + CH + 8:tok0 + CH + 16, :]), writes=uh_.b)
                        for half in range(2):
                            pb = obanks[half]
                            for hh in range(4):
                                hd = half * 4 + hh
                                P.op("dve", lambda e, pb=pb, hh=hh, hd=hd: e.bn_stats(out=stats[:, hd, :], in_=pb[:, hh * 128:(hh + 1) * 128]),
                                     reads=pb.b + stats.b, writes=stats.b)
                        for hd in range(8):
                            P.op("dve", lambda e, hd=hd: e.bn_aggr(out=mv[:, hd, :], in_=stats[:, hd, :]),
                                 reads=stats.b + mv.b, writes=mv.b)
                        P.op("act", lambda e: e.activation(out=rs[:], in_=mv[:, :, 1], func=AF.Ln, scale=1.0, bias=EPS),
                             reads=mv.b, writes=rs.b)
                        P.op("act", lambda e: e.activation(out=rs[:], in_=rs[:], func=AF.Exp, scale=-0.5),
                             reads=rs.b, writes=rs.b)
                        P.op("dve", lambda e: e.scalar_tensor_tensor(out=nmr[:], in0=mv[:, :, 0], scalar=-1.0, in1=rs[:],
                                                                      op0=ALU.mult, op1=ALU.mult), reads=mv.b + rs.b, writes=nmr.b)
                        for half in range(2):
                            pb = obanks[half]
                            for hh in range(4):
                                hd = half * 4 + hh
                                P.op("act", lambda e, pb=pb, hh=hh, hd=hd: e.activation(
                                    out=on_[:, hd, :], in_=pb[:, hh * 128:(hh + 1) * 128], func=AF.Identity,
                                    scale=rs[:, hd:hd + 1], bias=nmr[:, hd:hd + 1]),
                                    reads=pb.b + rs.b + nmr.b + on_.b, writes=on_.b)
                        yield
                        pbt = psb[6]
                        p16 = pbt[:].bitcast(BF16)
                        for hd in range(8):
                            P.op("pe", lambda e, hd=hd: e.transpose(p16[:, hd * 128:(hd + 1) * 128], on_[:, hd, :], ident_b[:]),
                                 reads=on_.b + ident_b.b, writes=pbt.b, inc=(hd == 7))
                        for half in range(2):
                            P.op("dve" if half == 0 else "pool" if False else "dve", lambda e, half=half: e.tensor_tensor(
                                out=mT[:, half * 4:(half + 1) * 4, cs],
                                in0=p16[:, half * 512:(half + 1) * 512].rearrange("p (a b) -> p a b", a=4),
                                in1=gat[i][:, half * 4:(half + 1) * 4, cs], op=ALU.mult),
                                reads=pbt.b + gat[i].b, writes=[mT.b[k] for k in range(half * 4, half * 4 + 4)])
                        for half in range(2):
                            pb = psb[7 - half]
                            for ff in range(4):
                                f = half * 4 + ff
                                g = f // 2
                                bo = (c * 4 + g) * 128
                                P.op("pe", lambda e, pb=pb, ff=ff, f=f, bo=bo: e.matmul(
                                    pb[:, ff * 128:(ff + 1) * 128], lhsT=u_[:, f * 128:(f + 1) * 128], rhs=bM[i][:, bo:bo + 128],
                                    start=True, stop=False), reads=u_.b + bM[i].b, writes=pb.b, inc=False)
                                P.op("pe", lambda e, pb=pb, ff=ff, f=f, bo=bo: e.matmul(
                                    pb[:, ff * 128:(ff + 1) * 128], lhsT=uh_[:, f * 128:(f + 1) * 128], rhs=bH[i][:, bo:bo + 128],
                                    start=False, stop=True), reads=uh_.b + bH[i].b, writes=pb.b, inc=(ff == 3))
                            P.op("act", lambda e, pb=pb, half=half: e.copy(
                                out=pT[:, half * 4:(half + 1) * 4, cs], in_=pb[:].rearrange("p (a b) -> p a b", a=4)),
                                reads=pb.b, writes=[pT.b[k] for k in range(half * 4, half * 4 + 4)])

                    tb_i = [0]

                    def tail_bank():
                        b_ = psb[(0, 1, 6, 7)[tb_i[0] % 4]]
                        tb_i[0] += 1
                        return b_

                    def tile_tail(t):
                        i = t % NBUF
                        c0 = t * TC
                        for hh in range(8):
                            P.dma(lambda e, hh=hh: e.dma_start(out=xT[0][:, hh, :], in_=xT_d[hh * 128:(hh + 1) * 128, c0:c0 + TC]),
                                  writes=[xT[0].b[hh]])
                        for dchunk in range(8):
                            g, dc = dchunk // 2, dchunk % 2
                            pb = tail_bank()
                            for cc in range(2):
                                P.op("pe", lambda e, pb=pb, g=g, dc=dc, cc=cc: e.matmul(
                                    pb[:, 0:TC], lhsT=wP[:, g, cc, dc * 128:(dc + 1) * 128], rhs=pT[:, 2 * g + cc, :],
                                    start=(cc == 0), stop=(cc == 1)), reads=[wP.b[2 * g + cc], pT.b[2 * g + cc]], writes=pb.b, inc=(cc == 1))
                            tm_ = tmp[dchunk % 2]
                            P.op("dve", lambda e, pb=pb, dchunk=dchunk, tm_=tm_: e.scalar_tensor_tensor(
                                out=tm_[:], in0=pb[:, 0:TC], scalar=pscale[:, dchunk:dchunk + 1], in1=sgp[i][:, dchunk, :],
                                op0=ALU.mult, op1=ALU.mult), reads=pb.b + sgp[i].b + vec.b, writes=tm_.b)
                            P.op("dve", lambda e, dchunk=dchunk, tm_=tm_: e.tensor_tensor(
                                out=mT[:, dchunk, :], in0=mT[:, dchunk, :], in1=tm_[:], op=ALU.add),
                                reads=tm_.b + [mT.b[dchunk]], writes=[mT.b[dchunk]])
                        for n in range(8):
                            pb = tail_bank()
                            for k in range(8):
                                P.op("pe", lambda e, pb=pb, n=n, k=k: e.matmul(
                                    pb[:, 0:TC], lhsT=wO[:, k, n * 128:(n + 1) * 128], rhs=mT[:, k, :], start=(k == 0), stop=(k == 7)),
                                    reads=[wO.b[k], mT.b[k]], writes=pb.b, inc=(k == 7))
                            P.op("act", lambda e, pb=pb, n=n: e.copy(out=yy[:, n, :], in_=pb[:, 0:TC]), reads=pb.b, writes=[yy.b[n]])
                        rms_stats(None, yy, TC, pT, rstd, msb_bank=tail_bank())
                        for k in range(8):
                            eng = "dve"
                            if eng == "dve":
                                P.op(eng, lambda e, k=k: e.scalar_tensor_tensor(
                                    out=yy[:, k, :], in0=yy[:, k, :], scalar=g_post[:, k:k + 1], in1=rstd[:], op0=ALU.mult, op1=ALU.mult),
                                    reads=[yy.b[k]] + rstd.b + vec.b, writes=[yy.b[k]])
                            else:
                                P.op(eng, lambda e, k=k: e.tensor_tensor(out=yy[:, k, :], in0=yy[:, k, :], in1=rstd[:], op=ALU.mult),
                                     reads=[yy.b[k]] + rstd.b, writes=[yy.b[k]])
                                P.op(eng, lambda e, k=k: e.tensor_scalar(out=yy[:, k, :], in0=yy[:, k, :], scalar1=g_post[:, k:k + 1],
                                                                         scalar2=None, op0=ALU.mult),
                                     reads=[yy.b[k]] + vec.b, writes=[yy.b[k]])
                            P.op(eng, lambda e, k=k: e.tensor_tensor(out=xT[0][:, k, :], in0=xT[0][:, k, :], in1=yy[:, k, :], op=ALU.add),
                                 reads=[yy.b[k], xT[0].b[k]], writes=[xT[0].b[k]])
                            P.dma(lambda e, k=k: e.dma_start(out=xm_d[k * 128:(k + 1) * 128, c0:c0 + TC], in_=xT[0][:, k, :]),
                                  reads=[xT[0].b[k]])

                    gens = {}
                    load_tile(0)
                    for g in range(NCH + 2):
                        if g < NCH:
                            t, c = divmod(g, CPT)
                            gens[g] = chunk(t, c)
                            next(gens[g])
                        if 1 <= g <= NCH:
                            next(gens[g - 1])
                        if g >= 2:
                            for _ in gens.pop(g - 2):
                                pass
                            tp, cp = divmod(g - 2, CPT)
                            if cp == CPT - 1:
                                tile_tail(tp)
                        if g < NCH and c == 1 and t + 1 < NTC:
                            load_tile(t + 1)
                    P.barrier()

            if KSTOP >= 3:
                _pass2()
            def _pass3():
                with ExitStack() as es:
                    NTM = NT // TM
                    w1 = sb(es, "w1", [128, 8, DFF], BF16, nb=8)
                    w2 = sb(es, "w2", [128, 32, D], BF16, nb=32)
                    for k in range(8):
                        P.dma(lambda e, k=k: e.dma_start(out=w1[:, k, :], in_=w_mlp1[l, k * 128:(k + 1) * 128, :]),
                              writes=[w1.b[k]], eng="pool")
                    for k in range(32):
                        P.dma(lambda e, k=k: e.dma_start(out=w2[:, k, :], in_=w_mlp2[l, k * 128:(k + 1) * 128, :]),
                              writes=[w2.b[k]], eng="pool")
                    xT = [sb(es, "m_xT%d" % i, [128, 8, TM], F32, nb=8) for i in range(2)]
                    hT = [sb(es, "m_hT%d" % i, [128, 8, TM], BF16, nb=8) for i in range(2)]
                    fT = sb(es, "m_fT", [128, 32, TM], BF16, nb=32)
                    rl = [sb(es, "m_rl%d" % i, [128, TM], F32) for i in range(3)]
                    yy = sb(es, "m_y", [128, 8, TM], F32, nb=8)
                    rstd = sb(es, "m_rstd", [128, TM], F32)
                    rstd2 = sb(es, "m_rstd2", [128, TM], F32)
                    otok = [sb(es, "m_otok%d" % i, [128, D], F32) for i in range(1)] if l == depth - 1 else None
                    g_pre = vec[:, 2, l, :]
                    g_post = vec[:, 3, l, :]

                    def load_x(t):
                        c0 = t * TM
                        for k in range(8):
                            P.dma(lambda e, k=k: e.dma_start(out=xT[t % 2][:, k, :], in_=xm_d[k * 128:(k + 1) * 128, c0:c0 + TM]),
                                  writes=[xT[t % 2].b[k]])

                    def norm_x(t):
                        h, x = hT[t % 2], xT[t % 2]
                        rms_stats(None, x, TM, h, rstd)
                        for k in range(8):
                            eng = "dve"
                            P.op(eng, lambda e, k=k: e.scalar_tensor_tensor(
                                out=h[:, k, :], in0=x[:, k, :], scalar=g_pre[:, k:k + 1], in1=rstd[:], op0=ALU.mult, op1=ALU.mult),
                                reads=[x.b[k]] + rstd.b + vec.b, writes=[h.b[k]])

                    def stage1(t, lo_, hi_):
                        h = hT[t % 2]
                        for cidx in range(lo_, hi_):
                            pb = bank()
                            for k in range(8):
                                P.op("pe", lambda e, pb=pb, cidx=cidx, k=k: e.matmul(
                                    pb[:, 0:TM], lhsT=w1[:, k, cidx * 128:(cidx + 1) * 128], rhs=h[:, k, :], start=(k == 0), stop=(k == 7)),
                                    reads=[w1.b[k], h.b[k]], writes=pb.b, inc=(k == 7))
                            r = rl[cidx % 3]
                            P.op("act", lambda e, pb=pb, r=r: e.activation(out=r[:], in_=pb[:, 0:TM], func=AF.Relu),
                                 reads=pb.b, writes=r.b)
                            if cidx % 2 == 0:
                                P.op("dve", lambda e, r=r, cidx=cidx: e.tensor_tensor(out=fT[:, cidx, :], in0=r[:], in1=r[:], op=ALU.mult),
                                     reads=r.b, writes=[fT.b[cidx]])
                            else:
                                P.op("act", lambda e, r=r, cidx=cidx: e.activation(out=fT[:, cidx, :], in_=r[:], func=AF.Square),
                                     reads=r.b, writes=[fT.b[cidx]])

                    def stage2(t):
                        x = xT[t % 2]
                        c0 = t * TM
                        for n in range(8):
                            pb = bank()
                            for cidx in range(32):
                                P.op("pe", lambda e, pb=pb, n=n, cidx=cidx: e.matmul(
                                    pb[:, 0:TM], lhsT=w2[:, cidx, n * 128:(n + 1) * 128], rhs=fT[:, cidx, :],
                                    start=(cidx == 0), stop=(cidx == 31)), reads=[w2.b[cidx], fT.b[cidx]], writes=pb.b, inc=(cidx == 31))
                            P.op("act", lambda e, pb=pb, n=n: e.copy(out=yy[:, n, :], in_=pb[:, 0:TM]), reads=pb.b, writes=[yy.b[n]])
                        rms_stats(None, yy, TM, hT[t % 2], rstd2)
                        for k in range(8):
                            eng = "dve"
                            if eng == "dve":
                                P.op(eng, lambda e, k=k: e.scalar_tensor_tensor(
                                    out=yy[:, k, :], in0=yy[:, k, :], scalar=g_post[:, k:k + 1], in1=rstd2[:], op0=ALU.mult, op1=ALU.mult),
                                    reads=[yy.b[k]] + rstd2.b + vec.b, writes=[yy.b[k]])
                            else:
                                P.op(eng, lambda e, k=k: e.tensor_tensor(out=yy[:, k, :], in0=yy[:, k, :], in1=rstd2[:], op=ALU.mult),
                                     reads=[yy.b[k]] + rstd2.b, writes=[yy.b[k]])
                                P.op(eng, lambda e, k=k: e.tensor_scalar(out=yy[:, k, :], in0=yy[:, k, :], scalar1=g_post[:, k:k + 1],
                                                                         scalar2=None, op0=ALU.mult),
                                     reads=[yy.b[k]] + vec.b, writes=[yy.b[k]])
                            P.op(eng, lambda e, k=k: e.tensor_tensor(out=x[:, k, :], in0=x[:, k, :], in1=yy[:, k, :], op=ALU.add),
                                 reads=[yy.b[k], x.b[k]], writes=[x.b[k]])
                            if l < depth - 1:
                                P.dma(lambda e, k=k: e.dma_start(out=xT_d[k * 128:(k + 1) * 128, c0:c0 + TM], in_=x[:, k, :]),
                                      reads=[x.b[k]])
                        if l == depth - 1:
                            for c in range(TM // CH):
                                ot = otok[0]
                                for half in range(2):
                                    pb = bank()
                                    for kk in range(4):
                                        k = half * 4 + kk
                                        P.op("pe", lambda e, pb=pb, k=k, kk=kk, c=c: e.transpose(
                                            pb[:, kk * 128:(kk + 1) * 128], x[:, k, c * CH:(c + 1) * CH], ident_f[:]),
                                            reads=[x.b[k]] + ident_f.b, writes=pb.b, inc=(kk == 3))
                                    P.op("act" if half else "dve",
                                         (lambda e, pb=pb, half=half, ot=ot: e.copy(out=ot[:, half * 512:(half + 1) * 512], in_=pb[:]))
                                         if half else
                                         (lambda e, pb=pb, half=half, ot=ot: e.tensor_copy(out=ot[:, half * 512:(half + 1) * 512], in_=pb[:])),
                                         reads=pb.b + ot.b, writes=ot.b)
                                P.dma(lambda e, c=c, ot=ot: e.dma_start(out=y_out[c0 + c * CH:c0 + (c + 1) * CH, :], in_=ot[:]),
                                      reads=ot.b)

                    load_x(0)
                    norm_x(0)
                    for t in range(NTM):
                        stage1(t, 0, 16)
                        if t + 1 < NTM:
                            load_x(t + 1)
                        stage1(t, 16, 32)
                        if t + 1 < NTM:
                            norm_x(t + 1)
                        stage2(t)
                    P.barrier()

            if KSTOP >= 4:
                _pass3()
        for l_ in range(depth):
            do_layer(l_)
        P.barrier()
        P.emit()
    return nc, P


def host_tables(seq_lens, NT, TC=512):
    NCH = NT // CH
    pos = np.zeros(NT, np.int64)
    seq_id = np.full(NT, -1, np.int64)
    seq_start = np.zeros(NT, np.int64)
    seq_len = np.ones(NT, np.int64) * CH
    o = 0
    for si, L in enumerate(seq_lens):
        pos[o:o + L] = np.arange(L)
        seq_id[o:o + L] = si
        seq_start[o:o + L] = o
        seq_len[o:o + L] = L
        o += L
    for c in range(o // CH, NCH):
        pos[c * CH:(c + 1) * CH] = np.arange(CH)
        seq_id[c * CH:(c + 1) * CH] = 1000 + c
        seq_start[c * CH:(c + 1) * CH] = c * CH
    half = DK // 2
    inv = (ROPE_BASE ** (-np.arange(half, dtype=np.float32) / np.float32(half))).astype(np.float32)
    ang = pos.astype(np.float32)[:, None] * inv[None, :]
    rope_t = np.concatenate([np.cos(ang), np.sin(ang)], axis=1).astype(np.float32)
    cid = seq_id[::CH]
    cont = (cid[:-1] == cid[1:]).astype(np.float32)
    flagkv = np.zeros((128, NCH), np.float32)
    flagkv[0:64, :NCH - 1] = cont[None, :]
    flagkv[64:128, 1:] = cont[None, :]
    scanflag = np.zeros((128, NCH), np.float32)
    scanflag[0:64, :NCH - 1] = cont[None, :]
    scanflag[64:128, :NCH - 1] = cont[::-1][None, :]
    CPT = TC // CH
    bandM = np.zeros((NT // TC, 128, CPT, 4, 128), np.float32)
    bandH = np.zeros((NT // TC, 16, CPT, 4, 128), np.float32)
    i_idx = np.arange(CH)
    for c in range(NCH):
        t, cc = c // CPT, c % CPT
        tok = c * CH + i_idx
        s0 = seq_start[tok]
        s1 = s0 + seq_len[tok]
        for g, w in enumerate(POOL_WINDOWS):
            hw = w // 2
            lo = np.maximum(tok - hw, s0)
            hi = np.minimum(tok + hw, s1)
            cntv = (hi - lo).astype(np.float32)
            for r in range(-8, CH + 8):
                j = c * CH + r
                val = ((j >= lo) & (j < hi)).astype(np.float32) / cntv
                if 0 <= r < CH:
                    val = val - (i_idx == r).astype(np.float32)
                    bandM[t, r, cc, g, :] = val
                elif r < 0:
                    bandH[t, r + 8, cc, g, :] = val
                else:
                    bandH[t, 8 + (r - CH), cc, g, :] = val
    bandM = bandM.reshape(NT // TC, 128, -1).astype(ml_dtypes.bfloat16)
    bandH = bandH.reshape(NT // TC, 16, -1).astype(ml_dtypes.bfloat16)
    return dict(rope_t=rope_t, flagkv=flagkv, scanflag=scanflag, bandM=bandM, bandH=bandH)


def const_table():
    j = np.arange(128, dtype=np.float32)[:, None]
    i = np.arange(128, dtype=np.float32)[None, :]
    c = np.zeros((128, 5 * 128 + 4), np.float32)
    c[:, 516:644] = np.eye(128, dtype=np.float32)
    c[:, 0:128] = np.maximum(i - j, 0)
    c[:, 128:256] = np.maximum(j - i, 0)
    c[:, 256:384] = (i >= j)
    c[:, 384:512] = (j > i)
    c[:, 512] = 127 - j[:, 0]
    c[:, 513] = j[:, 0]
    c[:, 514] = j[:, 0] + 1
    c[:, 515] = 128 - j[:, 0]
    return c


def shared_inputs(inp, depth):
    f = lambda a: np.ascontiguousarray(np.asarray(a, dtype=np.float32))
    vecs = np.zeros((128, 6, depth, 8), np.float32)
    for idx, name in enumerate(["norm_mix_pre", "norm_mix_post", "norm_mlp_pre", "norm_mlp_post", "pool_scale", "ret_gn"]):
        a = f(inp[name])[:depth]
        vecs[:, idx] = a.reshape(depth, 8, 128).transpose(2, 0, 1)
    dec = np.concatenate([f(inp["ret_decay_fwd"])[:depth].reshape(-1), f(inp["ret_decay_bwd"])[:depth].reshape(-1)])
    decays = np.ascontiguousarray(np.broadcast_to(dec[None, :], (128, dec.size)))
    return dict(w_in=f(inp["w_in"])[:depth], w_out=f(inp["w_out"])[:depth], w_mlp1=f(inp["w_mlp1"])[:depth],
                w_mlp2=f(inp["w_mlp2"])[:depth], pool_w=f(inp["pool_w"])[:depth], vecs=vecs, decays=decays,
                consts=const_table())


_CACHE = {}


def run_cores(core_tokens, core_seqlens, NT, depth, inp, n_cores=8):
    key = (NT, depth)
    if key not in _CACHE:
        _CACHE[key] = build(NT, depth)[0]
    nc = _CACHE[key]
    sh = shared_inputs(inp, depth)
    in_maps = []
    for ci in range(n_cores):
        xt = np.zeros((NT, D), np.float32)
        a = core_tokens[ci]
        xt[:a.shape[0]] = a
        m = dict(sh)
        m["x_in"] = xt
        m.update(host_tables(core_seqlens[ci], NT))
        in_maps.append(m)
    res = run_bass_kernel_spmd(nc, in_maps, core_ids=list(range(n_cores)))
    return [r["y_out"] for r in res.results]


def kernel(**inputs):
    xp = np.asarray(inputs["x_prompt"], dtype=np.float32)
    xs = np.asarray(inputs["x_sample"], dtype=np.float32)
    depth = 4
    NT = 16384
    toks, lens = [xp[0]], [[16384]]
    for ci in range(4):
        toks.append(xs[ci * 4:(ci + 1) * 4].reshape(-1, D))
        lens.append([4096] * 4)
    for ci in range(3):
        toks.append(np.zeros((0, D), np.float32))
        lens.append([])
    outs = run_cores(toks, lens, NT, depth, inputs)
    y_prompt = outs[0].reshape(1, 16384, D).astype(np.float32)
    y_sample = np.concatenate([outs[1 + ci].reshape(4, 4096, D) for ci in range(4)], axis=0).astype(np.float32)
    return (y_prompt, y_sample)
```
